# Optimizing a Trainium2 kernel written in Bass

```python
import math
import jax
import jax.numpy as jnp
from jax import lax
import numpy as np

D_MODEL = 2048
BATCH = 16
SEQ = 2048
DEPTH = 4

D_MIX = D_MODEL
HG_DK = 128
HG_DV = 128
HG_WIDTH = D_MIX // 2
HG_HEADS = HG_WIDTH // HG_DV
HG_QK = HG_HEADS * HG_DK
HG_CHUNK = 32
DA_HEAD_DIM = 64
DA_VDIM = 2 * DA_HEAD_DIM
DA_WIDTH = D_MIX // 4
DA_HEADS = DA_WIDTH // DA_VDIM
DA_QK = DA_HEADS * 2 * DA_HEAD_DIM
SC_WIDTH = D_MIX - HG_WIDTH - DA_WIDTH
SC_GROUPS = 4
SC_GROUP_DIM = SC_WIDTH // SC_GROUPS
SC_KSIZE = 3
D_FF = ((8 * D_MODEL // 3 + 127) // 128) * 128
ROPE_THETA = 10000.0
Q_BLOCK = 128
EPS = 1e-6
IN_SPLITS = (HG_QK, HG_WIDTH, HG_QK, HG_QK, HG_WIDTH, DA_QK, DA_QK, DA_WIDTH, SC_WIDTH, SC_WIDTH, SC_WIDTH)
D_IN = sum(IN_SPLITS)

kernel_name = "hybrid_hgrn2_diffattn_shortconv_macaron"


def _rms_norm(x, gain):
    xf = x.astype(jnp.float32)
    y = xf * lax.rsqrt(jnp.mean(xf * xf, axis=-1, keepdims=True) + EPS)
    return (y * gain.astype(jnp.float32)).astype(x.dtype)


def _swiglu(h, w_gate, w_up, w_down):
    return (jax.nn.silu(h @ w_gate) * (h @ w_up)) @ w_down


def _rope(t, cos, sin):
    half = t.shape[-1] // 2
    t1, t2 = t[..., :half], t[..., half:]
    return jnp.concatenate([t1 * cos - t2 * sin, t2 * cos + t1 * sin], axis=-1)


def _split_columns(proj):
    parts = []
    start = 0
    for width in IN_SPLITS:
        parts.append(proj[..., start:start + width])
        start += width
    return parts


def _gla_chunk_scan(q, k, v, log_f):
    b_, s_, h_, dk = q.shape
    dv = v.shape[-1]
    n_chunks = s_ // HG_CHUNK

    def to_chunks(t):
        return t.astype(jnp.float32).reshape(b_, n_chunks, HG_CHUNK, h_, t.shape[-1]).transpose(1, 0, 3, 2, 4)

    lower = jnp.tril(jnp.ones((HG_CHUNK, HG_CHUNK), dtype=bool))[:, :, None]

    def step(state, inp):
        qc, kc, vc, lf = inp
        cum = jnp.cumsum(lf, axis=2)
        o_inter = jnp.einsum('bhtk,bhkv->bhtv', qc * jnp.exp(cum), state)
        rel = cum[:, :, :, None, :] - cum[:, :, None, :, :]
        decay = jnp.exp(jnp.where(lower, rel, -jnp.inf))
        scores = jnp.einsum('bhtsk,bhsk->bhts', qc[:, :, :, None, :] * decay, kc)
        o_intra = jnp.einsum('bhts,bhsv->bhtv', scores, vc)
        cum_end = cum[:, :, -1:, :]
        state = (jnp.exp(cum_end[:, :, 0, :])[..., None] * state
                 + jnp.einsum('bhsk,bhsv->bhkv', kc * jnp.exp(cum_end - cum), vc))
        return state, o_inter + o_intra

    state0 = jnp.zeros((b_, h_, dk, dv), jnp.float32)
    _, o = lax.scan(step, state0, (to_chunks(q), to_chunks(k), to_chunks(v), to_chunks(log_f)))
    return o.transpose(1, 0, 3, 2, 4).reshape(b_, s_, h_, dv)


def _hgrn2_gates(z, lb):
    zf = z.astype(jnp.float32)
    f = lb + (1.0 - lb) * jax.nn.sigmoid(zf)
    return jnp.log(f), (1.0 - lb) * jax.nn.sigmoid(-zf)


def _hgrn2_mixer(q, i, z_fwd, z_bwd, g, lb_fwd, lb_bwd, norm_gain):
    b_, s_, _ = q.shape

    def heads(t):
        return t.reshape(b_, s_, HG_HEADS, -1)

    def flip(t):
        return jnp.flip(t, axis=1)

    lf_fwd, k_fwd = _hgrn2_gates(z_fwd, lb_fwd)
    lf_bwd, k_bwd = _hgrn2_gates(z_bwd, lb_bwd)
    qh, ih = heads(q), heads(i)
    o_fwd = _gla_chunk_scan(qh, heads(k_fwd), ih, heads(lf_fwd))
    o_bwd = flip(_gla_chunk_scan(flip(qh), flip(heads(k_bwd)), flip(ih), flip(heads(lf_bwd))))
    o = _rms_norm(o_fwd + o_bwd, norm_gain.reshape(HG_HEADS, HG_DV))
    return o.reshape(b_, s_, HG_WIDTH).astype(g.dtype) * jax.nn.silu(g)


def _diff_attention(q, k, v, q_gain, k_gain, lam, cos, sin):
    b_, s_, _ = q.shape
    dt = q.dtype

    def prep(t, gain):
        t = _rms_norm(t.reshape(b_, s_, DA_HEADS, 2, DA_HEAD_DIM), gain)
        return _rope(t.astype(jnp.float32), cos, sin).astype(dt)

    qh, kh = prep(q, q_gain), prep(k, k_gain)
    vh = v.reshape(b_, s_, DA_HEADS, DA_VDIM)
    n_blocks = s_ // Q_BLOCK
    q_blocks = qh.reshape(b_, n_blocks, Q_BLOCK, DA_HEADS, 2, DA_HEAD_DIM).transpose(1, 0, 2, 3, 4, 5)
    scale = DA_HEAD_DIM ** -0.5

    def block(qb):
        s = jnp.einsum('bqhmd,bkhmd->bhmqk', qb, kh).astype(jnp.float32) * scale
        p = jax.nn.softmax(s, axis=-1)
        a = p[:, :, 0] - lam * p[:, :, 1]
        return jnp.einsum('bhqk,bkhe->bqhe', a.astype(dt), vh)

    o = lax.map(block, q_blocks)
    return o.transpose(1, 0, 2, 3, 4).reshape(b_, s_, DA_HEADS, DA_VDIM)


def _short_conv_mixer(b_gate, c_gate, u, conv_w, conv_b, norm_gain):
    b_, s_, _ = u.shape
    v = c_gate * u
    pad = SC_KSIZE // 2
    vp = jnp.pad(v, ((0, 0), (pad, pad), (0, 0)))
    y = conv_b
    for j in range(SC_KSIZE):
        y = y + vp[:, j:j + s_] * conv_w[j]
    y = b_gate * y
    y = _rms_norm(y.reshape(b_, s_, SC_GROUPS, SC_GROUP_DIM), norm_gain.reshape(SC_GROUPS, SC_GROUP_DIM))
    return y.reshape(b_, s_, SC_WIDTH)


def setup_inputs(seed: int = 0) -> dict:
    key = jax.random.key(seed)
    ks = jax.random.split(key, 32)
    f32 = jnp.float32

    def nrm(k, shape, scale):
        return jax.random.normal(k, shape, f32) * scale

    def gain(k, shape):
        return 1.0 + 0.1 * jax.random.normal(k, shape, f32)

    sd = D_MODEL ** -0.5
    sf = D_FF ** -0.5
    sm = D_MIX ** -0.5
    return {
        "x": jax.random.normal(ks[0], (BATCH, SEQ, D_MODEL), f32),
        "positions": jnp.broadcast_to(jnp.arange(SEQ, dtype=jnp.int32), (BATCH, SEQ)),
        "ffn1_norm": gain(ks[1], (DEPTH, D_MODEL)),
        "ffn1_w_gate": nrm(ks[2], (DEPTH, D_MODEL, D_FF), sd),
        "ffn1_w_up": nrm(ks[3], (DEPTH, D_MODEL, D_FF), sd),
        "ffn1_w_down": nrm(ks[4], (DEPTH, D_FF, D_MODEL), sf),
        "mix_norm": gain(ks[5], (DEPTH, D_MODEL)),
        "w_in": nrm(ks[6], (DEPTH, D_MODEL, D_IN), sd),
        "hgrn_lb_logits": nrm(ks[7], (2, DEPTH, HG_QK), 0.1),
        "hgrn_norm": gain(ks[8], (DEPTH, HG_WIDTH)),
        "da_q_norm": gain(ks[9], (DEPTH, DA_HEAD_DIM)),
        "da_k_norm": gain(ks[10], (DEPTH, DA_HEAD_DIM)),
        "da_lambda_q1": nrm(ks[11], (DEPTH, DA_HEAD_DIM), 0.1),
        "da_lambda_k1": nrm(ks[12], (DEPTH, DA_HEAD_DIM), 0.1),
        "da_lambda_q2": nrm(ks[13], (DEPTH, DA_HEAD_DIM), 0.1),
        "da_lambda_k2": nrm(ks[14], (DEPTH, DA_HEAD_DIM), 0.1),
        "da_out_norm": gain(ks[15], (DEPTH, DA_WIDTH)),
        "conv_w": nrm(ks[16], (DEPTH, SC_KSIZE, SC_WIDTH), SC_KSIZE ** -0.5),
        "conv_b": nrm(ks[17], (DEPTH, SC_WIDTH), 0.02),
        "conv_norm": gain(ks[18], (DEPTH, SC_WIDTH)),
        "w_out": nrm(ks[19], (DEPTH, D_MIX, D_MODEL), sm),
        "ffn2_norm": gain(ks[20], (DEPTH, D_MODEL)),
        "ffn2_w_gate": nrm(ks[21], (DEPTH, D_MODEL, D_FF), sd),
        "ffn2_w_up": nrm(ks[22], (DEPTH, D_MODEL, D_FF), sd),
        "ffn2_w_down": nrm(ks[23], (DEPTH, D_FF, D_MODEL), sf),
    }


def reference(x, positions, ffn1_norm, ffn1_w_gate, ffn1_w_up, ffn1_w_down, mix_norm, w_in,
              hgrn_lb_logits, hgrn_norm, da_q_norm, da_k_norm, da_lambda_q1, da_lambda_k1,
              da_lambda_q2, da_lambda_k2, da_out_norm, conv_w, conv_b, conv_norm, w_out,
              ffn2_norm, ffn2_w_gate, ffn2_w_up, ffn2_w_down):
    f32 = jnp.float32
    inv_freq = ROPE_THETA ** (-jnp.arange(0, DA_HEAD_DIM, 2, dtype=f32) / DA_HEAD_DIM)
    ang = positions.astype(f32)[..., None] * inv_freq
    cos = jnp.cos(ang)[:, :, None, None, :]
    sin = jnp.sin(ang)[:, :, None, None, :]
    lb_all = jnp.cumsum(jax.nn.softmax(hgrn_lb_logits.astype(f32), axis=1), axis=1)
    lb_all = lb_all - lb_all[:, :1]

    for layer in range(DEPTH):
        h = _rms_norm(x, ffn1_norm[layer])
        x = x + 0.5 * _swiglu(h, ffn1_w_gate[layer], ffn1_w_up[layer], ffn1_w_down[layer])

        h = _rms_norm(x, mix_norm[layer])
        (hg_q, hg_i, hg_zf, hg_zb, hg_g, da_q, da_k, da_v,
         sc_b, sc_c, sc_u) = _split_columns(h @ w_in[layer])

        y_a = _hgrn2_mixer(hg_q, hg_i, hg_zf, hg_zb, hg_g,
                           lb_all[0, layer], lb_all[1, layer], hgrn_norm[layer])

        lam_init = 0.8 - 0.6 * math.exp(-0.3 * layer)
        lam = (jnp.exp(jnp.sum(da_lambda_q1[layer].astype(f32) * da_lambda_k1[layer].astype(f32)))
               - jnp.exp(jnp.sum(da_lambda_q2[layer].astype(f32) * da_lambda_k2[layer].astype(f32)))
               + lam_init)
        o_b = _diff_attention(da_q, da_k, da_v, da_q_norm[layer], da_k_norm[layer], lam, cos, sin)
        y_b = (_rms_norm(o_b, da_out_norm[layer].reshape(DA_HEADS, DA_VDIM)) * (1.0 - lam_init)).reshape(
            x.shape[0], x.shape[1], DA_WIDTH)

        y_c = _short_conv_mixer(sc_b, sc_c, sc_u, conv_w[layer], conv_b[layer], conv_norm[layer])

        y = jnp.concatenate([y_a.astype(h.dtype), y_b.astype(h.dtype), y_c.astype(h.dtype)], axis=-1)
        x = x + y @ w_out[layer]

        h = _rms_norm(x, ffn2_norm[layer])
        x = x + 0.5 * _swiglu(h, ffn2_w_gate[layer], ffn2_w_up[layer], ffn2_w_down[layer])
    return x
```

```python
import math
from contextlib import ExitStack

import numpy as np
import concourse.bass as bass
import concourse.mybir as mybir
from concourse.bass_utils import run_bass_kernel_spmd

F32 = mybir.dt.float32
BF16 = mybir.dt.bfloat16
I32 = mybir.dt.int32
ALU = mybir.AluOpType
AF = mybir.ActivationFunctionType
AX = mybir.AxisListType

D = 2048
S = 2048
DFF = 5504
NFC = 43
DEPTH = 4
NCORES = 8
EPS = 1e-6
NBLK = 64

COMPUTE = ("pe", "act", "dve", "pool")
ALLENG = ("pe", "act", "dve", "pool", "sp")


class Prog:
    def __init__(self, nc, es):
        self.nc = nc
        self.es = es
        self.streams = {e: [] for e in ALLENG}
        self.sem = {e: es.enter_context(nc.semaphore("s_" + e)) for e in COMPUTE}
        self.cnt = {e: 0 for e in COMPUTE}
        self.dsem = {}
        self.dcnt = {}
        self.waited = {e: {} for e in ALLENG}
        self.lastw = {}
        self.readers = {}

    def _deps(self, eng, reads, writes, is_dma=False):
        toks = []
        for r in reads:
            for t in self.lastw.get(r, {}).values():
                toks.append((t, True))
        for w in writes:
            for t in self.lastw.get(w, {}).values():
                toks.append((t, False))
            rd = self.readers.get(w)
            if rd:
                for t in rd.values():
                    toks.append((t, False))
        need = {}
        for (kind, key, val), raw in toks:
            if not is_dma and kind == "e" and key == eng:
                if not raw or eng == "pe":
                    continue
            k = (kind, key)
            if self.waited[eng].get(k, 0) >= val:
                continue
            if need.get(k, 0) < val:
                need[k] = val
        for k, v in need.items():
            self.waited[eng][k] = v
        return [(k[0], k[1], v) for k, v in need.items()]

    def _record(self, tok, reads, writes):
        for w in writes:
            self.lastw.setdefault(w, {})[(tok[0], tok[1])] = tok
            self.readers[w] = {}
        for r in reads:
            self.readers.setdefault(r, {})[(tok[0], tok[1])] = tok

    def op(self, eng, fn, reads=(), writes=()):
        waits = self._deps(eng, reads, writes)
        self.cnt[eng] += 1
        tok = ("e", eng, self.cnt[eng])
        self._record(tok, reads, writes)
        self.streams[eng].append(("op", waits, fn, None))
        return tok

    def dma(self, eng, semname, fn, reads=(), writes=(), n=1):
        if semname not in self.dsem:
            self.dsem[semname] = self.es.enter_context(self.nc.semaphore("d_" + semname))
            self.dcnt[semname] = 0
        waits = self._deps(eng, reads, writes, is_dma=True)
        self.dcnt[semname] += 16 * n
        tok = ("d", semname, self.dcnt[semname])
        self._record(tok, reads, writes)
        self.streams[eng].append(("dma", waits, fn, semname))
        return tok

    def wait_all(self, eng):
        waits = []
        for e in COMPUTE:
            if self.cnt[e] > 0 and e != eng:
                waits.append(("e", e, self.cnt[e]))
        for s, v in self.dcnt.items():
            if v > 0:
                waits.append(("d", s, v))
        self.streams[eng].append(("wait", waits, None, None))

    def emit(self):
        prog = self

        def semh(kind, key):
            return prog.sem[key] if kind == "e" else prog.dsem[key]

        def run(engname, eobj):
            for kind, waits, fn, semname in prog.streams[engname]:
                for (k, key, v) in waits:
                    eobj.wait_ge(semh(k, key), v)
                if kind == "op":
                    fn(eobj).then_inc(prog.sem[engname], 1)
                elif kind == "dma":
                    inss = fn(eobj)
                    if not isinstance(inss, (list, tuple)):
                        inss = [inss]
                    for i in inss:
                        i.then_inc(prog.dsem[semname], 16)

        with self.nc.Block() as block:
            @block.tensor
            def _(e):
                run("pe", e)

            @block.scalar
            def _(e):
                run("act", e)

            @block.vector
            def _(e):
                run("dve", e)

            @block.gpsimd
            def _(e):
                run("pool", e)

            @block.sync
            def _(e):
                run("sp", e)


SCR_BYTES = 75 * 1024
PAGE = 1024


class K:
    def __init__(self, nc, es, cfg):
        self.nc, self.es, self.cfg = nc, es, cfg
        L = DEPTH
        dt = nc.dram_tensor
        ein = dict(kind="ExternalInput")
        self.xT = dt("xT", [2, 16, 128, S], F32, **ein).ap()
        self.wgu = dt("wgu", [L, 2, NFC, 128, 4096], F32, **ein).ap()
        self.wd = dt("wd", [L, 2, 16 * 128 * 4, 1376], F32, **ein).ap()
        self.win = dt("win", [L, NBLK // 2, 128, 4096], F32, **ein).ap()
        self.wout = dt("wout", [L, 16 * 128 * 16 * 128 // 2048, 2048], F32, **ein).ap()
        self.gains = dt("gains", [L, 3, 128, 16], F32, **ein).ap()
        self.yT = dt("yT", [2, 16, 128, S], F32, kind="ExternalOutput").ap()
        self.ident_d = dt("ident", [128, 128], BF16, **ein).ap()
        self.mask_d = dt("masks", [2, 128, 128], BF16, **ein).ap()
        self.invf_d = dt("invf", [128, 32], F32, **ein).ap()
        self.lbl_d = dt("lbl", [128, 16, 4], F32, **ein).ap()
        self.pos_d = dt("pos", [2, 128, 16], I32, **ein).ap()
        self.small_d = dt("small", [L, 128, 32], F32, **ein).ap()
        self.bc_d = dt("bc", [L, 128, 384], F32, **ein).ap()
        self.cs_s = dt("cs_s", [2, 128, 512], F32).ap()
        self.y_s = [dt(f"y_s{c}", [128, S], BF16).ap() for c in range(16)]
        self.wgu_s = [[dt(f"wgu_s{l}_{f}", [NFC, 128, 4096], BF16).ap() for f in range(2)] for l in range(L)]
        self.wd_s = [[dt(f"wd_s{l}_{f}", [16 * 128 * 4, 1376], BF16).ap() for f in range(2)] for l in range(L)]
        self.win_s = [dt(f"win_s{l}", [NBLK // 2, 128, 4096], BF16).ap() for l in range(L)]
        self.wout_s = [dt(f"wout_s{l}", [16 * 128 * 16 * 128 // 2048, 2048], BF16).ap() for l in range(L)]

        sb = lambda name, shape, d: es.enter_context(nc.sbuf_tensor(name, shape, d))
        self.X = sb("X", [128, 16, S], F32)
        self.SCR = sb("SCR", [128, SCR_BYTES // 2], BF16)
        self.ones_bf = sb("ones_bf", [128, 128], BF16)
        self.gtile = sb("gtile", [128, 3, 16], F32)
        self.eps_t = sb("eps_t", [128, 1], F32)
        self.ident = sb("ident_sb", [128, 128], BF16)
        self.zeros_bf = sb("zeros_bf", [128, 128], BF16)
        self.one_f = sb("one_f", [128, 1], F32)
        self.mask_f = sb("mask_f", [128, 128], BF16)
        self.mask_b = sb("mask_b", [128, 128], BF16)
        self.invf = sb("invf_sb", [128, 32], F32)
        self.lbl = sb("lbl_sb", [128, 16, 4], F32)
        self.lbs = sb("lbs", [128, 16], F32)
        self.lb = sb("lb", [128, 16, 4], F32)
        self.oml = sb("oml", [128, 16, 4], F32)
        self.rcol = sb("rcol", [128, 16], F32)
        self.r2col = sb("r2col", [128, 16], F32)
        self.small = sb("small_sb", [128, 32], F32)
        self.bc = sb("bc_sb", [128, 384], F32)
        self.nlam = sb("nlam", [128, 1], F32)
        self.wcnt = 0
        self.ps = [es.enter_context(nc.psum_tensor(f"ps{i}", [128, 512], F32)) for i in range(8)]
        self.P = Prog(nc, es)
        self.Xhi = self.X[:].bitcast(BF16)

    def sv(self, off, nelem, dtype):
        nbytes = nelem * (4 if dtype == F32 else 2)
        assert off % 4 == 0 and off + nbytes <= SCR_BYTES, (off, nbytes)
        v = self.SCR[:, off // 2:(off + nbytes) // 2]
        if dtype == F32:
            v = v.bitcast(F32)
        keys = [("S", pg) for pg in range(off // PAGE, (off + nbytes - 1) // PAGE + 1)]
        return v, keys

    def xhi(self, c, t0, t1):
        return self.Xhi[:, c, 2 * t0 + 1:2 * t1:2]

    @staticmethod
    def xk(cs, tgs):
        return [("X", c, tg) for c in cs for tg in tgs]

    def setup(self):
        P = self.P
        P.op("pool", lambda e: e.memset(self.ones_bf[:], 1.0), writes=["ones_bf"])
        P.op("pool", lambda e: e.memset(self.eps_t[:], EPS), writes=["eps_t"])

    def prep(self, layers):
        P = self.P
        IN_SLOTS, OUT_SLOTS = 3, 3
        in_off = [i * 16384 for i in range(IN_SLOTS)]
        out_off = [IN_SLOTS * 16384 + i * 8192 for i in range(OUT_SLOTS)]
        cnt = 0
        for l in layers:
            P.dma("sp", "gl", lambda e, l=l: e.dma_start(out=self.gtile[:], in_=self.gains[l].rearrange("g p k -> p g k")),
                  writes=["gtile"])
            jobs = []
            for f in range(2):
                for c in range(NFC):
                    jobs.append((self.wgu[l, f, c], self.wgu_s[l][f][c], 0 if f == 0 else 2))
            for b in range(NBLK // 2):
                jobs.append((self.win[l, b], self.win_s[l][b], 1))
            for src, dst, gi in jobs:
                si, so = cnt % IN_SLOTS, cnt % OUT_SLOTS
                tin, kin = self.sv(in_off[si], 4096, F32)
                tout, kout = self.sv(out_off[so], 4096, BF16)
                P.dma("sp", f"pi{si}", lambda e, tin=tin, src=src: e.dma_start(out=tin, in_=src), writes=kin)
                eng = "dve" if cnt % 3 != 2 else "pool"
                g_b = self.gtile[:, gi, :].unsqueeze(1).unsqueeze(3).to_broadcast([128, 2, 16, 128])
                tin4 = tin.rearrange("p (j k c) -> p j k c", j=2, k=16)
                tout4 = tout.rearrange("p (j k c) -> p j k c", j=2, k=16)
                P.op(eng, lambda e, a=tout4, b=tin4, g=g_b: e.tensor_tensor(out=a, in0=b, in1=g, op=ALU.mult),
                     reads=kin + ["gtile"], writes=kout)
                P.dma("act", f"po{so}", lambda e, tout=tout, dst=dst: e.dma_start(out=dst, in_=tout), reads=kout,
                      writes=[("W", "gu_in", l)])
                cnt += 1
            for f in range(2):
                n_rows = self.wd.shape[2]
                step = n_rows // 8
                for i in range(8):
                    P.dma("pool", "cast", lambda e, l=l, f=f, i=i, step=step: e.dma_start(
                        out=self.wd_s[l][f][i * step:(i + 1) * step, :], in_=self.wd[l, f, i * step:(i + 1) * step, :]),
                        writes=[("W", "d", l)])
            n_rows = self.wout.shape[1]
            step = n_rows // 4
            for i in range(4):
                P.dma("pool", "cast", lambda e, l=l, i=i, step=step: e.dma_start(
                    out=self.wout_s[l][i * step:(i + 1) * step, :], in_=self.wout[l, i * step:(i + 1) * step, :]),
                    writes=[("W", "d", l)])

    def load_x(self, s):
        for c in range(16):
            self.P.dma("sp", "ldx", lambda e, c=c: e.dma_start(out=self.X[:, c, :], in_=self.xT[s, c]),
                       writes=self.xk([c], range(4)))

    def store_x(self, s):
        for c in range(16):
            self.P.dma("sp", "stx", lambda e, c=c: e.dma_start(out=self.yT[s, c], in_=self.X[:, c, :]),
                       reads=self.xk([c], range(4)))

    def rstd(self, t0, nt, r_view, r_keys, sq_off, scale_extra=None, rh_view=None, rh_keys=None, bank=6):
        P = self.P
        tgs = sorted({t // 512 for t in (t0, t0 + nt - 1)})
        for c in range(16):
            sq, ksq = self.sv(sq_off + (c % 2) * 1024, 512, BF16)
            P.op("act", lambda e, c=c, sq=sq: e.activation(out=sq[:, 0:nt], in_=self.X[:, c, t0:t0 + nt], func=AF.Square),
                 reads=self.xk([c], tgs), writes=ksq)
            P.op("pe", lambda e, c=c, sq=sq: e.matmul(self.ps[bank][:, 0:nt], lhsT=self.ones_bf[:], rhs=sq[:, 0:nt],
                                                     start=(c == 0), stop=(c == 15)),
                 reads=ksq + ["ones_bf"], writes=[("ps", bank)])
        P.op("act", lambda e: e.activation(out=r_view[:, 0:nt], in_=self.ps[bank][:, 0:nt], func=AF.Sqrt,
                                           bias=self.eps_t[:], scale=1.0 / D),
             reads=[("ps", bank), "eps_t"], writes=r_keys)
        P.op("dve", lambda e: e.reciprocal(out=r_view[:, 0:nt], in_=r_view[:, 0:nt]), reads=r_keys, writes=r_keys)
        if rh_view is not None:
            P.op("pool", lambda e: e.tensor_scalar(out=rh_view[:, 0:nt], in0=r_view[:, 0:nt], scalar1=0.5, scalar2=None,
                                                   op0=ALU.mult), reads=r_keys, writes=rh_keys)

    def ffn(self, l, f):
        P = self.P
        ACT0 = 0
        WS = 44032
        WSZ = 22528
        RB = WS + WSZ
        RH = RB + 2048
        TA = RH + 2048
        TB = TA
        SQ = TA + 4096
        assert SQ + 2048 <= SCR_BYTES
        for tg in range(4):
            self._ffn_tg(l, f, tg, ACT0, WS, RB, RH, TA, TB, SQ)

    def _ffn_tg(self, l, f, tg, ACT0, WS, RB, RH, TA, TB, SQ):
        P = self.P
        if True:
            t0 = tg * 512
            rb, krb = self.sv(RB, 512, F32)
            rh, krh = self.sv(RH, 512, F32)
            self.rstd(t0, 512, rb, krb, SQ, rh_view=rh, rh_keys=krh)
            for c in range(NFC):
                slot = c % 2
                wt, kw = self.sv(WS + slot * 8192, 4096, BF16)
                wt4 = wt.rearrange("p (j k c) -> p j k c", j=2, k=16)
                P.dma("sp", f"wa{slot}", lambda e, wt=wt, c=c: e.dma_start(out=wt, in_=self.wgu_s[l][f][c]),
                      reads=[("W", "gu_in", l)], writes=kw)
                bg, bu = c % 2, 2 + c % 2

                def mm(e, j, bank, wt4=wt4):
                    ins = None
                    for kc in range(16):
                        ins = e.matmul(self.ps[bank][:], lhsT=wt4[:, j, kc, :], rhs=self.xhi(kc, t0, t0 + 512),
                                       start=(kc == 0), stop=(kc == 15))
                    return ins
                P.op("pe", lambda e, mm=mm, bg=bg: mm(e, 0, bg), reads=kw + self.xk(range(16), [tg]), writes=[("ps", bg)])
                P.op("pe", lambda e, mm=mm, bu=bu: mm(e, 1, bu), reads=kw + self.xk(range(16), [tg]), writes=[("ps", bu)])
                ta, kta = self.sv(TA + (c % 2) * 2048, 512, F32)
                P.op("dve", lambda e, ta=ta, bg=bg: e.tensor_tensor(out=ta, in0=self.ps[bg][:], in1=rb, op=ALU.mult),
                     reads=[("ps", bg)] + krb, writes=kta)
                P.op("act", lambda e, ta=ta: e.activation(out=ta, in_=ta, func=AF.Silu), reads=kta, writes=kta)
                av, kav = self.sv(ACT0 + c * 1024, 512, BF16)
                P.op("dve", lambda e, ta=ta, av=av, bu=bu: e.tensor_tensor(out=av, in0=self.ps[bu][:], in1=ta, op=ALU.mult),
                     reads=[("ps", bu)] + kta, writes=kav)
            actv, kact = self.sv(ACT0, NFC * 512, BF16)
            act3 = actv.rearrange("p (c t) -> p c t", c=NFC)
            rows_per_n = 512
            for n in range(16):
                slot = n % 2
                wt, kw = self.sv(WS + slot * 11264, NFC * 128, BF16)
                wt3 = wt.rearrange("p (c n) -> p c n", c=NFC)
                src = self.wd_s[l][f][n * rows_per_n:(n + 1) * rows_per_n, :].rearrange("(p a) b -> p (a b)", p=128)
                P.dma("sp", f"wb{slot}", lambda e, wt=wt, src=src: e.dma_start(out=wt, in_=src),
                      reads=[("W", "d", l)], writes=kw)
                bd = 4 + n % 2

                def mmd(e, bd=bd, wt3=wt3):
                    ins = None
                    for fc in range(NFC):
                        ins = e.matmul(self.ps[bd][:], lhsT=wt3[:, fc, :], rhs=act3[:, fc, :],
                                       start=(fc == 0), stop=(fc == NFC - 1))
                    return ins
                P.op("pe", mmd, reads=kw + kact, writes=[("ps", bd)])
                tb, ktb = self.sv(TB + (n % 2) * 2048, 512, F32)
                P.op("dve", lambda e, tb=tb, bd=bd: e.tensor_tensor(out=tb, in0=self.ps[bd][:], in1=rh, op=ALU.mult),
                     reads=[("ps", bd)] + krh, writes=ktb)
                xv = self.X[:, n, t0:t0 + 512]
                P.op("pool", lambda e, tb=tb, xv=xv: e.tensor_tensor(out=xv, in0=xv, in1=tb, op=ALU.add),
                     reads=ktb + self.xk([n], [tg]), writes=self.xk([n], [tg]))

    def E(self, eng, meth, reads, writes, **kw):
        return self.P.op(eng, lambda e: getattr(e, meth)(**kw), reads=list(reads), writes=list(writes))

    def MM(self, reads, writes, mms):
        def fn(e):
            ins = None
            for (o, l_, r_, st, sp) in mms:
                ins = e.matmul(o, lhsT=l_, rhs=r_, start=st, stop=sp)
            return ins
        return self.P.op("pe", fn, reads=list(reads), writes=list(writes))

    def TR(self, reads, writes, trs):
        def fn(e):
            ins = None
            for (o, i_) in trs:
                ins = e.transpose(o, i_, self.ident[:])
            return ins
        return self.P.op("pe", fn, reads=list(reads) + ["ident"], writes=list(writes))

    def DMA(self, eng, sem, out, in_, reads, writes):
        return self.P.dma(eng, sem, lambda e: e.dma_start(out=out, in_=in_), reads=list(reads), writes=list(writes))

    def pk(self, banks):
        return [("ps", b) for b in banks]

    M_RB, M_WS, M_T1, M_T2, M_T3 = 0, 8192, 16384, 24640, 32832
    M_B1, M_B2, M_B3, M_B4, M_B5, M_B6, M_MSK, M_SM = 41024, 45120, 49216, 53568, 57664, 61760, 65856, 69952

    def wblk(self, l, blk):
        return self.win_s[l][blk // 2][:, (blk % 2) * 2048:(blk % 2 + 1) * 2048]

    def load_w(self, l, blk):
        slot = self.wcnt % 2
        self.wcnt += 1
        wt, kw = self.sv(self.M_WS + slot * 4096, 2048, BF16)
        self.DMA("sp", f"mw{slot}", wt, self.wblk(l, blk), [("W", "gu_in", l)], kw)
        return wt.rearrange("p (k c) -> p k c", k=16), kw

    def proj_fm(self, l, blk, b0):
        wt3, kw = self.load_w(l, blk)
        for tg in range(4):
            self.MM(kw + self.xk(range(16), [tg]), self.pk([b0 + tg]),
                    [(self.ps[b0 + tg][:], wt3[:, kc, :], self.xhi(kc, tg * 512, tg * 512 + 512), kc == 0, kc == 15)
                     for kc in range(16)])

    def proj_tm(self, l, blk, b0):
        wt3, kw = self.load_w(l, blk)
        for i in range(16):
            o = self.ps[b0 + i // 4][:, (i % 4) * 128:(i % 4 + 1) * 128]
            self.MM(kw + self.xk(range(16), [i // 4]), self.pk([b0 + i // 4]),
                    [(o, self.xhi(kc, i * 128, i * 128 + 128), wt3[:, kc, :], kc == 0, kc == 15) for kc in range(16)])

    def mixer_setup(self):
        P = self.P
        self.E("pool", "memset", [], ["zeros_bf"], ap=self.zeros_bf[:], constant=0.0)
        self.E("pool", "memset", [], ["one_f"], ap=self.one_f[:], constant=1.0)
        self.DMA("sp", "cst", self.ident[:], self.ident_d[:, :], [], ["ident"])
        self.DMA("sp", "cst", self.mask_f[:], self.mask_d[0], [], ["mask_f"])
        self.DMA("sp", "cst", self.mask_b[:], self.mask_d[1], [], ["mask_b"])
        self.DMA("sp", "cst", self.invf[:], self.invf_d[:, :], [], ["invf"])
        lbl = self.lbl
        self.DMA("sp", "cst", lbl[:], self.lbl_d[:, :, :], [], ["lbl"])
        self.E("act", "activation", ["lbl"], ["lbl"], out=lbl[:], in_=lbl[:], func=AF.Exp)
        self.E("dve", "tensor_reduce", ["lbl"], ["lbs"], out=self.lbs[:], in_=lbl[:], axis=AX.X, op=ALU.add)
        self.E("dve", "reciprocal", ["lbs"], ["lbs"], out=self.lbs[:], in_=self.lbs[:])
        self.E("dve", "tensor_tensor", ["lbl", "lbs"], ["lbl"], out=lbl[:], in0=lbl[:],
               in1=self.lbs[:].unsqueeze(2).to_broadcast([128, 16, 4]), op=ALU.mult)
        self.E("pool", "memset", [], ["lb"], ap=self.lb[:, :, 0:1], constant=0.0)
        for j in range(1, 4):
            self.E("dve", "tensor_tensor", ["lbl", "lb"], ["lb"], out=self.lb[:, :, j:j + 1], in0=self.lb[:, :, j - 1:j],
                   in1=lbl[:, :, j:j + 1], op=ALU.add)
        self.E("dve", "tensor_scalar", ["lb"], ["oml"], out=self.oml[:], in0=self.lb[:], scalar1=-1.0, scalar2=1.0,
               op0=ALU.mult, op1=ALU.add)

    def cossin(self, s):
        PI = math.pi
        ang, ka = self.sv(self.M_T1, 512, F32)
        tmp, kt = self.sv(self.M_T2, 512, F32)
        ki_, kki = self.sv(self.M_T3, 512, F32)
        pi_t, kpi = self.sv(self.M_T3 + 4096, 512, F32)
        kint = pi_t.bitcast(I32)
        posf, kpf = self.sv(self.M_B1, 16, F32)
        posi = posf.bitcast(I32)
        self.DMA("sp", "cst", posi, self.pos_d[s], [], kpf)
        self.E("dve", "tensor_copy", kpf, kpf, out=posf, in_=posi)
        ang3 = ang.rearrange("p (i j) -> p i j", i=16)
        self.E("dve", "tensor_tensor", kpf + ["invf"], ka, out=ang3, in0=posf.unsqueeze(2).to_broadcast([128, 16, 32]),
               in1=self.invf[:].unsqueeze(1).to_broadcast([128, 16, 32]), op=ALU.mult)
        for which, shift in ((0, 0.0), (1, PI / 2)):
            self.E("dve", "tensor_scalar", ka, kt, out=tmp, in0=ang, scalar1=shift, scalar2=1.0 / (2 * PI), op0=ALU.add,
                   op1=ALU.mult)
            self.E("dve", "tensor_copy", kt, kpi, out=kint, in_=tmp)
            self.E("dve", "tensor_copy", kpi, kki, out=ki_, in_=kint)
            self.E("dve", "scalar_tensor_tensor", kki + ka, kt, out=tmp, in0=ki_, scalar=-2 * PI, in1=ang, op0=ALU.mult,
                   op1=ALU.add)
            if shift != 0.0:
                self.E("dve", "tensor_scalar", kt, kt, out=tmp, in0=tmp, scalar1=shift, scalar2=None, op0=ALU.add)
            self.E("dve", "tensor_scalar", kt, kki, out=ki_, in0=tmp, scalar1=PI, scalar2=-2 * PI, op0=ALU.is_gt, op1=ALU.mult)
            self.E("dve", "tensor_tensor", kt + kki, kt, out=tmp, in0=tmp, in1=ki_, op=ALU.add)
            self.E("dve", "tensor_scalar", kt, kki, out=ki_, in0=tmp, scalar1=-PI, scalar2=2 * PI, op0=ALU.is_lt, op1=ALU.mult)
            self.E("dve", "tensor_tensor", kt + kki, kt, out=tmp, in0=tmp, in1=ki_, op=ALU.add)
            self.E("dve", "tensor_scalar", kt, kt, out=tmp, in0=tmp, scalar1=PI, scalar2=-PI, op0=ALU.min, op1=ALU.max)
            self.E("act", "activation", kt, kki, out=ki_, in_=tmp, func=AF.Sin)
            self.DMA("sp", "cst", self.cs_s[1 - which], ki_, kki, [("cs",)])

    def mixer(self, l, s):
        P = self.P
        self.wcnt = 0
        rb, krb = self.sv(self.M_RB, 2048, F32)
        for tg in range(4):
            v, kv = self.sv(self.M_RB + tg * 2048, 512, F32)
            self.rstd(tg * 512, 512, v, kv, self.M_SM)
        for i in range(16):
            self.MM(krb + ["one_f"], self.pk([7]), [(self.ps[7][:, i:i + 1], rb[0:1, i * 128:(i + 1) * 128], self.one_f[0:1, 0:1], True, True)])
        self.E("dve", "tensor_copy", self.pk([7]), ["rcol"], out=self.rcol[:], in_=self.ps[7][:, 0:16])
        self.E("dve", "tensor_tensor", ["rcol"], ["r2col"], out=self.r2col[:], in0=self.rcol[:], in1=self.rcol[:], op=ALU.mult)
        self.DMA("sp", "cst", self.small[:], self.small_d[l], [], ["small"])
        self.DMA("sp", "cst", self.bc[:], self.bc_d[l], [], ["bc"])
        parts = self.cfg.get("mix", ("conv", "attn", "hgrn"))
        if "conv" in parts:
            for gi in range(4):
                self.conv_group(l, gi, rb, krb)
        if "attn" in parts:
            self.attn_setup(l)
            for a in range(4):
                self.attn_head(l, a, rb, krb)
        if "hgrn" in parts:
            self.hgrn_setup()
            for h in range(8):
                self.hgrn_head(l, h, rb, krb)
        ych = []
        if "hgrn" in parts:
            ych += list(range(0, 8))
        if "attn" in parts:
            ych += list(range(8, 12))
        if "conv" in parts:
            ych += list(range(12, 16))
        self.wout_stage(l, ych)

    def conv_group(self, l, gi, rb, krb):
        T1, k1 = self.sv(self.M_T1, 2050, F32)
        T2, k2 = self.sv(self.M_T2, 2048, F32)
        T3, k3 = self.sv(self.M_T3, 2048, F32)
        SQ, ksq = self.sv(self.M_B6, 2048, BF16)
        YB, kyb = self.sv(self.M_B5, 2048, BF16)
        sm = self.small
        c0 = gi * 5
        self.E("pool", "memset", [], k1, ap=T1[:, 0:1], constant=0.0)
        self.E("pool", "memset", [], k1, ap=T1[:, 2049:2050], constant=0.0)
        self.proj_fm(l, 56 + gi, 0)
        for tg in range(4):
            self.E("dve", "tensor_tensor", self.pk([tg]) + krb, k1, out=T1[:, 1 + tg * 512:1 + tg * 512 + 512],
                   in0=self.ps[tg][:], in1=rb[:, tg * 512:(tg + 1) * 512], op=ALU.mult)
        self.proj_fm(l, 60 + gi, 4)
        for tg in range(4):
            self.E("dve", "tensor_tensor", self.pk([4 + tg]) + krb, k2, out=T2[:, tg * 512:(tg + 1) * 512],
                   in0=self.ps[4 + tg][:], in1=rb[:, tg * 512:(tg + 1) * 512], op=ALU.mult)
        self.E("pool", "tensor_tensor", k1 + k2, k1, out=T1[:, 1:2049], in0=T1[:, 1:2049], in1=T2, op=ALU.mult)
        self.E("dve", "tensor_scalar", k1 + ["small"], k2, out=T2, in0=T1[:, 1:2049], scalar1=sm[:, c0 + 1:c0 + 2],
               scalar2=sm[:, c0 + 3:c0 + 4], op0=ALU.mult, op1=ALU.add)
        self.E("dve", "scalar_tensor_tensor", k1 + k2 + ["small"], k2, out=T2, in0=T1[:, 0:2048], scalar=sm[:, c0:c0 + 1],
               in1=T2, op0=ALU.mult, op1=ALU.add)
        self.E("dve", "scalar_tensor_tensor", k1 + k2 + ["small"], k2, out=T2, in0=T1[:, 2:2050], scalar=sm[:, c0 + 2:c0 + 3],
               in1=T2, op0=ALU.mult, op1=ALU.add)
        self.proj_fm(l, 52 + gi, 0)
        for tg in range(4):
            self.E("dve", "tensor_tensor", self.pk([tg]) + krb, k3, out=T3[:, tg * 512:(tg + 1) * 512],
                   in0=self.ps[tg][:], in1=rb[:, tg * 512:(tg + 1) * 512], op=ALU.mult)
        self.E("pool", "tensor_tensor", k2 + k3, k2, out=T2, in0=T2, in1=T3, op=ALU.mult)
        self.pnorm_store(T2, k2, T3, k3, SQ, ksq, YB, kyb, 4, sm[:, c0 + 4:c0 + 5], None, None, 12 + gi, 1.0)

    def pnorm_store(self, Tin, kin, Ttmp, ktmp, SQ, ksq, YB, kyb, b0, gain_col, mul_t, kmul, ychunk, in_is_psum_banks=None):
        self.E("act", "activation", kin, ksq, out=SQ, in_=Tin, func=AF.Square)
        for tg in range(4):
            self.MM(ksq + ["ones_bf"], self.pk([b0 + tg]),
                    [(self.ps[b0 + tg][:], self.ones_bf[:], SQ[:, tg * 512:(tg + 1) * 512], True, True)])
            self.E("act", "activation", self.pk([b0 + tg]) + ["eps_t"], ktmp, out=Ttmp[:, tg * 512:(tg + 1) * 512],
                   in_=self.ps[b0 + tg][:], func=AF.Sqrt, bias=self.eps_t[:], scale=1.0 / 128)
        self.E("dve", "reciprocal", ktmp, ktmp, out=Ttmp, in_=Ttmp)
        if mul_t is None:
            self.E("dve", "scalar_tensor_tensor", kin + ktmp + ["small"], kyb, out=YB, in0=Tin, scalar=gain_col, in1=Ttmp,
                   op0=ALU.mult, op1=ALU.mult)
        else:
            self.E("dve", "scalar_tensor_tensor", kin + ktmp + ["small"], ktmp, out=Ttmp, in0=Tin, scalar=gain_col, in1=Ttmp,
                   op0=ALU.mult, op1=ALU.mult)
            self.E("pool", "tensor_tensor", ktmp + kmul, kyb, out=YB, in0=Ttmp, in1=mul_t, op=ALU.mult)
        self.DMA("sp", "ysto", self.y_s[ychunk], YB, kyb, [("y", ychunk)])

    def wout_stage(self, l, ych):
        for tg in range(4):
            yt, ky = self.sv(self.M_T1, 16 * 512, BF16)
            yt3 = yt.rearrange("p (c t) -> p c t", c=16)
            for c in ych:
                self.DMA("sp", "yld", yt3[:, c, :], self.y_s[c][:, tg * 512:(tg + 1) * 512], [("y", c)], ky)
            for n in range(16):
                slot = self.wcnt % 2
                self.wcnt += 1
                wt, kw = self.sv(self.M_WS + slot * 4096, 2048, BF16)
                self.DMA("sp", f"mw{slot}", wt, self.wout_s[l][n * 128:(n + 1) * 128, :], [("W", "d", l)], kw)
                wt3 = wt.rearrange("p (k c) -> p k c", k=16)
                b = n % 2
                self.MM(kw + ky, self.pk([b]), [(self.ps[b][:], wt3[:, c, :], yt3[:, c, :], j == 0, j == len(ych) - 1)
                                                for j, c in enumerate(ych)])
                xv = self.X[:, n, tg * 512:(tg + 1) * 512]
                self.E("dve", "tensor_tensor", self.pk([b]) + self.xk([n], [tg]), self.xk([n], [tg]), out=xv, in0=self.ps[b][:],
                       in1=xv, op=ALU.add)

    def attn_setup(self, l):
        bc3 = self.bc[:].rearrange("p (g d) -> p g d", g=6)
        lt, kl = self.sv(self.M_SM, 128, F32)
        lt3 = lt.rearrange("p (g d) -> p g d", g=2)
        ls, kls = self.sv(self.M_SM + 512, 2, F32)
        self.E("dve", "tensor_tensor", ["bc"], kl, out=lt3, in0=bc3[:, 2:6:2, :], in1=bc3[:, 3:6:2, :], op=ALU.mult)
        self.E("dve", "tensor_reduce", kl, kls, out=ls, in_=lt3, axis=AX.X, op=ALU.add)
        self.E("act", "activation", kls, kls, out=ls, in_=ls, func=AF.Exp)
        lam_init = 0.8 - 0.6 * math.exp(-0.3 * l)
        self.E("dve", "tensor_tensor", kls, ["nlam"], out=self.nlam[:], in0=ls[:, 1:2], in1=ls[:, 0:1], op=ALU.subtract)
        self.E("dve", "tensor_scalar", ["nlam"], ["nlam"], out=self.nlam[:], in0=self.nlam[:], scalar1=-lam_init, scalar2=None,
               op0=ALU.add)
        self.lam_init = lam_init

    def attn_head(self, l, a, rb, krb):
        sv = self.sv
        COS, kcos = sv(self.M_T2, 512, F32)
        SIN, ksin = sv(self.M_T2 + 2048, 512, F32)
        self.DMA("sp", "cst", COS, self.cs_s[0], [("cs",)], kcos)
        self.DMA("sp", "cst", SIN, self.cs_s[1], [("cs",)], ksin)
        cos4 = COS.rearrange("p (i d) -> p i d", i=16).unsqueeze(2).to_broadcast([128, 16, 2, 32])
        sin4 = SIN.rearrange("p (i d) -> p i d", i=16).unsqueeze(2).to_broadcast([128, 16, 2, 32])
        QT, kqt = sv(self.M_B1, 2048, BF16)
        KT, kkt = sv(self.M_B2, 2048, BF16)
        VA, kva = sv(self.M_B3, 16 * 130, BF16)
        VA3 = VA.rearrange("p (i c) -> p i c", i=16)
        QN, kqn = sv(self.M_B4, 2048, BF16)
        QN4 = QN.rearrange("p (i m d) -> p i m d", i=16, m=2)
        QN3 = QN.rearrange("p (i c) -> p i c", i=16)
        T1, k1 = sv(self.M_T1, 2048, F32)
        T1v = T1.rearrange("p (i m d) -> p i m d", i=16, m=2)
        RT, krt = sv(self.M_B5, 2048, F32)
        RT5 = RT.rearrange("p (h i m d) -> p h i m d", h=2, i=16, m=2)
        SS, kss = sv(self.M_SM + 1024, 32, F32)
        SS3 = SS.rearrange("p (i m) -> p i m", i=16)
        bc3 = self.bc[:].rearrange("p (g d) -> p g d", g=6)
        for which, (blk, dst, kdst) in enumerate(((40 + a, QT, kqt), (44 + a, KT, kkt))):
            self.proj_tm(l, blk, 0)
            for b in range(4):
                self.E("act", "activation", self.pk([b]), k1, out=T1[:, b * 512:(b + 1) * 512], in_=self.ps[b][:], func=AF.Square)
            self.E("dve", "tensor_reduce", k1, kss, out=SS, in_=T1.rearrange("p (g d) -> p g d", d=64), axis=AX.X, op=ALU.add)
            self.E("dve", "tensor_tensor", kss + ["r2col"], kss, out=SS3, in0=SS3,
                   in1=self.r2col[:].unsqueeze(2).to_broadcast([128, 16, 2]), op=ALU.mult)
            self.E("act", "activation", kss + ["eps_t"], kss, out=SS, in_=SS, func=AF.Sqrt, bias=self.eps_t[:], scale=1.0 / 64)
            self.E("dve", "reciprocal", kss, kss, out=SS, in_=SS)
            self.E("dve", "tensor_tensor", kss + ["rcol"], kss, out=SS3, in0=SS3,
                   in1=self.rcol[:].unsqueeze(2).to_broadcast([128, 16, 2]), op=ALU.mult)
            if which == 0:
                self.E("dve", "tensor_scalar", kss, kss, out=SS, in0=SS, scalar1=0.125, scalar2=None, op0=ALU.mult)
            for b in range(4):
                self.E("dve", "tensor_tensor", self.pk([b]) + kss, k1, out=T1v[:, 4 * b:4 * b + 4, :, :],
                       in0=self.ps[b][:].rearrange("p (i m d) -> p i m d", i=4, m=2),
                       in1=SS3[:, 4 * b:4 * b + 4, :].unsqueeze(3).to_broadcast([128, 4, 2, 64]), op=ALU.mult)
            self.E("dve", "tensor_tensor", k1 + ["bc"], k1, out=T1.rearrange("p (g d) -> p g d", d=64),
                   in0=T1.rearrange("p (g d) -> p g d", d=64),
                   in1=bc3[:, which, :].unsqueeze(1).to_broadcast([128, 32, 64]), op=ALU.mult)
            t1 = T1v[:, :, :, 0:32]
            t2 = T1v[:, :, :, 32:64]
            self.E("dve", "tensor_tensor", k1 + kcos, krt, out=RT5[:, 0], in0=t1, in1=cos4, op=ALU.mult)
            self.E("pool", "tensor_tensor", k1 + ksin, krt, out=RT5[:, 1], in0=t2, in1=sin4, op=ALU.mult)
            self.E("dve", "tensor_tensor", krt, kqn, out=QN4[:, :, :, 0:32], in0=RT5[:, 0], in1=RT5[:, 1], op=ALU.subtract)
            self.E("dve", "tensor_tensor", k1 + kcos, krt, out=RT5[:, 0], in0=t2, in1=cos4, op=ALU.mult)
            self.E("pool", "tensor_tensor", k1 + ksin, krt, out=RT5[:, 1], in0=t1, in1=sin4, op=ALU.mult)
            self.E("dve", "tensor_tensor", krt, kqn, out=QN4[:, :, :, 32:64], in0=RT5[:, 0], in1=RT5[:, 1], op=ALU.add)
            for hb in range(2):
                pT = self.ps[4 + hb][:].bitcast(BF16)
                self.TR(kqn, self.pk([4 + hb]), [(pT[:, j * 128:(j + 1) * 128], QN3[:, hb * 8 + j, :]) for j in range(8)])
                self.E("act", "activation", self.pk([4 + hb]), kdst, out=dst[:, hb * 1024:(hb + 1) * 1024], in_=pT, func=AF.Copy)
        self.proj_tm(l, 48 + a, 0)
        for b in range(4):
            self.E("dve", "tensor_tensor", self.pk([b]) + ["rcol"], kva, out=VA3[:, 4 * b:4 * b + 4, 0:128],
                   in0=self.ps[b][:].rearrange("p (i c) -> p i c", i=4),
                   in1=self.rcol[:, 4 * b:4 * b + 4].unsqueeze(2).to_broadcast([128, 4, 128]), op=ALU.mult)
        self.E("pool", "memset", [], kva, ap=VA3[:, :, 128:129], constant=1.0)
        PT, kpt = sv(self.M_T2, 16 * 512, BF16)
        PT3 = PT.rearrange("p (k q) -> p k q", k=16)
        ON, kon = sv(self.M_MSK, 1024, F32)
        ON4 = ON.rearrange("p (m q d) -> p m q d", m=2, q=4)
        YB, kyb = sv(self.M_B5, 2048, BF16)
        OB, kob = sv(self.M_B4, 512, BF16)
        OB3 = OB.rearrange("p (q d) -> p q d", q=4)
        RS, krs = sv(self.M_SM + 1280, 1, F32)
        S4, ks4 = sv(self.M_SM + 1344, 4, F32)
        for qg in range(4):
            for m in range(2):
                for kt in range(16):
                    b = 2 + (kt % 2)
                    self.MM(kkt + kqt, self.pk([b]), [(self.ps[b][:], KT[m * 64:(m + 1) * 64, kt * 128:(kt + 1) * 128],
                                                      QT[m * 64:(m + 1) * 64, qg * 512:(qg + 1) * 512], True, True)])
                    self.E("act", "activation", self.pk([b]), kpt, out=PT3[:, kt, :], in_=self.ps[b][:], func=AF.Exp)
                for qt in range(4):
                    bo = 6 + (qt % 2)
                    self.MM(kpt + kva, self.pk([bo]), [(self.ps[bo][:, 0:129], PT3[:, kt, qt * 128:(qt + 1) * 128], VA3[:, kt, 0:129],
                                                       kt == 0, kt == 15) for kt in range(16)])
                    self.E("dve", "reciprocal", self.pk([bo]), krs, out=RS, in_=self.ps[bo][:, 128:129])
                    self.E("dve", "tensor_scalar", self.pk([bo]) + krs, kon, out=ON4[:, m, qt, :], in0=self.ps[bo][:, 0:128],
                           scalar1=RS, scalar2=None, op0=ALU.mult)
            self.E("dve", "scalar_tensor_tensor", kon + ["nlam"], kon, out=ON4[:, 0], in0=ON4[:, 1], scalar=self.nlam[:],
                   in1=ON4[:, 0], op0=ALU.mult, op1=ALU.add)
            self.E("dve", "tensor_tensor", kon, kon, out=ON4[:, 1], in0=ON4[:, 0], in1=ON4[:, 0], op=ALU.mult)
            self.E("dve", "tensor_reduce", kon, ks4, out=S4, in_=ON4[:, 1], axis=AX.X, op=ALU.add)
            self.E("act", "activation", ks4 + ["eps_t"], ks4, out=S4, in_=S4, func=AF.Sqrt, bias=self.eps_t[:], scale=1.0 / 128)
            self.E("dve", "reciprocal", ks4, ks4, out=S4, in_=S4)
            self.E("dve", "tensor_scalar", ks4, ks4, out=S4, in0=S4, scalar1=1.0 - self.lam_init, scalar2=None, op0=ALU.mult)
            self.E("dve", "tensor_tensor", kon + ks4, kob, out=OB3, in0=ON4[:, 0], in1=S4.unsqueeze(2).to_broadcast([128, 4, 128]),
                   op=ALU.mult)
            pT = self.ps[4][:].bitcast(BF16)
            self.TR(kob, self.pk([4]), [(pT[:, j * 128:(j + 1) * 128], OB3[:, j, :]) for j in range(4)])
            self.E("act", "activation", self.pk([4]) + ["small"], kyb, out=YB[:, qg * 512:(qg + 1) * 512], in_=pT[:, 0:512],
                   func=AF.Copy, scale=self.small[:, 28 + a:29 + a])
        self.DMA("sp", "ysto", self.y_s[8 + a], YB, kyb, [("y", 8 + a)])

    def hgrn_setup(self):
        MSK, kmsk = self.sv(self.M_MSK, 2048, BF16)
        self.E("pool", "memset", [], kmsk, ap=MSK, constant=1.0)
        self.E("pool", "memset", [], kmsk, ap=MSK.rearrange("p (c k) -> p c k", k=64)[:, :, 0:1], constant=0.0)

    def hgrn_head(self, l, h, rb, krb):
        sv = self.sv
        MSK, kmsk = sv(self.M_MSK, 2048, BF16)
        T1, k1 = sv(self.M_T1, 2048, F32)
        T2, k2 = sv(self.M_T2, 2048, F32)
        T3, k3 = sv(self.M_T3, 2048, F32)
        QS, kqs = sv(self.M_B1, 2048, BF16)
        GS, kgs = sv(self.M_B2, 2048, BF16)
        VT, kvt = sv(self.M_B3, 2048, BF16)
        VT3 = VT.rearrange("p (i c) -> p i c", i=16)
        QP, kqp = sv(self.M_B4, 2048, BF16)
        KP, kkp = sv(self.M_B5, 2048, BF16)
        KPT, kkpt = sv(self.M_B6, 2048, BF16)
        KPT3 = KPT.rearrange("p (i c) -> p i c", i=16)
        sm0 = self.M_SM
        Sst = [sv(sm0 + j * 512, 128, F32) for j in range(2)]
        Sp = [sv(sm0 + 1024 + j * 256, 128, BF16) for j in range(2)]
        MS = [sv(sm0 + 1536 + j * 256, 128, BF16) for j in range(2)]
        CL = [sv(sm0 + 2048 + j * 512, 128, F32) for j in range(2)]
        KV, kkv = sv(sm0 + 3072, 128, F32)
        MC, kmc = sv(sm0 + 3584, 32, F32)
        BC, kbc = sv(sm0 + 3712, 32, F32)
        EM, kem = sv(sm0 + 3840, 32, F32)
        EB, keb = sv(sm0 + 3968, 32, F32)
        EBM, kebm = sv(sm0 + 4096, 32, F32)
        T3v = T3.rearrange("p (c k) -> p c k", k=64)
        self.proj_fm(l, h, 0)
        for tg in range(4):
            self.E("dve", "tensor_tensor", self.pk([tg]) + krb, kqs, out=QS[:, tg * 512:(tg + 1) * 512], in0=self.ps[tg][:],
                   in1=rb[:, tg * 512:(tg + 1) * 512], op=ALU.mult)
        self.proj_fm(l, 32 + h, 0)
        for tg in range(4):
            self.E("dve", "tensor_tensor", self.pk([tg]) + krb, k1, out=T1[:, tg * 512:(tg + 1) * 512], in0=self.ps[tg][:],
                   in1=rb[:, tg * 512:(tg + 1) * 512], op=ALU.mult)
        self.E("act", "activation", k1, kgs, out=GS, in_=T1, func=AF.Silu)
        self.proj_tm(l, 8 + h, 0)
        for b in range(4):
            self.E("dve", "tensor_tensor", self.pk([b]) + ["rcol"], kvt, out=VT3[:, 4 * b:4 * b + 4, :],
                   in0=self.ps[b][:].rearrange("p (i c) -> p i c", i=4),
                   in1=self.rcol[:, 4 * b:4 * b + 4].unsqueeze(2).to_broadcast([128, 4, 128]), op=ALU.mult)
        for b in range(4, 8):
            self.MM(kqs + ["zeros_bf"], self.pk([b]), [(self.ps[b][:], self.zeros_bf[:], QS[:, 0:512], True, True)])
        lbi = lambda d: self.lb[:, d * 8 + h, l:l + 1]
        omli = lambda d: self.oml[:, d * 8 + h, l:l + 1]
        for d in range(2):
            self.proj_fm(l, 16 + 8 * d + h, 0)
            for tg in range(4):
                self.E("dve", "tensor_tensor", self.pk([tg]) + krb, k1, out=T1[:, tg * 512:(tg + 1) * 512], in0=self.ps[tg][:],
                       in1=rb[:, tg * 512:(tg + 1) * 512], op=ALU.mult)
            self.E("act", "activation", k1, k1, out=T1, in_=T1, func=AF.Sigmoid)
            self.E("dve", "tensor_scalar", k1 + ["lb", "oml"], k1, out=T1, in0=T1, scalar1=omli(d), scalar2=lbi(d), op0=ALU.mult,
                   op1=ALU.add)
            self.E("act", "activation", k1, k2, out=T2, in_=T1, func=AF.Ln)
            self.E("dve", "tensor_scalar", k1, k1, out=T1, in0=T1, scalar1=-1.0, scalar2=1.0, op0=ALU.mult, op1=ALU.add)
            self.E("dve", "tensor_tensor_scan", k2 + kmsk, k3, out=T3, data0=MSK, data1=T2, initial=0.0, op0=ALU.mult,
                   op1=ALU.add)
            if d == 0:
                self.E("dve", "tensor_copy", k3, kmc, out=MC, in_=T3v[:, :, 31])
                self.E("dve", "tensor_copy", k3, kbc, out=BC, in_=T3v[:, :, 63])
            else:
                self.E("dve", "tensor_copy", k3, kbc, out=BC, in_=T3v[:, :, 63])
                self.E("dve", "tensor_tensor", k2 + k3, k3, out=T3, in0=T2, in1=T3, op=ALU.subtract)
                self.E("dve", "tensor_tensor", k3 + kbc, k3, out=T3v, in0=T3v, in1=BC.unsqueeze(2).to_broadcast([128, 32, 64]),
                       op=ALU.add)
                self.E("dve", "tensor_copy", k3, kmc, out=MC, in_=T3v[:, :, 32])
            self.E("dve", "tensor_tensor", k3 + kmc, k3, out=T3v, in0=T3v, in1=MC.unsqueeze(2).to_broadcast([128, 32, 64]),
                   op=ALU.subtract)
            self.E("act", "activation", k3, k2, out=T2, in_=T3, func=AF.Exp)
            self.E("dve", "tensor_tensor", k2 + kqs, kqp, out=QP, in0=T2, in1=QS, op=ALU.mult)
            self.E("act", "activation", k3, k2, out=T2, in_=T3, func=AF.Exp, scale=-1.0)
            self.E("pool", "tensor_tensor", k2 + k1, kkp, out=KP, in0=T2, in1=T1, op=ALU.mult)
            self.E("act", "activation", kmc, kem, out=EM, in_=MC, func=AF.Exp)
            self.E("act", "activation", kbc, keb, out=EB, in_=BC, func=AF.Exp)
            self.E("dve", "tensor_tensor", kbc + kmc, kebm, out=EBM, in0=BC, in1=MC, op=ALU.subtract)
            self.E("act", "activation", kebm, kebm, out=EBM, in_=EBM, func=AF.Exp)
            for hb in range(2):
                pT = self.ps[hb][:].bitcast(BF16)
                self.TR(kkp, self.pk([hb]), [(pT[:, j * 128:(j + 1) * 128], KP[:, (hb * 8 + j) * 128:(hb * 8 + j + 1) * 128])
                                             for j in range(8)])
                self.E("act", "activation", self.pk([hb]), kkpt, out=KPT[:, hb * 1024:(hb + 1) * 1024], in_=pT, func=AF.Copy)
            mask = self.mask_f if d == 0 else self.mask_b
            mkey = "mask_f" if d == 0 else "mask_b"
            for i in range(16):
                j = i % 2
                bsc = 2 + j
                self.MM(kkp + kqp, self.pk([bsc]), [(self.ps[bsc][:, 0:128], KP[:, i * 128:(i + 1) * 128], QP[:, i * 128:(i + 1) * 128],
                                                    True, True)])
                cl, kcl = CL[j]
                ms, kms = MS[j]
                self.E("dve", "tensor_scalar", self.pk([bsc]), kcl, out=cl, in0=self.ps[bsc][:, 0:128], scalar1=1e30, scalar2=-1e30,
                       op0=ALU.min, op1=ALU.max)
                self.E("pool", "tensor_tensor", kcl + [mkey], kms, out=ms, in0=cl, in1=mask[:], op=ALU.mult)
                bo = 4 + i // 4
                self.MM(kms + kvt, self.pk([bo]), [(self.ps[bo][:, (i % 4) * 128:(i % 4 + 1) * 128], VT3[:, i, :], ms, False, False)])
            order = list(range(32)) if d == 0 else list(range(31, -1, -1))
            for step, c in enumerate(order):
                i, hh = c // 2, c % 2
                pj = step % 2
                st, kst = Sst[pj]
                stn, kstn = Sst[1 - pj]
                sp_, ksp = Sp[pj]
                bo = 4 + c // 8
                if step > 0:
                    self.MM(ksp + kqp, self.pk([bo]), [(self.ps[bo][:, (c % 8) * 64:(c % 8 + 1) * 64], sp_, QP[:, c * 64:(c + 1) * 64],
                                                       False, False)])
                if step == 31:
                    break
                ba = step % 2
                self.MM(kkpt + kvt, self.pk([ba]), [(self.ps[ba][:, 0:128], KPT3[hh * 64:(hh + 1) * 64, i, :],
                                                    VT3[hh * 64:(hh + 1) * 64, i, :], True, True)])
                self.E("dve", "tensor_scalar", self.pk([ba]) + kebm, kkv, out=KV, in0=self.ps[ba][:, 0:128], scalar1=EBM[:, c:c + 1],
                       scalar2=None, op0=ALU.mult)
                if step == 0:
                    self.E("dve", "tensor_copy", kkv, kstn, out=stn, in_=KV)
                else:
                    self.E("dve", "scalar_tensor_tensor", kst + kkv + keb, kstn, out=stn, in0=st, scalar=EB[:, c:c + 1], in1=KV,
                           op0=ALU.mult, op1=ALU.add)
                cn = order[step + 1]
                spn, kspn = Sp[1 - pj]
                self.E("act", "activation", kstn + kem, kspn, out=spn, in_=stn, func=AF.Copy, scale=EM[:, cn:cn + 1])
        for tg in range(4):
            self.E("act", "activation", self.pk([4 + tg]), k1, out=T1[:, tg * 512:(tg + 1) * 512], in_=self.ps[4 + tg][:], func=AF.Copy)
        SQ, ksq = sv(self.M_B6, 2048, BF16)
        YB, kyb = sv(self.M_B5, 2048, BF16)
        self.pnorm_store(T1, k1, T2, k2, SQ, ksq, YB, kyb, 0, self.small[:, 20 + h:21 + h], GS, kgs, h)


def build(cfg):
    nc = bass.Bass("TRN2", target_bir_lowering=False)
    with ExitStack() as es:
        k = K(nc, es, cfg)
        k.setup()
        layers = cfg.get("layers", list(range(DEPTH)))
        k.mixer_setup()
        k.prep(layers)
        for s in range(cfg.get("nseq", 2)):
            k.load_x(s)
            if "mix" in cfg["stages"]:
                k.cossin(s)
            for l in layers:
                if "ffn1" in cfg["stages"]:
                    k.ffn(l, 0)
                if "mix" in cfg["stages"]:
                    k.mixer(l, s)
                if "ffn2" in cfg["stages"]:
                    k.ffn(l, 1)
            k.store_x(s)
        k.P.wait_all("sp")
        k.P.emit()
    return nc


def host_layouts(inp):
    L = DEPTH
    f = lambda a: np.ascontiguousarray(np.asarray(a, dtype=np.float32))
    out = {}

    def kxn(w, ncol_blocks):
        w = np.asarray(w, dtype=np.float32).reshape(L, 16, 128, ncol_blocks, 128)
        return w.transpose(0, 3, 2, 1, 4)

    wgu = np.empty((L, 2, NFC, 128, 2, 16, 128), np.float32)
    for fi, (g, u) in enumerate((("ffn1_w_gate", "ffn1_w_up"), ("ffn2_w_gate", "ffn2_w_up"))):
        wgu[:, fi, :, :, 0] = kxn(inp[g], NFC)
        wgu[:, fi, :, :, 1] = kxn(inp[u], NFC)
    out["wgu"] = wgu.reshape(L, 2, NFC, 128, 4096)
    wd = np.empty((L, 2, 16, 128, NFC, 128), np.float32)
    for fi, dname in enumerate(("ffn1_w_down", "ffn2_w_down")):
        w = np.asarray(inp[dname], dtype=np.float32).reshape(L, NFC, 128, 16, 128)
        wd[:, fi] = w.transpose(0, 3, 2, 1, 4)
    out["wd"] = wd.reshape(L, 2, -1, 1376)
    win = kxn(inp["w_in"], NBLK)
    out["win"] = np.ascontiguousarray(win).reshape(L, NBLK // 2, 2, 128, 16, 128).transpose(0, 1, 3, 2, 4, 5).reshape(
        L, NBLK // 2, 128, 4096)
    out["wout"] = np.ascontiguousarray(kxn(inp["w_out"], 16)).reshape(L, -1, 2048)
    gains = np.stack([np.asarray(inp[n], dtype=np.float32).reshape(L, 16, 128).transpose(0, 2, 1)
                      for n in ("ffn1_norm", "mix_norm", "ffn2_norm")], axis=1)
    out["gains"] = f(gains)
    import ml_dtypes
    bf = ml_dtypes.bfloat16
    out["ident"] = np.eye(128, dtype=np.float32).astype(bf)
    ii = np.arange(128)
    same = (ii[:, None] // 64) == (ii[None, :] // 64)
    mf = (same & (ii[:, None] <= ii[None, :])).astype(np.float32)
    mb = (same & (ii[:, None] >= ii[None, :])).astype(np.float32)
    out["masks"] = np.stack([mf, mb]).astype(bf)
    inv_freq = (10000.0 ** (-np.arange(0, 64, 2, dtype=np.float32) / 64)).astype(np.float32)
    out["invf"] = np.broadcast_to(inv_freq[None, :], (128, 32)).copy()
    lg = np.asarray(inp["hgrn_lb_logits"], dtype=np.float32).reshape(2, L, 8, 128)
    out["lbl"] = lg.transpose(3, 0, 2, 1).reshape(128, 16, L)
    small = np.zeros((L, 128, 32), np.float32)
    cw = np.asarray(inp["conv_w"], dtype=np.float32).reshape(L, 3, 4, 128)
    cb = np.asarray(inp["conv_b"], dtype=np.float32).reshape(L, 4, 128)
    cn = np.asarray(inp["conv_norm"], dtype=np.float32).reshape(L, 4, 128)
    for gi in range(4):
        for j in range(3):
            small[:, :, gi * 5 + j] = cw[:, j, gi, :]
        small[:, :, gi * 5 + 3] = cb[:, gi, :]
        small[:, :, gi * 5 + 4] = cn[:, gi, :]
    small[:, :, 20:28] = np.asarray(inp["hgrn_norm"], dtype=np.float32).reshape(L, 8, 128).transpose(0, 2, 1)
    small[:, :, 28:32] = np.asarray(inp["da_out_norm"], dtype=np.float32).reshape(L, 4, 128).transpose(0, 2, 1)
    out["small"] = small
    bcv = np.concatenate([np.asarray(inp[n], dtype=np.float32) for n in
                          ("da_q_norm", "da_k_norm", "da_lambda_q1", "da_lambda_k1", "da_lambda_q2", "da_lambda_k2")], axis=1)
    out["bc"] = np.broadcast_to(bcv[:, None, :], (L, 128, 384)).copy()
    for k_ in out:
        if out[k_].dtype == np.float32:
            out[k_] = f(out[k_])
    return out


CFG_FULL = {"stages": ("ffn1", "mix", "ffn2"), "nseq": 2}


def kernel(**inputs):
    x = np.asarray(inputs["x"], dtype=np.float32)
    shared = host_layouts(inputs)
    nc = build(CFG_FULL)
    in_maps = []
    for core in range(NCORES):
        xs = x[2 * core:2 * core + 2]
        xT = np.ascontiguousarray(xs.transpose(0, 2, 1)).reshape(2, 16, 128, S)
        pos = np.asarray(inputs["positions"])[2 * core:2 * core + 2].astype(np.int32)
        m = {"xT": xT, "pos": np.ascontiguousarray(pos.reshape(2, 16, 128).transpose(0, 2, 1))}
        m.update(shared)
        in_maps.append(m)
    res = run_bass_kernel_spmd(nc, in_maps, core_ids=list(range(NCORES)))
    out = np.empty((16, S, D), np.float32)
    for core in range(NCORES):
        yT = res.results[core]["yT"].reshape(2, D, S)
        out[2 * core:2 * core + 2] = yT.transpose(0, 2, 1)
    return out
```

```python
import math
from contextlib import ExitStack

import numpy as np
import concourse.bass as bass
import concourse.mybir as mybir
from concourse.bass_utils import run_bass_kernel_spmd

F32 = mybir.dt.float32
BF16 = mybir.dt.bfloat16
I32 = mybir.dt.int32
ALU = mybir.AluOpType
AF = mybir.ActivationFunctionType
AX = mybir.AxisListType

D = 2048
S = 2048
DFF = 5504
NFC = 43
DEPTH = 4
NCORES = 8
EPS = 1e-6
NBLK = 64

COMPUTE = ("pe", "act", "dve", "pool")
ALLENG = ("pe", "act", "dve", "pool", "sp")


class Prog:
    def __init__(self, nc, es):
        self.nc = nc
        self.es = es
        self.streams = {e: [] for e in ALLENG}
        self.sem = {e: es.enter_context(nc.semaphore("s_" + e)) for e in COMPUTE}
        self.cnt = {e: 0 for e in COMPUTE}
        self.dsem = {}
        self.dcnt = {}
        self.waited = {e: {} for e in ALLENG}
        self.lastw = {}
        self.readers = {}

    def _deps(self, eng, reads, writes, is_dma=False):
        toks = []
        for r in reads:
            for t in self.lastw.get(r, {}).values():
                toks.append((t, True))
        for w in writes:
            for t in self.lastw.get(w, {}).values():
                toks.append((t, False))
            rd = self.readers.get(w)
            if rd:
                for t in rd.values():
                    toks.append((t, False))
        need = {}
        for (kind, key, val), raw in toks:
            if not is_dma and kind == "e" and key == eng:
                if not raw or eng == "pe":
                    continue
            k = (kind, key)
            if self.waited[eng].get(k, 0) >= val:
                continue
            if need.get(k, 0) < val:
                need[k] = val
        for k, v in need.items():
            self.waited[eng][k] = v
        return [(k[0], k[1], v) for k, v in need.items()]

    def _record(self, tok, reads, writes):
        for w in writes:
            self.lastw.setdefault(w, {})[(tok[0], tok[1])] = tok
            self.readers[w] = {}
        for r in reads:
            self.readers.setdefault(r, {})[(tok[0], tok[1])] = tok

    def op(self, eng, fn, reads=(), writes=()):
        waits = self._deps(eng, reads, writes)
        self.cnt[eng] += 1
        tok = ("e", eng, self.cnt[eng])
        self._record(tok, reads, writes)
        self.streams[eng].append(("op", waits, fn, None))
        return tok

    def dma(self, eng, semname, fn, reads=(), writes=(), n=1):
        if semname not in self.dsem:
            self.dsem[semname] = self.es.enter_context(self.nc.semaphore("d_" + semname))
            self.dcnt[semname] = 0
        waits = self._deps(eng, reads, writes, is_dma=True)
        self.dcnt[semname] += 16 * n
        tok = ("d", semname, self.dcnt[semname])
        self._record(tok, reads, writes)
        self.streams[eng].append(("dma", waits, fn, semname))
        return tok

    def wait_all(self, eng):
        waits = []
        for e in COMPUTE:
            if self.cnt[e] > 0 and e != eng:
                waits.append(("e", e, self.cnt[e]))
        for s, v in self.dcnt.items():
            if v > 0:
                waits.append(("d", s, v))
        self.streams[eng].append(("wait", waits, None, None))

    def emit(self):
        prog = self

        def semh(kind, key):
            return prog.sem[key] if kind == "e" else prog.dsem[key]

        def run(engname, eobj):
            for kind, waits, fn, semname in prog.streams[engname]:
                for (k, key, v) in waits:
                    eobj.wait_ge(semh(k, key), v)
                if kind == "op":
                    fn(eobj).then_inc(prog.sem[engname], 1)
                elif kind == "dma":
                    inss = fn(eobj)
                    if not isinstance(inss, (list, tuple)):
                        inss = [inss]
                    for i in inss:
                        i.then_inc(prog.dsem[semname], 16)

        with self.nc.Block() as block:
            @block.tensor
            def _(e):
                run("pe", e)

            @block.scalar
            def _(e):
                run("act", e)

            @block.vector
            def _(e):
                run("dve", e)

            @block.gpsimd
            def _(e):
                run("pool", e)

            @block.sync
            def _(e):
                run("sp", e)


SCR_BYTES = 75 * 1024
PAGE = 1024


class K:
    def __init__(self, nc, es, cfg):
        self.nc, self.es, self.cfg = nc, es, cfg
        L = DEPTH
        dt = nc.dram_tensor
        ein = dict(kind="ExternalInput")
        self.xT = dt("xT", [2, 16, 128, S], F32, **ein).ap()
        self.wgu = dt("wgu", [L, 2, NFC, 128, 4096], F32, **ein).ap()
        self.wd = dt("wd", [L, 2, 16 * 128 * 4, 1376], F32, **ein).ap()
        self.win = dt("win", [L, NBLK // 2, 128, 4096], F32, **ein).ap()
        self.wout = dt("wout", [L, 16 * 128 * 16 * 128 // 2048, 2048], F32, **ein).ap()
        self.gains = dt("gains", [L, 3, 128, 16], F32, **ein).ap()
        self.yT = dt("yT", [2, 16, 128, S], F32, kind="ExternalOutput").ap()
        self.ident_d = dt("ident", [128, 128], BF16, **ein).ap()
        self.mask_d = dt("masks", [2, 128, 128], BF16, **ein).ap()
        self.invf_d = dt("invf", [128, 32], F32, **ein).ap()
        self.lbl_d = dt("lbl", [128, 16, 4], F32, **ein).ap()
        self.pos_d = dt("pos", [2, 128, 16], I32, **ein).ap()
        self.small_d = dt("small", [L, 128, 32], F32, **ein).ap()
        self.bc_d = dt("bc", [L, 128, 384], F32, **ein).ap()
        self.cs_s = dt("cs_s", [2, 128, 512], F32).ap()
        self.y_s = [dt(f"y_s{c}", [128, S], BF16).ap() for c in range(16)]
        self.wgu_s = [[dt(f"wgu_s{l}_{f}", [NFC, 128, 4096], BF16).ap() for f in range(2)] for l in range(L)]
        self.wd_s = [[dt(f"wd_s{l}_{f}", [16 * 128 * 4, 1376], BF16).ap() for f in range(2)] for l in range(L)]
        self.win_s = [dt(f"win_s{l}", [NBLK // 2, 128, 4096], BF16).ap() for l in range(L)]
        self.wout_s = [dt(f"wout_s{l}", [16 * 128 * 16 * 128 // 2048, 2048], BF16).ap() for l in range(L)]

        sb = lambda name, shape, d: es.enter_context(nc.sbuf_tensor(name, shape, d))
        self.X = sb("X", [128, 16, S], F32)
        self.SCR = sb("SCR", [128, SCR_BYTES // 2], BF16)
        self.ones_bf = sb("ones_bf", [128, 128], BF16)
        self.gtile = sb("gtile", [128, 3, 16], F32)
        self.eps_t = sb("eps_t", [128, 1], F32)
        self.ident = sb("ident_sb", [128, 128], BF16)
        self.zeros_bf = sb("zeros_bf", [128, 128], BF16)
        self.one_f = sb("one_f", [128, 1], F32)
        self.mask_f = sb("mask_f", [128, 128], BF16)
        self.mask_b = sb("mask_b", [128, 128], BF16)
        self.invf = sb("invf_sb", [128, 32], F32)
        self.lbl = sb("lbl_sb", [128, 16, 4], F32)
        self.lbs = sb("lbs", [128, 16], F32)
        self.lb = sb("lb", [128, 16, 4], F32)
        self.oml = sb("oml", [128, 16, 4], F32)
        self.rcol = sb("rcol", [128, 16], F32)
        self.r2col = sb("r2col", [128, 16], F32)
        self.small = sb("small_sb", [128, 32], F32)
        self.bc = sb("bc_sb", [128, 384], F32)
        self.nlam = sb("nlam", [128, 1], F32)
        self.wcnt = 0
        self.ps = [es.enter_context(nc.psum_tensor(f"ps{i}", [128, 512], F32)) for i in range(8)]
        self.P = Prog(nc, es)
        self.Xhi = self.X[:].bitcast(BF16)

    def sv(self, off, nelem, dtype):
        nbytes = nelem * (4 if dtype == F32 else 2)
        assert off % 4 == 0 and off + nbytes <= SCR_BYTES, (off, nbytes)
        v = self.SCR[:, off // 2:(off + nbytes) // 2]
        if dtype == F32:
            v = v.bitcast(F32)
        keys = [("S", pg) for pg in range(off // PAGE, (off + nbytes - 1) // PAGE + 1)]
        return v, keys

    def xhi(self, c, t0, t1):
        return self.Xhi[:, c, 2 * t0 + 1:2 * t1:2]

    @staticmethod
    def xk(cs, tgs):
        return [("X", c, tg) for c in cs for tg in tgs]

    def setup(self):
        P = self.P
        P.op("pool", lambda e: e.memset(self.ones_bf[:], 1.0), writes=["ones_bf"])
        P.op("pool", lambda e: e.memset(self.eps_t[:], EPS), writes=["eps_t"])

    def prep(self, layers):
        P = self.P
        IN_SLOTS, OUT_SLOTS = 3, 3
        in_off = [i * 16384 for i in range(IN_SLOTS)]
        out_off = [IN_SLOTS * 16384 + i * 8192 for i in range(OUT_SLOTS)]
        cnt = 0
        for l in layers:
            P.dma("sp", "gl", lambda e, l=l: e.dma_start(out=self.gtile[:], in_=self.gains[l].rearrange("g p k -> p g k")),
                  writes=["gtile"])
            jobs = []
            for f in range(2):
                for c in range(NFC):
                    jobs.append((self.wgu[l, f, c], self.wgu_s[l][f][c], 0 if f == 0 else 2))
            for b in range(NBLK // 2):
                jobs.append((self.win[l, b], self.win_s[l][b], 1))
            for src, dst, gi in jobs:
                si, so = cnt % IN_SLOTS, cnt % OUT_SLOTS
                tin, kin = self.sv(in_off[si], 4096, F32)
                tout, kout = self.sv(out_off[so], 4096, BF16)
                P.dma("sp", f"pi{si}", lambda e, tin=tin, src=src: e.dma_start(out=tin, in_=src), writes=kin)
                eng = "dve" if cnt % 3 != 2 else "pool"
                g_b = self.gtile[:, gi, :].unsqueeze(1).unsqueeze(3).to_broadcast([128, 2, 16, 128])
                tin4 = tin.rearrange("p (j k c) -> p j k c", j=2, k=16)
                tout4 = tout.rearrange("p (j k c) -> p j k c", j=2, k=16)
                P.op(eng, lambda e, a=tout4, b=tin4, g=g_b: e.tensor_tensor(out=a, in0=b, in1=g, op=ALU.mult),
                     reads=kin + ["gtile"], writes=kout)
                P.dma("act", f"po{so}", lambda e, tout=tout, dst=dst: e.dma_start(out=dst, in_=tout), reads=kout,
                      writes=[("W", "gu_in", l)])
                cnt += 1
            for f in range(2):
                n_rows = self.wd.shape[2]
                step = n_rows // 8
                for i in range(8):
                    P.dma("pool", "cast", lambda e, l=l, f=f, i=i, step=step: e.dma_start(
                        out=self.wd_s[l][f][i * step:(i + 1) * step, :], in_=self.wd[l, f, i * step:(i + 1) * step, :]),
                        writes=[("W", "d", l)])
            n_rows = self.wout.shape[1]
            step = n_rows // 4
            for i in range(4):
                P.dma("pool", "cast", lambda e, l=l, i=i, step=step: e.dma_start(
                    out=self.wout_s[l][i * step:(i + 1) * step, :], in_=self.wout[l, i * step:(i + 1) * step, :]),
                    writes=[("W", "d", l)])

    def load_x(self, s):
        for c in range(16):
            self.P.dma("sp", "ldx", lambda e, c=c: e.dma_start(out=self.X[:, c, :], in_=self.xT[s, c]),
                       writes=self.xk([c], range(4)))

    def store_x(self, s):
        for c in range(16):
            self.P.dma("sp", "stx", lambda e, c=c: e.dma_start(out=self.yT[s, c], in_=self.X[:, c, :]),
                       reads=self.xk([c], range(4)))

    def rstd(self, t0, nt, r_view, r_keys, sq_off, scale_extra=None, rh_view=None, rh_keys=None, bank=6):
        P = self.P
        tgs = sorted({t // 512 for t in (t0, t0 + nt - 1)})
        for c in range(16):
            sq, ksq = self.sv(sq_off + (c % 2) * 1024, 512, BF16)
            P.op("act", lambda e, c=c, sq=sq: e.activation(out=sq[:, 0:nt], in_=self.X[:, c, t0:t0 + nt], func=AF.Square),
                 reads=self.xk([c], tgs), writes=ksq)
            P.op("pe", lambda e, c=c, sq=sq: e.matmul(self.ps[bank][:, 0:nt], lhsT=self.ones_bf[:], rhs=sq[:, 0:nt],
                                                     start=(c == 0), stop=(c == 15)),
                 reads=ksq + ["ones_bf"], writes=self.pk([bank]))
        P.op("act", lambda e: e.activation(out=r_view[:, 0:nt], in_=self.ps[bank][:, 0:nt], func=AF.Sqrt,
                                           bias=self.eps_t[:], scale=1.0 / D),
             reads=self.pk([bank]) + ["eps_t"], writes=r_keys)
        P.op("dve", lambda e: e.reciprocal(out=r_view[:, 0:nt], in_=r_view[:, 0:nt]), reads=r_keys, writes=r_keys)
        if rh_view is not None:
            P.op("pool", lambda e: e.tensor_scalar(out=rh_view[:, 0:nt], in0=r_view[:, 0:nt], scalar1=0.5, scalar2=None,
                                                   op0=ALU.mult), reads=r_keys, writes=rh_keys)

    def ffn(self, l, f):
        P = self.P
        ACT0 = 0
        WS = 44032
        WSZ = 22528
        RB = WS + WSZ
        RH = RB + 2048
        TA = RH + 2048
        TB = TA
        SQ = TA + 4096
        assert SQ + 2048 <= SCR_BYTES
        for tg in range(4):
            self._ffn_tg(l, f, tg, ACT0, WS, RB, RH, TA, TB, SQ)

    def _ffn_tg(self, l, f, tg, ACT0, WS, RB, RH, TA, TB, SQ):
        P = self.P
        if True:
            t0 = tg * 512
            rb, krb = self.sv(RB, 512, F32)
            rh, krh = self.sv(RH, 512, F32)
            self.rstd(t0, 512, rb, krb, SQ, rh_view=rh, rh_keys=krh)
            for c in range(NFC):
                slot = c % 2
                wt, kw = self.sv(WS + slot * 8192, 4096, BF16)
                wt4 = wt.rearrange("p (j k c) -> p j k c", j=2, k=16)
                P.dma("sp", f"wa{slot}", lambda e, wt=wt, c=c: e.dma_start(out=wt, in_=self.wgu_s[l][f][c]),
                      reads=[("W", "gu_in", l)], writes=kw)
                bg, bu = c % 2, 2 + c % 2

                def mm(e, j, bank, wt4=wt4):
                    ins = None
                    for kc in range(16):
                        ins = e.matmul(self.ps[bank][:], lhsT=wt4[:, j, kc, :], rhs=self.xhi(kc, t0, t0 + 512),
                                       start=(kc == 0), stop=(kc == 15))
                    return ins
                P.op("pe", lambda e, mm=mm, bg=bg: mm(e, 0, bg), reads=kw + self.xk(range(16), [tg]), writes=self.pk([bg]))
                P.op("pe", lambda e, mm=mm, bu=bu: mm(e, 1, bu), reads=kw + self.xk(range(16), [tg]), writes=self.pk([bu]))
                ta, kta = self.sv(TA + (c % 2) * 2048, 512, F32)
                P.op("dve", lambda e, ta=ta, bg=bg: e.tensor_tensor(out=ta, in0=self.ps[bg][:], in1=rb, op=ALU.mult),
                     reads=self.pk([bg]) + krb, writes=kta)
                P.op("act", lambda e, ta=ta: e.activation(out=ta, in_=ta, func=AF.Silu), reads=kta, writes=kta)
                av, kav = self.sv(ACT0 + c * 1024, 512, BF16)
                P.op("dve", lambda e, ta=ta, av=av, bu=bu: e.tensor_tensor(out=av, in0=self.ps[bu][:], in1=ta, op=ALU.mult),
                     reads=self.pk([bu]) + kta, writes=kav)
            actv, kact = self.sv(ACT0, NFC * 512, BF16)
            act3 = actv.rearrange("p (c t) -> p c t", c=NFC)
            rows_per_n = 512
            for n in range(16):
                slot = n % 2
                wt, kw = self.sv(WS + slot * 11264, NFC * 128, BF16)
                wt3 = wt.rearrange("p (c n) -> p c n", c=NFC)
                src = self.wd_s[l][f][n * rows_per_n:(n + 1) * rows_per_n, :].rearrange("(p a) b -> p (a b)", p=128)
                P.dma("sp", f"wb{slot}", lambda e, wt=wt, src=src: e.dma_start(out=wt, in_=src),
                      reads=[("W", "d", l)], writes=kw)
                bd = 4 + n % 2

                def mmd(e, bd=bd, wt3=wt3):
                    ins = None
                    for fc in range(NFC):
                        ins = e.matmul(self.ps[bd][:], lhsT=wt3[:, fc, :], rhs=act3[:, fc, :],
                                       start=(fc == 0), stop=(fc == NFC - 1))
                    return ins
                P.op("pe", mmd, reads=kw + kact, writes=self.pk([bd]))
                tb, ktb = self.sv(TB + (n % 2) * 2048, 512, F32)
                P.op("dve", lambda e, tb=tb, bd=bd: e.tensor_tensor(out=tb, in0=self.ps[bd][:], in1=rh, op=ALU.mult),
                     reads=self.pk([bd]) + krh, writes=ktb)
                xv = self.X[:, n, t0:t0 + 512]
                P.op("pool", lambda e, tb=tb, xv=xv: e.tensor_tensor(out=xv, in0=xv, in1=tb, op=ALU.add),
                     reads=ktb + self.xk([n], [tg]), writes=self.xk([n], [tg]))

    def E(self, eng, meth, reads, writes, **kw):
        return self.P.op(eng, lambda e: getattr(e, meth)(**kw), reads=list(reads), writes=list(writes))

    def MM(self, reads, writes, mms):
        def fn(e):
            ins = None
            for (o, l_, r_, st, sp) in mms:
                ins = e.matmul(o, lhsT=l_, rhs=r_, start=st, stop=sp)
            return ins
        return self.P.op("pe", fn, reads=list(reads), writes=list(writes))

    def TR(self, reads, writes, trs):
        def fn(e):
            ins = None
            for (o, i_) in trs:
                ins = e.transpose(o, i_, self.ident[:])
            return ins
        return self.P.op("pe", fn, reads=list(reads) + ["ident"], writes=list(writes))

    def DMA(self, eng, sem, out, in_, reads, writes):
        return self.P.dma(eng, sem, lambda e: e.dma_start(out=out, in_=in_), reads=list(reads), writes=list(writes))

    def pk(self, banks):
        return [("ps", b) for b in banks]

    def pq(self, b, q):
        return [("ps", b)]

    M_RB, M_WS, M_T1, M_T2, M_T3 = 0, 8192, 16384, 24640, 32832
    M_B1, M_B2, M_B3, M_B4, M_B5, M_B6, M_MSK, M_SM = 41024, 45120, 49216, 53568, 57664, 61760, 65856, 69952

    def wblk(self, l, blk):
        return self.win_s[l][blk // 2][:, (blk % 2) * 2048:(blk % 2 + 1) * 2048]

    def load_w(self, l, blk):
        slot = self.wcnt % 2
        self.wcnt += 1
        wt, kw = self.sv(self.M_WS + slot * 4096, 2048, BF16)
        self.DMA("sp", f"mw{slot}", wt, self.wblk(l, blk), [("W", "gu_in", l)], kw)
        return wt.rearrange("p (k c) -> p k c", k=16), kw

    def proj_fm(self, l, blk, b0):
        wt3, kw = self.load_w(l, blk)
        for tg in range(4):
            self.MM(kw + self.xk(range(16), [tg]), self.pk([b0 + tg]),
                    [(self.ps[b0 + tg][:], wt3[:, kc, :], self.xhi(kc, tg * 512, tg * 512 + 512), kc == 0, kc == 15)
                     for kc in range(16)])

    def proj_tm(self, l, blk, b0):
        wt3, kw = self.load_w(l, blk)
        for i in range(16):
            o = self.ps[b0 + i // 4][:, (i % 4) * 128:(i % 4 + 1) * 128]
            self.MM(kw + self.xk(range(16), [i // 4]), self.pk([b0 + i // 4]),
                    [(o, self.xhi(kc, i * 128, i * 128 + 128), wt3[:, kc, :], kc == 0, kc == 15) for kc in range(16)])

    def mixer_setup(self):
        P = self.P
        self.E("pool", "memset", [], ["zeros_bf"], ap=self.zeros_bf[:], constant=0.0)
        self.E("pool", "memset", [], ["one_f"], ap=self.one_f[:], constant=1.0)
        self.DMA("sp", "cst", self.ident[:], self.ident_d[:, :], [], ["ident"])
        self.DMA("sp", "cst", self.mask_f[:], self.mask_d[0], [], ["mask_f"])
        self.DMA("sp", "cst", self.mask_b[:], self.mask_d[1], [], ["mask_b"])
        self.DMA("sp", "cst", self.invf[:], self.invf_d[:, :], [], ["invf"])
        lbl = self.lbl
        self.DMA("sp", "cst", lbl[:], self.lbl_d[:, :, :], [], ["lbl"])
        self.E("act", "activation", ["lbl"], ["lbl"], out=lbl[:], in_=lbl[:], func=AF.Exp)
        self.E("dve", "tensor_reduce", ["lbl"], ["lbs"], out=self.lbs[:], in_=lbl[:], axis=AX.X, op=ALU.add)
        self.E("dve", "reciprocal", ["lbs"], ["lbs"], out=self.lbs[:], in_=self.lbs[:])
        self.E("dve", "tensor_tensor", ["lbl", "lbs"], ["lbl"], out=lbl[:], in0=lbl[:],
               in1=self.lbs[:].unsqueeze(2).to_broadcast([128, 16, 4]), op=ALU.mult)
        self.E("pool", "memset", [], ["lb"], ap=self.lb[:, :, 0:1], constant=0.0)
        for j in range(1, 4):
            self.E("dve", "tensor_tensor", ["lbl", "lb"], ["lb"], out=self.lb[:, :, j:j + 1], in0=self.lb[:, :, j - 1:j],
                   in1=lbl[:, :, j:j + 1], op=ALU.add)
        self.E("dve", "tensor_scalar", ["lb"], ["oml"], out=self.oml[:], in0=self.lb[:], scalar1=-1.0, scalar2=1.0,
               op0=ALU.mult, op1=ALU.add)

    def cossin(self, s):
        PI = math.pi
        ang, ka = self.sv(self.M_T1, 512, F32)
        tmp, kt = self.sv(self.M_T2, 512, F32)
        ki_, kki = self.sv(self.M_T3, 512, F32)
        pi_t, kpi = self.sv(self.M_T3 + 4096, 512, F32)
        kint = pi_t.bitcast(I32)
        posf, kpf = self.sv(self.M_B1, 16, F32)
        posi = posf.bitcast(I32)
        self.DMA("sp", "cst", posi, self.pos_d[s], [], kpf)
        self.E("dve", "tensor_copy", kpf, kpf, out=posf, in_=posi)
        ang3 = ang.rearrange("p (i j) -> p i j", i=16)
        self.E("dve", "tensor_tensor", kpf + ["invf"], ka, out=ang3, in0=posf.unsqueeze(2).to_broadcast([128, 16, 32]),
               in1=self.invf[:].unsqueeze(1).to_broadcast([128, 16, 32]), op=ALU.mult)
        for which, shift in ((0, 0.0), (1, PI / 2)):
            self.E("dve", "tensor_scalar", ka, kt, out=tmp, in0=ang, scalar1=shift, scalar2=1.0 / (2 * PI), op0=ALU.add,
                   op1=ALU.mult)
            self.E("dve", "tensor_copy", kt, kpi, out=kint, in_=tmp)
            self.E("dve", "tensor_copy", kpi, kki, out=ki_, in_=kint)
            self.E("dve", "scalar_tensor_tensor", kki + ka, kt, out=tmp, in0=ki_, scalar=-2 * PI, in1=ang, op0=ALU.mult,
                   op1=ALU.add)
            if shift != 0.0:
                self.E("dve", "tensor_scalar", kt, kt, out=tmp, in0=tmp, scalar1=shift, scalar2=None, op0=ALU.add)
            self.E("dve", "tensor_scalar", kt, kki, out=ki_, in0=tmp, scalar1=PI, scalar2=-2 * PI, op0=ALU.is_gt, op1=ALU.mult)
            self.E("dve", "tensor_tensor", kt + kki, kt, out=tmp, in0=tmp, in1=ki_, op=ALU.add)
            self.E("dve", "tensor_scalar", kt, kki, out=ki_, in0=tmp, scalar1=-PI, scalar2=2 * PI, op0=ALU.is_lt, op1=ALU.mult)
            self.E("dve", "tensor_tensor", kt + kki, kt, out=tmp, in0=tmp, in1=ki_, op=ALU.add)
            self.E("dve", "tensor_scalar", kt, kt, out=tmp, in0=tmp, scalar1=PI, scalar2=-PI, op0=ALU.min, op1=ALU.max)
            self.E("act", "activation", kt, kki, out=ki_, in_=tmp, func=AF.Sin)
            self.DMA("sp", "cst", self.cs_s[1 - which], ki_, kki, [("cs",)])

    def mixer(self, l, s):
        P = self.P
        self.wcnt = 0
        rb, krb = self.sv(self.M_RB, 2048, F32)
        for tg in range(4):
            v, kv = self.sv(self.M_RB + tg * 2048, 512, F32)
            self.rstd(tg * 512, 512, v, kv, self.M_SM)
        for i in range(16):
            self.MM(krb + ["one_f"], self.pk([7]), [(self.ps[7][:, i:i + 1], rb[0:1, i * 128:(i + 1) * 128], self.one_f[0:1, 0:1], True, True)])
        self.E("dve", "tensor_copy", self.pk([7]), ["rcol"], out=self.rcol[:], in_=self.ps[7][:, 0:16])
        self.E("dve", "tensor_tensor", ["rcol"], ["r2col"], out=self.r2col[:], in0=self.rcol[:], in1=self.rcol[:], op=ALU.mult)
        self.DMA("sp", "cst", self.small[:], self.small_d[l], [], ["small"])
        self.DMA("sp", "cst", self.bc[:], self.bc_d[l], [], ["bc"])
        parts = self.cfg.get("mix", ("conv", "attn", "hgrn"))
        if "conv" in parts:
            for gi in range(4):
                self.conv_group(l, gi, rb, krb)
        if "attn" in parts:
            self.attn_setup(l)
            for a in range(4):
                self.attn_head(l, a, rb, krb)
        if "hgrn" in parts:
            self.hgrn_setup()
            for h in range(8):
                self.hgrn_head(l, h, rb, krb)
        ych = []
        if "hgrn" in parts:
            ych += list(range(0, 8))
        if "attn" in parts:
            ych += list(range(8, 12))
        if "conv" in parts:
            ych += list(range(12, 16))
        self.wout_stage(l, ych)

    def conv_group(self, l, gi, rb, krb):
        T1, k1 = self.sv(self.M_T1, 2050, F32)
        T2, k2 = self.sv(self.M_T2, 2048, F32)
        T3, k3 = self.sv(self.M_T3, 2048, F32)
        SQ, ksq = self.sv(self.M_B6, 2048, BF16)
        YB, kyb = self.sv(self.M_B5, 2048, BF16)
        sm = self.small
        c0 = gi * 5
        self.E("pool", "memset", [], k1, ap=T1[:, 0:1], constant=0.0)
        self.E("pool", "memset", [], k1, ap=T1[:, 2049:2050], constant=0.0)
        self.proj_fm(l, 56 + gi, 0)
        for tg in range(4):
            self.E("dve", "tensor_tensor", self.pk([tg]) + krb, k1, out=T1[:, 1 + tg * 512:1 + tg * 512 + 512],
                   in0=self.ps[tg][:], in1=rb[:, tg * 512:(tg + 1) * 512], op=ALU.mult)
        self.proj_fm(l, 60 + gi, 4)
        for tg in range(4):
            self.E("dve", "tensor_tensor", self.pk([4 + tg]) + krb, k2, out=T2[:, tg * 512:(tg + 1) * 512],
                   in0=self.ps[4 + tg][:], in1=rb[:, tg * 512:(tg + 1) * 512], op=ALU.mult)
        self.E("pool", "tensor_tensor", k1 + k2, k1, out=T1[:, 1:2049], in0=T1[:, 1:2049], in1=T2, op=ALU.mult)
        self.E("dve", "tensor_scalar", k1 + ["small"], k2, out=T2, in0=T1[:, 1:2049], scalar1=sm[:, c0 + 1:c0 + 2],
               scalar2=sm[:, c0 + 3:c0 + 4], op0=ALU.mult, op1=ALU.add)
        self.E("dve", "scalar_tensor_tensor", k1 + k2 + ["small"], k2, out=T2, in0=T1[:, 0:2048], scalar=sm[:, c0:c0 + 1],
               in1=T2, op0=ALU.mult, op1=ALU.add)
        self.E("dve", "scalar_tensor_tensor", k1 + k2 + ["small"], k2, out=T2, in0=T1[:, 2:2050], scalar=sm[:, c0 + 2:c0 + 3],
               in1=T2, op0=ALU.mult, op1=ALU.add)
        self.proj_fm(l, 52 + gi, 0)
        for tg in range(4):
            self.E("dve", "tensor_tensor", self.pk([tg]) + krb, k3, out=T3[:, tg * 512:(tg + 1) * 512],
                   in0=self.ps[tg][:], in1=rb[:, tg * 512:(tg + 1) * 512], op=ALU.mult)
        self.E("pool", "tensor_tensor", k2 + k3, k2, out=T2, in0=T2, in1=T3, op=ALU.mult)
        self.pnorm_store(T2, k2, T3, k3, SQ, ksq, YB, kyb, 4, sm[:, c0 + 4:c0 + 5], None, None, 12 + gi, 1.0)

    def pnorm_store(self, Tin, kin, Ttmp, ktmp, SQ, ksq, YB, kyb, b0, gain_col, mul_t, kmul, ychunk, in_is_psum_banks=None):
        self.E("act", "activation", kin, ksq, out=SQ, in_=Tin, func=AF.Square)
        for tg in range(4):
            self.MM(ksq + ["ones_bf"], self.pk([b0 + tg]),
                    [(self.ps[b0 + tg][:], self.ones_bf[:], SQ[:, tg * 512:(tg + 1) * 512], True, True)])
            self.E("act", "activation", self.pk([b0 + tg]) + ["eps_t"], ktmp, out=Ttmp[:, tg * 512:(tg + 1) * 512],
                   in_=self.ps[b0 + tg][:], func=AF.Sqrt, bias=self.eps_t[:], scale=1.0 / 128)
        self.E("dve", "reciprocal", ktmp, ktmp, out=Ttmp, in_=Ttmp)
        if mul_t is None:
            self.E("dve", "scalar_tensor_tensor", kin + ktmp + ["small"], kyb, out=YB, in0=Tin, scalar=gain_col, in1=Ttmp,
                   op0=ALU.mult, op1=ALU.mult)
        else:
            self.E("dve", "scalar_tensor_tensor", kin + ktmp + ["small"], ktmp, out=Ttmp, in0=Tin, scalar=gain_col, in1=Ttmp,
                   op0=ALU.mult, op1=ALU.mult)
            self.E("dve", "tensor_tensor", ktmp + kmul, kyb, out=YB, in0=Ttmp, in1=mul_t, op=ALU.mult)
        self.DMA("sp", "ysto", self.y_s[ychunk], YB, kyb, [("y", ychunk)])

    def wout_stage(self, l, ych):
        for tg in range(4):
            yt, ky = self.sv(self.M_T1, 16 * 512, BF16)
            yt3 = yt.rearrange("p (c t) -> p c t", c=16)
            for c in ych:
                self.DMA("sp", "yld", yt3[:, c, :], self.y_s[c][:, tg * 512:(tg + 1) * 512], [("y", c)], ky)
            for n in range(16):
                slot = self.wcnt % 2
                self.wcnt += 1
                wt, kw = self.sv(self.M_WS + slot * 4096, 2048, BF16)
                self.DMA("sp", f"mw{slot}", wt, self.wout_s[l][n * 128:(n + 1) * 128, :], [("W", "d", l)], kw)
                wt3 = wt.rearrange("p (k c) -> p k c", k=16)
                b = n % 2
                self.MM(kw + ky, self.pk([b]), [(self.ps[b][:], wt3[:, c, :], yt3[:, c, :], j == 0, j == len(ych) - 1)
                                                for j, c in enumerate(ych)])
                xv = self.X[:, n, tg * 512:(tg + 1) * 512]
                self.E("dve", "tensor_tensor", self.pk([b]) + self.xk([n], [tg]), self.xk([n], [tg]), out=xv, in0=self.ps[b][:],
                       in1=xv, op=ALU.add)

    def attn_setup(self, l):
        bc3 = self.bc[:].rearrange("p (g d) -> p g d", g=6)
        lt, kl = self.sv(self.M_SM, 128, F32)
        lt3 = lt.rearrange("p (g d) -> p g d", g=2)
        ls, kls = self.sv(self.M_SM + 512, 2, F32)
        self.E("dve", "tensor_tensor", ["bc"], kl, out=lt3, in0=bc3[:, 2:6:2, :], in1=bc3[:, 3:6:2, :], op=ALU.mult)
        self.E("dve", "tensor_reduce", kl, kls, out=ls, in_=lt3, axis=AX.X, op=ALU.add)
        self.E("act", "activation", kls, kls, out=ls, in_=ls, func=AF.Exp)
        lam_init = 0.8 - 0.6 * math.exp(-0.3 * l)
        self.E("dve", "tensor_tensor", kls, ["nlam"], out=self.nlam[:], in0=ls[:, 1:2], in1=ls[:, 0:1], op=ALU.subtract)
        self.E("dve", "tensor_scalar", ["nlam"], ["nlam"], out=self.nlam[:], in0=self.nlam[:], scalar1=-lam_init, scalar2=None,
               op0=ALU.add)
        self.lam_init = lam_init

    def attn_head(self, l, a, rb, krb):
        sv = self.sv
        COS, kcos = sv(self.M_T2, 512, F32)
        SIN, ksin = sv(self.M_T2 + 2048, 512, F32)
        self.DMA("sp", "cst", COS, self.cs_s[0], [("cs",)], kcos)
        self.DMA("sp", "cst", SIN, self.cs_s[1], [("cs",)], ksin)
        cos4 = COS.rearrange("p (i d) -> p i d", i=16).unsqueeze(2).to_broadcast([128, 16, 2, 32])
        sin4 = SIN.rearrange("p (i d) -> p i d", i=16).unsqueeze(2).to_broadcast([128, 16, 2, 32])
        QT, kqt = sv(self.M_B1, 2048, BF16)
        KT, kkt = sv(self.M_B2, 2048, BF16)
        VA, kva = sv(self.M_B3, 16 * 130, BF16)
        VA3 = VA.rearrange("p (i c) -> p i c", i=16)
        QN, kqn = sv(self.M_B4, 2048, BF16)
        QN4 = QN.rearrange("p (i m d) -> p i m d", i=16, m=2)
        QN3 = QN.rearrange("p (i c) -> p i c", i=16)
        T1, k1 = sv(self.M_T1, 2048, F32)
        T1v = T1.rearrange("p (i m d) -> p i m d", i=16, m=2)
        RT, krt = sv(self.M_B5, 2048, F32)
        RT5 = RT.rearrange("p (h i m d) -> p h i m d", h=2, i=16, m=2)
        SS, kss = sv(self.M_SM + 1024, 32, F32)
        SS3 = SS.rearrange("p (i m) -> p i m", i=16)
        bc3 = self.bc[:].rearrange("p (g d) -> p g d", g=6)
        for which, (blk, dst, kdst) in enumerate(((40 + a, QT, kqt), (44 + a, KT, kkt))):
            self.proj_tm(l, blk, 0)
            for b in range(4):
                self.E("act", "activation", self.pk([b]), k1, out=T1[:, b * 512:(b + 1) * 512], in_=self.ps[b][:], func=AF.Square)
            self.E("dve", "tensor_reduce", k1, kss, out=SS, in_=T1.rearrange("p (g d) -> p g d", d=64), axis=AX.X, op=ALU.add)
            self.E("dve", "tensor_tensor", kss + ["r2col"], kss, out=SS3, in0=SS3,
                   in1=self.r2col[:].unsqueeze(2).to_broadcast([128, 16, 2]), op=ALU.mult)
            self.E("act", "activation", kss + ["eps_t"], kss, out=SS, in_=SS, func=AF.Sqrt, bias=self.eps_t[:], scale=1.0 / 64)
            self.E("dve", "reciprocal", kss, kss, out=SS, in_=SS)
            self.E("dve", "tensor_tensor", kss + ["rcol"], kss, out=SS3, in0=SS3,
                   in1=self.rcol[:].unsqueeze(2).to_broadcast([128, 16, 2]), op=ALU.mult)
            if which == 0:
                self.E("dve", "tensor_scalar", kss, kss, out=SS, in0=SS, scalar1=0.125, scalar2=None, op0=ALU.mult)
            for b in range(4):
                self.E("dve", "tensor_tensor", self.pk([b]) + kss, k1, out=T1v[:, 4 * b:4 * b + 4, :, :],
                       in0=self.ps[b][:].rearrange("p (i m d) -> p i m d", i=4, m=2),
                       in1=SS3[:, 4 * b:4 * b + 4, :].unsqueeze(3).to_broadcast([128, 4, 2, 64]), op=ALU.mult)
            self.E("dve", "tensor_tensor", k1 + ["bc"], k1, out=T1.rearrange("p (g d) -> p g d", d=64),
                   in0=T1.rearrange("p (g d) -> p g d", d=64),
                   in1=bc3[:, which, :].unsqueeze(1).to_broadcast([128, 32, 64]), op=ALU.mult)
            t1 = T1v[:, :, :, 0:32]
            t2 = T1v[:, :, :, 32:64]
            self.E("dve", "tensor_tensor", k1 + kcos, krt, out=RT5[:, 0], in0=t1, in1=cos4, op=ALU.mult)
            self.E("pool", "tensor_tensor", k1 + ksin, krt, out=RT5[:, 1], in0=t2, in1=sin4, op=ALU.mult)
            self.E("dve", "tensor_tensor", krt, kqn, out=QN4[:, :, :, 0:32], in0=RT5[:, 0], in1=RT5[:, 1], op=ALU.subtract)
            self.E("dve", "tensor_tensor", k1 + kcos, krt, out=RT5[:, 0], in0=t2, in1=cos4, op=ALU.mult)
            self.E("pool", "tensor_tensor", k1 + ksin, krt, out=RT5[:, 1], in0=t1, in1=sin4, op=ALU.mult)
            self.E("dve", "tensor_tensor", krt, kqn, out=QN4[:, :, :, 32:64], in0=RT5[:, 0], in1=RT5[:, 1], op=ALU.add)
            for hb in range(2):
                pT = self.ps[4 + hb][:].bitcast(BF16)
                self.TR(kqn, self.pk([4 + hb]), [(pT[:, j * 128:(j + 1) * 128], QN3[:, hb * 8 + j, :]) for j in range(8)])
                self.E("act", "activation", self.pk([4 + hb]), kdst, out=dst[:, hb * 1024:(hb + 1) * 1024], in_=pT, func=AF.Copy)
        self.proj_tm(l, 48 + a, 0)
        for b in range(4):
            self.E("dve", "tensor_tensor", self.pk([b]) + ["rcol"], kva, out=VA3[:, 4 * b:4 * b + 4, 0:128],
                   in0=self.ps[b][:].rearrange("p (i c) -> p i c", i=4),
                   in1=self.rcol[:, 4 * b:4 * b + 4].unsqueeze(2).to_broadcast([128, 4, 128]), op=ALU.mult)
        self.E("pool", "memset", [], kva, ap=VA3[:, :, 128:129], constant=1.0)
        PT, kpt = sv(self.M_T2, 16 * 512, BF16)
        PT3 = PT.rearrange("p (k q) -> p k q", k=16)
        ON, kon = sv(self.M_MSK, 1024, F32)
        ON4 = ON.rearrange("p (m q d) -> p m q d", m=2, q=4)
        YB, kyb = sv(self.M_B5, 2048, BF16)
        OB, kob = sv(self.M_B4, 512, BF16)
        OB3 = OB.rearrange("p (q d) -> p q d", q=4)
        RS, krs = sv(self.M_SM + 1280, 1, F32)
        S4, ks4 = sv(self.M_SM + 1344, 4, F32)
        for qg in range(4):
            for m in range(2):
                for kt in range(16):
                    b = 2 + (kt % 2)
                    self.MM(kkt + kqt, self.pk([b]), [(self.ps[b][:], KT[m * 64:(m + 1) * 64, kt * 128:(kt + 1) * 128],
                                                      QT[m * 64:(m + 1) * 64, qg * 512:(qg + 1) * 512], True, True)])
                    self.E("act", "activation", self.pk([b]), kpt, out=PT3[:, kt, :], in_=self.ps[b][:], func=AF.Exp)
                for qt in range(4):
                    bo = 6 + (qt % 2)
                    self.MM(kpt + kva, self.pk([bo]), [(self.ps[bo][:, 0:129], PT3[:, kt, qt * 128:(qt + 1) * 128], VA3[:, kt, 0:129],
                                                       kt == 0, kt == 15) for kt in range(16)])
                    self.E("dve", "reciprocal", self.pk([bo]), krs, out=RS, in_=self.ps[bo][:, 128:129])
                    self.E("dve", "tensor_scalar", self.pk([bo]) + krs, kon, out=ON4[:, m, qt, :], in0=self.ps[bo][:, 0:128],
                           scalar1=RS, scalar2=None, op0=ALU.mult)
            self.E("dve", "scalar_tensor_tensor", kon + ["nlam"], kon, out=ON4[:, 0], in0=ON4[:, 1], scalar=self.nlam[:],
                   in1=ON4[:, 0], op0=ALU.mult, op1=ALU.add)
            self.E("dve", "tensor_tensor", kon, kon, out=ON4[:, 1], in0=ON4[:, 0], in1=ON4[:, 0], op=ALU.mult)
            self.E("dve", "tensor_reduce", kon, ks4, out=S4, in_=ON4[:, 1], axis=AX.X, op=ALU.add)
            self.E("act", "activation", ks4 + ["eps_t"], ks4, out=S4, in_=S4, func=AF.Sqrt, bias=self.eps_t[:], scale=1.0 / 128)
            self.E("dve", "reciprocal", ks4, ks4, out=S4, in_=S4)
            self.E("dve", "tensor_scalar", ks4, ks4, out=S4, in0=S4, scalar1=1.0 - self.lam_init, scalar2=None, op0=ALU.mult)
            self.E("dve", "tensor_tensor", kon + ks4, kob, out=OB3, in0=ON4[:, 0], in1=S4.unsqueeze(2).to_broadcast([128, 4, 128]),
                   op=ALU.mult)
            pT = self.ps[4][:].bitcast(BF16)
            self.TR(kob, self.pk([4]), [(pT[:, j * 128:(j + 1) * 128], OB3[:, j, :]) for j in range(4)])
            self.E("act", "activation", self.pk([4]) + ["small"], kyb, out=YB[:, qg * 512:(qg + 1) * 512], in_=pT[:, 0:512],
                   func=AF.Copy, scale=self.small[:, 28 + a:29 + a])
        self.DMA("sp", "ysto", self.y_s[8 + a], YB, kyb, [("y", 8 + a)])

    def hgrn_setup(self):
        MSK, kmsk = self.sv(self.M_MSK, 2048, BF16)
        self.E("pool", "memset", [], kmsk, ap=MSK, constant=1.0)
        self.E("pool", "memset", [], kmsk, ap=MSK.rearrange("p (c k) -> p c k", k=64)[:, :, 0:1], constant=0.0)

    def hgrn_head(self, l, h, rb, krb):
        sv = self.sv
        MSK, kmsk = sv(self.M_MSK, 2048, BF16)
        T1, k1 = sv(self.M_T1, 2048, F32)
        T2, k2 = sv(self.M_T2, 2048, F32)
        T3, k3 = sv(self.M_T3, 2048, F32)
        QS, kqs = sv(self.M_B1, 2048, BF16)
        GS, kgs = sv(self.M_B2, 2048, BF16)
        VT, kvt = sv(self.M_B3, 2048, BF16)
        VT3 = VT.rearrange("p (i c) -> p i c", i=16)
        QP, kqp = sv(self.M_B4, 2048, BF16)
        KP, kkp = sv(self.M_B5, 2048, BF16)
        KPT, kkpt = sv(self.M_B6, 2048, BF16)
        KPT3 = KPT.rearrange("p (i c) -> p i c", i=16)
        sm0 = self.M_SM
        Sst = [sv(sm0 + j * 512, 128, F32) for j in range(2)]
        Sp = [sv(sm0 + 1024 + j * 256, 128, BF16) for j in range(4)]
        MS = [sv(sm0 + 2048 + j * 256, 128, BF16) for j in range(4)]
        KVb = [sv(sm0 + 3072 + j * 512, 128, F32) for j in range(4)]
        MC, kmc = sv(sm0 + 5120, 32, F32)
        BC, kbc = sv(sm0 + 5248, 32, F32)
        EM, kem = sv(sm0 + 5376, 32, F32)
        EB, keb = sv(sm0 + 5504, 32, F32)
        EBM, kebm = sv(sm0 + 5632, 32, F32)
        T3v = T3.rearrange("p (c k) -> p c k", k=64)
        self.proj_fm(l, h, 0)
        for tg in range(4):
            self.E("dve", "tensor_tensor", self.pk([tg]) + krb, kqs, out=QS[:, tg * 512:(tg + 1) * 512], in0=self.ps[tg][:],
                   in1=rb[:, tg * 512:(tg + 1) * 512], op=ALU.mult)
        self.proj_fm(l, 32 + h, 0)
        for tg in range(4):
            self.E("dve", "tensor_tensor", self.pk([tg]) + krb, k1, out=T1[:, tg * 512:(tg + 1) * 512], in0=self.ps[tg][:],
                   in1=rb[:, tg * 512:(tg + 1) * 512], op=ALU.mult)
        self.E("act", "activation", k1, kgs, out=GS, in_=T1, func=AF.Silu)
        self.proj_tm(l, 8 + h, 0)
        for b in range(4):
            self.E("dve", "tensor_tensor", self.pk([b]) + ["rcol"], kvt, out=VT3[:, 4 * b:4 * b + 4, :],
                   in0=self.ps[b][:].rearrange("p (i c) -> p i c", i=4),
                   in1=self.rcol[:, 4 * b:4 * b + 4].unsqueeze(2).to_broadcast([128, 4, 128]), op=ALU.mult)
        for b in range(4, 8):
            self.MM(kqs + ["zeros_bf"], self.pk([b]), [(self.ps[b][:], self.zeros_bf[:], QS[:, 0:512], True, True)])
        lbi = lambda d: self.lb[:, d * 8 + h, l:l + 1]
        omli = lambda d: self.oml[:, d * 8 + h, l:l + 1]
        for d in range(2):
            self.proj_fm(l, 16 + 8 * d + h, 0)
            for tg in range(4):
                self.E("dve", "tensor_tensor", self.pk([tg]) + krb, k1, out=T1[:, tg * 512:(tg + 1) * 512], in0=self.ps[tg][:],
                       in1=rb[:, tg * 512:(tg + 1) * 512], op=ALU.mult)
            self.E("act", "activation", k1, k1, out=T1, in_=T1, func=AF.Sigmoid)
            self.E("dve", "tensor_scalar", k1 + ["lb", "oml"], k1, out=T1, in0=T1, scalar1=omli(d), scalar2=lbi(d), op0=ALU.mult,
                   op1=ALU.add)
            self.E("act", "activation", k1, k2, out=T2, in_=T1, func=AF.Ln)
            self.E("dve", "tensor_scalar", k1, k1, out=T1, in0=T1, scalar1=-1.0, scalar2=1.0, op0=ALU.mult, op1=ALU.add)
            self.E("dve", "tensor_tensor_scan", k2 + kmsk, k3, out=T3, data0=MSK, data1=T2, initial=0.0, op0=ALU.mult,
                   op1=ALU.add)
            if d == 0:
                self.E("dve", "tensor_copy", k3, kmc, out=MC, in_=T3v[:, :, 31])
                self.E("dve", "tensor_copy", k3, kbc, out=BC, in_=T3v[:, :, 63])
            else:
                self.E("dve", "tensor_copy", k3, kbc, out=BC, in_=T3v[:, :, 63])
                self.E("dve", "tensor_tensor", k2 + k3, k3, out=T3, in0=T2, in1=T3, op=ALU.subtract)
                self.E("dve", "tensor_tensor", k3 + kbc, k3, out=T3v, in0=T3v, in1=BC.unsqueeze(2).to_broadcast([128, 32, 64]),
                       op=ALU.add)
                self.E("dve", "tensor_copy", k3, kmc, out=MC, in_=T3v[:, :, 32])
            self.E("dve", "tensor_tensor", k3 + kmc, k3, out=T3v, in0=T3v, in1=MC.unsqueeze(2).to_broadcast([128, 32, 64]),
                   op=ALU.subtract)
            self.E("act", "activation", k3, k2, out=T2, in_=T3, func=AF.Exp)
            self.E("dve", "tensor_tensor", k2 + kqs, kqp, out=QP, in0=T2, in1=QS, op=ALU.mult)
            self.E("act", "activation", k3, k2, out=T2, in_=T3, func=AF.Exp, scale=-1.0)
            self.E("dve", "tensor_tensor", k2 + k1, kkp, out=KP, in0=T2, in1=T1, op=ALU.mult)
            self.E("act", "activation", kmc, kem, out=EM, in_=MC, func=AF.Exp)
            self.E("act", "activation", kbc, keb, out=EB, in_=BC, func=AF.Exp)
            self.E("dve", "tensor_tensor", kbc + kmc, kebm, out=EBM, in0=BC, in1=MC, op=ALU.subtract)
            self.E("act", "activation", kebm, kebm, out=EBM, in_=EBM, func=AF.Exp)
            for hb in range(2):
                pT = self.ps[hb][:].bitcast(BF16)
                self.TR(kkp, self.pk([hb]), [(pT[:, j * 128:(j + 1) * 128], KP[:, (hb * 8 + j) * 128:(hb * 8 + j + 1) * 128])
                                             for j in range(8)])
                self.E("act", "activation", self.pk([hb]), kkpt, out=KPT[:, hb * 1024:(hb + 1) * 1024], in_=pT, func=AF.Copy)
            mask = self.mask_f if d == 0 else self.mask_b
            mkey = "mask_f" if d == 0 else "mask_b"
            def c1(i):
                b, q = i % 4, 0
                self.MM(kkp + kqp, self.pq(b, q), [(self.ps[b][:, q * 128:(q + 1) * 128], KP[:, i * 128:(i + 1) * 128],
                                                   QP[:, i * 128:(i + 1) * 128], True, True)])
                ms, kms = MS[i % 4]
                self.E("dve", "scalar_tensor_tensor", self.pq(b, q) + [mkey], kms, out=ms, in0=self.ps[b][:, q * 128:(q + 1) * 128],
                       scalar=3.0e38, in1=mask[:], op0=ALU.min, op1=ALU.mult)

            def c2(i):
                ms, kms = MS[i % 4]
                bo = 4 + i // 4
                self.MM(kms + kvt, self.pq(bo, i % 4), [(self.ps[bo][:, (i % 4) * 128:(i % 4 + 1) * 128], VT3[:, i, :], ms, False, False)])
            for i in range(16 + 2):
                if i < 16:
                    c1(i)
                if i >= 2:
                    c2(i - 2)
            order = list(range(32)) if d == 0 else list(range(31, -1, -1))
            KVA, kkva = sv(self.M_T2, 32 * 128, F32)
            KVA3 = KVA.rearrange("p (c v) -> p c v", c=32)
            for sa in range(31):
                c = order[sa]
                i, hh = c // 2, c % 2
                bnk = sa % 4
                self.MM(kkpt + kvt, self.pk([bnk]), [(self.ps[bnk][:, 0:128], KPT3[hh * 64:(hh + 1) * 64, i, :],
                                                     VT3[hh * 64:(hh + 1) * 64, i, :], True, True)])
                self.E("dve", "tensor_scalar", self.pk([bnk]) + kebm, kkva, out=KVA3[:, sa, :], in0=self.ps[bnk][:, 0:128],
                       scalar1=EBM[:, c:c + 1], scalar2=None, op0=ALU.mult)
            prev = KVA3[:, 0, :]
            kprev = kkva
            for sc in range(31):
                c = order[sc]
                if sc > 0:
                    stn, kstn = Sst[sc % 2]
                    self.E("dve", "scalar_tensor_tensor", kprev + kkva + keb, kstn, out=stn, in0=prev, scalar=EB[:, c:c + 1],
                           in1=KVA3[:, sc, :], op0=ALU.mult, op1=ALU.add)
                    prev, kprev = stn, kstn
                cn = order[sc + 1]
                spn, kspn = Sp[(sc + 1) % 4]
                self.E("act", "activation", kprev + kem, kspn, out=spn, in_=prev, func=AF.Copy, scale=EM[:, cn:cn + 1])
                bo = 4 + cn // 8
                self.MM(kspn + kqp, self.pk([bo]), [(self.ps[bo][:, (cn % 8) * 64:(cn % 8 + 1) * 64], spn,
                                                    QP[:, cn * 64:(cn + 1) * 64], False, False)])
        for tg in range(4):
            self.E("act", "activation", self.pk([4 + tg]), k1, out=T1[:, tg * 512:(tg + 1) * 512], in_=self.ps[4 + tg][:], func=AF.Copy)
        SQ, ksq = sv(self.M_B6, 2048, BF16)
        YB, kyb = sv(self.M_B5, 2048, BF16)
        self.pnorm_store(T1, k1, T2, k2, SQ, ksq, YB, kyb, 0, self.small[:, 20 + h:21 + h], GS, kgs, h)


def build(cfg):
    nc = bass.Bass("TRN2", target_bir_lowering=False)
    with ExitStack() as es:
        k = K(nc, es, cfg)
        k.setup()
        layers = cfg.get("layers", list(range(DEPTH)))
        k.mixer_setup()
        k.prep(layers)
        for s in range(cfg.get("nseq", 2)):
            k.load_x(s)
            if "mix" in cfg["stages"]:
                k.cossin(s)
            for l in layers:
                if "ffn1" in cfg["stages"]:
                    k.ffn(l, 0)
                if "mix" in cfg["stages"]:
                    k.mixer(l, s)
                if "ffn2" in cfg["stages"]:
                    k.ffn(l, 1)
            k.store_x(s)
        k.P.wait_all("sp")
        k.P.emit()
    return nc


def host_layouts(inp):
    L = DEPTH
    f = lambda a: np.ascontiguousarray(np.asarray(a, dtype=np.float32))
    out = {}

    def kxn(w, ncol_blocks):
        w = np.asarray(w, dtype=np.float32).reshape(L, 16, 128, ncol_blocks, 128)
        return w.transpose(0, 3, 2, 1, 4)

    wgu = np.empty((L, 2, NFC, 128, 2, 16, 128), np.float32)
    for fi, (g, u) in enumerate((("ffn1_w_gate", "ffn1_w_up"), ("ffn2_w_gate", "ffn2_w_up"))):
        wgu[:, fi, :, :, 0] = kxn(inp[g], NFC)
        wgu[:, fi, :, :, 1] = kxn(inp[u], NFC)
    out["wgu"] = wgu.reshape(L, 2, NFC, 128, 4096)
    wd = np.empty((L, 2, 16, 128, NFC, 128), np.float32)
    for fi, dname in enumerate(("ffn1_w_down", "ffn2_w_down")):
        w = np.asarray(inp[dname], dtype=np.float32).reshape(L, NFC, 128, 16, 128)
        wd[:, fi] = w.transpose(0, 3, 2, 1, 4)
    out["wd"] = wd.reshape(L, 2, -1, 1376)
    win = kxn(inp["w_in"], NBLK)
    out["win"] = np.ascontiguousarray(win).reshape(L, NBLK // 2, 2, 128, 16, 128).transpose(0, 1, 3, 2, 4, 5).reshape(
        L, NBLK // 2, 128, 4096)
    out["wout"] = np.ascontiguousarray(kxn(inp["w_out"], 16)).reshape(L, -1, 2048)
    gains = np.stack([np.asarray(inp[n], dtype=np.float32).reshape(L, 16, 128).transpose(0, 2, 1)
                      for n in ("ffn1_norm", "mix_norm", "ffn2_norm")], axis=1)
    out["gains"] = f(gains)
    import ml_dtypes
    bf = ml_dtypes.bfloat16
    out["ident"] = np.eye(128, dtype=np.float32).astype(bf)
    ii = np.arange(128)
    same = (ii[:, None] // 64) == (ii[None, :] // 64)
    mf = (same & (ii[:, None] <= ii[None, :])).astype(np.float32)
    mb = (same & (ii[:, None] >= ii[None, :])).astype(np.float32)
    out["masks"] = np.stack([mf, mb]).astype(bf)
    inv_freq = (10000.0 ** (-np.arange(0, 64, 2, dtype=np.float32) / 64)).astype(np.float32)
    out["invf"] = np.broadcast_to(inv_freq[None, :], (128, 32)).copy()
    lg = np.asarray(inp["hgrn_lb_logits"], dtype=np.float32).reshape(2, L, 8, 128)
    out["lbl"] = lg.transpose(3, 0, 2, 1).reshape(128, 16, L)
    small = np.zeros((L, 128, 32), np.float32)
    cw = np.asarray(inp["conv_w"], dtype=np.float32).reshape(L, 3, 4, 128)
    cb = np.asarray(inp["conv_b"], dtype=np.float32).reshape(L, 4, 128)
    cn = np.asarray(inp["conv_norm"], dtype=np.float32).reshape(L, 4, 128)
    for gi in range(4):
        for j in range(3):
            small[:, :, gi * 5 + j] = cw[:, j, gi, :]
        small[:, :, gi * 5 + 3] = cb[:, gi, :]
        small[:, :, gi * 5 + 4] = cn[:, gi, :]
    small[:, :, 20:28] = np.asarray(inp["hgrn_norm"], dtype=np.float32).reshape(L, 8, 128).transpose(0, 2, 1)
    small[:, :, 28:32] = np.asarray(inp["da_out_norm"], dtype=np.float32).reshape(L, 4, 128).transpose(0, 2, 1)
    out["small"] = small
    bcv = np.concatenate([np.asarray(inp[n], dtype=np.float32) for n in
                          ("da_q_norm", "da_k_norm", "da_lambda_q1", "da_lambda_k1", "da_lambda_q2", "da_lambda_k2")], axis=1)
    out["bc"] = np.broadcast_to(bcv[:, None, :], (L, 128, 384)).copy()
    for k_ in out:
        if out[k_].dtype == np.float32:
            out[k_] = f(out[k_])
    return out


CFG_FULL = {"stages": ("ffn1", "mix", "ffn2"), "nseq": 2}


def kernel(**inputs):
    x = np.asarray(inputs["x"], dtype=np.float32)
    shared = host_layouts(inputs)
    nc = build(CFG_FULL)
    in_maps = []
    for core in range(NCORES):
        xs = x[2 * core:2 * core + 2]
        xT = np.ascontiguousarray(xs.transpose(0, 2, 1)).reshape(2, 16, 128, S)
        pos = np.asarray(inputs["positions"])[2 * core:2 * core + 2].astype(np.int32)
        m = {"xT": xT, "pos": np.ascontiguousarray(pos.reshape(2, 16, 128).transpose(0, 2, 1))}
        m.update(shared)
        in_maps.append(m)
    res = run_bass_kernel_spmd(nc, in_maps, core_ids=list(range(NCORES)))
    out = np.empty((16, S, D), np.float32)
    for core in range(NCORES):
        yT = res.results[core]["yT"].reshape(2, D, S)
        out[2 * core:2 * core + 2] = yT.transpose(0, 2, 1)
    return out
```

```python
import math
from contextlib import ExitStack

import numpy as np
import concourse.bass as bass
import concourse.mybir as mybir
from concourse.bass_utils import run_bass_kernel_spmd

F32 = mybir.dt.float32
BF16 = mybir.dt.bfloat16
I32 = mybir.dt.int32
ALU = mybir.AluOpType
AF = mybir.ActivationFunctionType
AX = mybir.AxisListType

D = 2048
S = 2048
DFF = 5504
NFC = 43
DEPTH = 4
NCORES = 8
EPS = 1e-6
NBLK = 64

COMPUTE = ("pe", "act", "dve", "pool")
ALLENG = ("pe", "act", "dve", "pool", "sp")


class Prog:
    def __init__(self, nc, es):
        self.nc = nc
        self.es = es
        self.streams = {e: [] for e in ALLENG}
        self.sem = {e: es.enter_context(nc.semaphore("s_" + e)) for e in COMPUTE}
        self.cnt = {e: 0 for e in COMPUTE}
        self.dsem = {}
        self.dcnt = {}
        self.waited = {e: {} for e in ALLENG}
        self.lastw = {}
        self.readers = {}

    def _deps(self, eng, reads, writes, is_dma=False):
        toks = []
        for r in reads:
            for t in self.lastw.get(r, {}).values():
                toks.append((t, True))
        for w in writes:
            for t in self.lastw.get(w, {}).values():
                toks.append((t, False))
            rd = self.readers.get(w)
            if rd:
                for t in rd.values():
                    toks.append((t, False))
        need = {}
        for (kind, key, val), raw in toks:
            if not is_dma and kind == "e" and key == eng:
                if not raw or eng == "pe":
                    continue
            k = (kind, key)
            if self.waited[eng].get(k, 0) >= val:
                continue
            if need.get(k, 0) < val:
                need[k] = val
        for k, v in need.items():
            self.waited[eng][k] = v
        return [(k[0], k[1], v) for k, v in need.items()]

    def _record(self, tok, reads, writes):
        for w in writes:
            self.lastw.setdefault(w, {})[(tok[0], tok[1])] = tok
            self.readers[w] = {}
        for r in reads:
            self.readers.setdefault(r, {})[(tok[0], tok[1])] = tok

    def op(self, eng, fn, reads=(), writes=()):
        waits = self._deps(eng, reads, writes)
        self.cnt[eng] += 1
        tok = ("e", eng, self.cnt[eng])
        self._record(tok, reads, writes)
        self.streams[eng].append(("op", waits, fn, None))
        return tok

    def dma(self, eng, semname, fn, reads=(), writes=(), n=1):
        if semname not in self.dsem:
            self.dsem[semname] = self.es.enter_context(self.nc.semaphore("d_" + semname))
            self.dcnt[semname] = 0
        waits = self._deps(eng, reads, writes, is_dma=True)
        self.dcnt[semname] += 16 * n
        tok = ("d", semname, self.dcnt[semname])
        self._record(tok, reads, writes)
        self.streams[eng].append(("dma", waits, fn, semname))
        return tok

    def wait_all(self, eng):
        waits = []
        for e in COMPUTE:
            if self.cnt[e] > 0 and e != eng:
                waits.append(("e", e, self.cnt[e]))
        for s, v in self.dcnt.items():
            if v > 0:
                waits.append(("d", s, v))
        self.streams[eng].append(("wait", waits, None, None))

    def emit(self):
        prog = self

        def semh(kind, key):
            return prog.sem[key] if kind == "e" else prog.dsem[key]

        def run(engname, eobj):
            for kind, waits, fn, semname in prog.streams[engname]:
                for (k, key, v) in waits:
                    eobj.wait_ge(semh(k, key), v)
                if kind == "op":
                    fn(eobj).then_inc(prog.sem[engname], 1)
                elif kind == "dma":
                    inss = fn(eobj)
                    if not isinstance(inss, (list, tuple)):
                        inss = [inss]
                    for i in inss:
                        i.then_inc(prog.dsem[semname], 16)

        with self.nc.Block() as block:
            @block.tensor
            def _(e):
                run("pe", e)

            @block.scalar
            def _(e):
                run("act", e)

            @block.vector
            def _(e):
                run("dve", e)

            @block.gpsimd
            def _(e):
                run("pool", e)

            @block.sync
            def _(e):
                run("sp", e)


SCR_BYTES = 75 * 1024
PAGE = 1024
SMALL_START = 68 * 1024


class K:
    def __init__(self, nc, es, cfg):
        self.nc, self.es, self.cfg = nc, es, cfg
        L = DEPTH
        dt = nc.dram_tensor
        ein = dict(kind="ExternalInput")
        self.xT = dt("xT", [2, 16, 128, S], F32, **ein).ap()
        self.wgu = dt("wgu", [L, 2, NFC, 128, 4096], F32, **ein).ap()
        self.wd = dt("wd", [L, 2, 16 * 128 * 4, 1376], F32, **ein).ap()
        self.win = dt("win", [L, NBLK // 2, 128, 4096], F32, **ein).ap()
        self.wout = dt("wout", [L, 16 * 128 * 16 * 128 // 2048, 2048], F32, **ein).ap()
        self.gains = dt("gains", [L, 3, 128, 16], F32, **ein).ap()
        self.yT = dt("yT", [2, 16, 128, S], F32, kind="ExternalOutput").ap()
        self.ident_d = dt("ident", [128, 128], BF16, **ein).ap()
        self.mask_d = dt("masks", [2, 128, 128], BF16, **ein).ap()
        self.invf_d = dt("invf", [128, 32], F32, **ein).ap()
        self.lbl_d = dt("lbl", [128, 16, 4], F32, **ein).ap()
        self.pos_d = dt("pos", [2, 128, 16], I32, **ein).ap()
        self.small_d = dt("small", [L, 128, 32], F32, **ein).ap()
        self.bc_d = dt("bc", [L, 128, 384], F32, **ein).ap()
        self.cs_s = dt("cs_s", [2, 128, 512], F32).ap()
        self.y_s = [dt(f"y_s{c}", [128, S], BF16).ap() for c in range(16)]
        self.wgu_s = [[dt(f"wgu_s{l}_{f}", [NFC, 128, 4096], BF16).ap() for f in range(2)] for l in range(L)]
        self.wd_s = [[dt(f"wd_s{l}_{f}", [16 * 128 * 4, 1376], BF16).ap() for f in range(2)] for l in range(L)]
        self.win_s = [dt(f"win_s{l}", [NBLK // 2, 128, 4096], BF16).ap() for l in range(L)]
        self.wout_s = [dt(f"wout_s{l}", [16 * 128 * 16 * 128 // 2048, 2048], BF16).ap() for l in range(L)]

        sb = lambda name, shape, d: es.enter_context(nc.sbuf_tensor(name, shape, d))
        self.X = sb("X", [128, 16, S], F32)
        self.SCR = sb("SCR", [128, SCR_BYTES // 2], BF16)
        self.ones_bf = sb("ones_bf", [128, 128], BF16)
        self.gtile = sb("gtile", [128, 3, 16], F32)
        self.eps_t = sb("eps_t", [128, 1], F32)
        self.ident = sb("ident_sb", [128, 128], BF16)
        self.zeros_bf = sb("zeros_bf", [128, 128], BF16)
        self.one_f = sb("one_f", [128, 1], F32)
        self.mask_f = sb("mask_f", [128, 128], BF16)
        self.mask_b = sb("mask_b", [128, 128], BF16)
        self.invf = sb("invf_sb", [128, 32], F32)
        self.lbl = sb("lbl_sb", [128, 16, 4], F32)
        self.lbs = sb("lbs", [128, 16], F32)
        self.lb = sb("lb", [128, 16, 4], F32)
        self.oml = sb("oml", [128, 16, 4], F32)
        self.rcol = sb("rcol", [128, 16], F32)
        self.r2col = sb("r2col", [128, 16], F32)
        self.small = sb("small_sb", [128, 32], F32)
        self.bc = sb("bc_sb", [128, 384], F32)
        self.nlam = sb("nlam", [128, 1], F32)
        self.wcnt = 0
        self.ps = [es.enter_context(nc.psum_tensor(f"ps{i}", [128, 512], F32)) for i in range(8)]
        self.P = Prog(nc, es)
        self.Xhi = self.X[:].bitcast(BF16)

    def sv(self, off, nelem, dtype):
        nbytes = nelem * (4 if dtype == F32 else 2)
        assert off % 4 == 0 and off + nbytes <= SCR_BYTES, (off, nbytes)
        v = self.SCR[:, off // 2:(off + nbytes) // 2]
        if dtype == F32:
            v = v.bitcast(F32)
        keys = []
        b, end = off, off + nbytes
        while b < end:
            if b < SMALL_START:
                keys.append(("S", b // PAGE))
                b = (b // PAGE + 1) * PAGE
            else:
                keys.append(("s", b // 256))
                b = (b // 256 + 1) * 256
        return v, keys

    def xhi(self, c, t0, t1):
        return self.Xhi[:, c, 2 * t0 + 1:2 * t1:2]

    @staticmethod
    def xk(cs, tgs):
        return [("X", c, tg) for c in cs for tg in tgs]

    def setup(self):
        P = self.P
        P.op("pool", lambda e: e.memset(self.ones_bf[:], 1.0), writes=["ones_bf"])
        P.op("pool", lambda e: e.memset(self.eps_t[:], EPS), writes=["eps_t"])

    def prep(self, layers):
        P = self.P
        IN_SLOTS, OUT_SLOTS = 3, 3
        in_off = [i * 16384 for i in range(IN_SLOTS)]
        out_off = [IN_SLOTS * 16384 + i * 8192 for i in range(OUT_SLOTS)]
        cnt = 0
        for l in layers:
            P.dma("sp", "gl", lambda e, l=l: e.dma_start(out=self.gtile[:], in_=self.gains[l].rearrange("g p k -> p g k")),
                  writes=["gtile"])
            jobs = []
            for f in range(2):
                for c in range(NFC):
                    jobs.append((self.wgu[l, f, c], self.wgu_s[l][f][c], 0 if f == 0 else 2))
            for b in range(NBLK // 2):
                jobs.append((self.win[l, b], self.win_s[l][b], 1))
            for src, dst, gi in jobs:
                si, so = cnt % IN_SLOTS, cnt % OUT_SLOTS
                tin, kin = self.sv(in_off[si], 4096, F32)
                tout, kout = self.sv(out_off[so], 4096, BF16)
                P.dma("sp", f"pi{si}", lambda e, tin=tin, src=src: e.dma_start(out=tin, in_=src), writes=kin)
                eng = "dve" if cnt % 3 != 2 else "pool"
                g_b = self.gtile[:, gi, :].unsqueeze(1).unsqueeze(3).to_broadcast([128, 2, 16, 128])
                tin4 = tin.rearrange("p (j k c) -> p j k c", j=2, k=16)
                tout4 = tout.rearrange("p (j k c) -> p j k c", j=2, k=16)
                P.op(eng, lambda e, a=tout4, b=tin4, g=g_b: e.tensor_tensor(out=a, in0=b, in1=g, op=ALU.mult),
                     reads=kin + ["gtile"], writes=kout)
                P.dma("act", f"po{so}", lambda e, tout=tout, dst=dst: e.dma_start(out=dst, in_=tout), reads=kout,
                      writes=[("W", "gu_in", l)])
                cnt += 1
            for f in range(2):
                n_rows = self.wd.shape[2]
                step = n_rows // 8
                for i in range(8):
                    P.dma("pool", "cast", lambda e, l=l, f=f, i=i, step=step: e.dma_start(
                        out=self.wd_s[l][f][i * step:(i + 1) * step, :], in_=self.wd[l, f, i * step:(i + 1) * step, :]),
                        writes=[("W", "d", l)])
            n_rows = self.wout.shape[1]
            step = n_rows // 4
            for i in range(4):
                P.dma("pool", "cast", lambda e, l=l, i=i, step=step: e.dma_start(
                    out=self.wout_s[l][i * step:(i + 1) * step, :], in_=self.wout[l, i * step:(i + 1) * step, :]),
                    writes=[("W", "d", l)])

    def load_x(self, s):
        for c in range(16):
            self.P.dma("sp", "ldx", lambda e, c=c: e.dma_start(out=self.X[:, c, :], in_=self.xT[s, c]),
                       writes=self.xk([c], range(4)))

    def store_x(self, s):
        for c in range(16):
            self.P.dma("sp", "stx", lambda e, c=c: e.dma_start(out=self.yT[s, c], in_=self.X[:, c, :]),
                       reads=self.xk([c], range(4)))

    def rstd(self, t0, nt, r_view, r_keys, sq_off, scale_extra=None, rh_view=None, rh_keys=None, bank=6):
        P = self.P
        tgs = sorted({t // 512 for t in (t0, t0 + nt - 1)})
        for c in range(16):
            sq, ksq = self.sv(sq_off + (c % 2) * 1024, 512, BF16)
            P.op("act", lambda e, c=c, sq=sq: e.activation(out=sq[:, 0:nt], in_=self.X[:, c, t0:t0 + nt], func=AF.Square),
                 reads=self.xk([c], tgs), writes=ksq)
            P.op("pe", lambda e, c=c, sq=sq: e.matmul(self.ps[bank][:, 0:nt], lhsT=self.ones_bf[:], rhs=sq[:, 0:nt],
                                                     start=(c == 0), stop=(c == 15)),
                 reads=ksq + ["ones_bf"], writes=self.pk([bank]))
        P.op("act", lambda e: e.activation(out=r_view[:, 0:nt], in_=self.ps[bank][:, 0:nt], func=AF.Sqrt,
                                           bias=self.eps_t[:], scale=1.0 / D),
             reads=self.pk([bank]) + ["eps_t"], writes=r_keys)
        P.op("dve", lambda e: e.reciprocal(out=r_view[:, 0:nt], in_=r_view[:, 0:nt]), reads=r_keys, writes=r_keys)
        if rh_view is not None:
            P.op("pool", lambda e: e.tensor_scalar(out=rh_view[:, 0:nt], in0=r_view[:, 0:nt], scalar1=0.5, scalar2=None,
                                                   op0=ALU.mult), reads=r_keys, writes=rh_keys)

    def ffn(self, l, f):
        P = self.P
        ACT0 = 0
        WS = 44032
        WSZ = 22528
        RB = WS + WSZ
        RH = RB + 2048
        TA = RH + 2048
        TB = TA
        SQ = TA + 4096
        assert SQ + 2048 <= SCR_BYTES
        for tg in range(4):
            self._ffn_tg(l, f, tg, ACT0, WS, RB, RH, TA, TB, SQ)

    def _ffn_tg(self, l, f, tg, ACT0, WS, RB, RH, TA, TB, SQ):
        P = self.P
        if True:
            t0 = tg * 512
            rb, krb = self.sv(RB, 512, F32)
            rh, krh = self.sv(RH, 512, F32)
            self.rstd(t0, 512, rb, krb, SQ, rh_view=rh, rh_keys=krh)
            for c in range(NFC):
                slot = c % 2
                wt, kw = self.sv(WS + slot * 8192, 4096, BF16)
                wt4 = wt.rearrange("p (j k c) -> p j k c", j=2, k=16)
                P.dma("sp", f"wa{slot}", lambda e, wt=wt, c=c: e.dma_start(out=wt, in_=self.wgu_s[l][f][c]),
                      reads=[("W", "gu_in", l)], writes=kw)
                bg, bu = c % 2, 2 + c % 2

                def mm(e, j, bank, wt4=wt4):
                    ins = None
                    for kc in range(16):
                        ins = e.matmul(self.ps[bank][:], lhsT=wt4[:, j, kc, :], rhs=self.xhi(kc, t0, t0 + 512),
                                       start=(kc == 0), stop=(kc == 15))
                    return ins
                P.op("pe", lambda e, mm=mm, bg=bg: mm(e, 0, bg), reads=kw + self.xk(range(16), [tg]), writes=self.pk([bg]))
                P.op("pe", lambda e, mm=mm, bu=bu: mm(e, 1, bu), reads=kw + self.xk(range(16), [tg]), writes=self.pk([bu]))
                ta, kta = self.sv(TA + (c % 2) * 2048, 512, F32)
                P.op("dve", lambda e, ta=ta, bg=bg: e.tensor_tensor(out=ta, in0=self.ps[bg][:], in1=rb, op=ALU.mult),
                     reads=self.pk([bg]) + krb, writes=kta)
                P.op("act", lambda e, ta=ta: e.activation(out=ta, in_=ta, func=AF.Silu), reads=kta, writes=kta)
                av, kav = self.sv(ACT0 + c * 1024, 512, BF16)
                P.op("dve", lambda e, ta=ta, av=av, bu=bu: e.tensor_tensor(out=av, in0=self.ps[bu][:], in1=ta, op=ALU.mult),
                     reads=self.pk([bu]) + kta, writes=kav)
            actv, kact = self.sv(ACT0, NFC * 512, BF16)
            act3 = actv.rearrange("p (c t) -> p c t", c=NFC)
            rows_per_n = 512
            for n in range(16):
                slot = n % 2
                wt, kw = self.sv(WS + slot * 11264, NFC * 128, BF16)
                wt3 = wt.rearrange("p (c n) -> p c n", c=NFC)
                src = self.wd_s[l][f][n * rows_per_n:(n + 1) * rows_per_n, :].rearrange("(p a) b -> p (a b)", p=128)
                P.dma("sp", f"wb{slot}", lambda e, wt=wt, src=src: e.dma_start(out=wt, in_=src),
                      reads=[("W", "d", l)], writes=kw)
                bd = 4 + n % 2

                def mmd(e, bd=bd, wt3=wt3):
                    ins = None
                    for fc in range(NFC):
                        ins = e.matmul(self.ps[bd][:], lhsT=wt3[:, fc, :], rhs=act3[:, fc, :],
                                       start=(fc == 0), stop=(fc == NFC - 1))
                    return ins
                P.op("pe", mmd, reads=kw + kact, writes=self.pk([bd]))
                tb, ktb = self.sv(TB + (n % 2) * 2048, 512, F32)
                P.op("dve", lambda e, tb=tb, bd=bd: e.tensor_tensor(out=tb, in0=self.ps[bd][:], in1=rh, op=ALU.mult),
                     reads=self.pk([bd]) + krh, writes=ktb)
                xv = self.X[:, n, t0:t0 + 512]
                P.op("pool", lambda e, tb=tb, xv=xv: e.tensor_tensor(out=xv, in0=xv, in1=tb, op=ALU.add),
                     reads=ktb + self.xk([n], [tg]), writes=self.xk([n], [tg]))

    def E(self, eng, meth, reads, writes, **kw):
        return self.P.op(eng, lambda e: getattr(e, meth)(**kw), reads=list(reads), writes=list(writes))

    def MM(self, reads, writes, mms):
        def fn(e):
            ins = None
            for (o, l_, r_, st, sp) in mms:
                ins = e.matmul(o, lhsT=l_, rhs=r_, start=st, stop=sp)
            return ins
        return self.P.op("pe", fn, reads=list(reads), writes=list(writes))

    def TR(self, reads, writes, trs):
        def fn(e):
            ins = None
            for (o, i_) in trs:
                ins = e.transpose(o, i_, self.ident[:])
            return ins
        return self.P.op("pe", fn, reads=list(reads) + ["ident"], writes=list(writes))

    def DMA(self, eng, sem, out, in_, reads, writes):
        return self.P.dma(eng, sem, lambda e: e.dma_start(out=out, in_=in_), reads=list(reads), writes=list(writes))

    def pk(self, banks):
        return [("ps", b) for b in banks]

    def pq(self, b, q):
        return [("ps", b)]

    M_RB, M_WS, M_T2, M_T3, M_T1 = 0, 8192, 16384, 24576, 32768
    M_B1, M_B2, M_B3, M_B4, M_B5, M_B6, M_MSK, M_SM = 41216, 45312, 49408, 53760, 57856, 61952, 66048, 70144

    def wblk(self, l, blk):
        return self.win_s[l][blk // 2][:, (blk % 2) * 2048:(blk % 2 + 1) * 2048]

    def load_w(self, l, blk):
        slot = self.wcnt % 2
        self.wcnt += 1
        wt, kw = self.sv(self.M_WS + slot * 4096, 2048, BF16)
        self.DMA("sp", f"mw{slot}", wt, self.wblk(l, blk), [("W", "gu_in", l)], kw)
        return wt.rearrange("p (k c) -> p k c", k=16), kw

    def proj_fm(self, l, blk, b0):
        wt3, kw = self.load_w(l, blk)
        for tg in range(4):
            self.MM(kw + self.xk(range(16), [tg]), self.pk([b0 + tg]),
                    [(self.ps[b0 + tg][:], wt3[:, kc, :], self.xhi(kc, tg * 512, tg * 512 + 512), kc == 0, kc == 15)
                     for kc in range(16)])

    def proj_tm(self, l, blk, b0):
        wt3, kw = self.load_w(l, blk)
        for i in range(16):
            o = self.ps[b0 + i // 4][:, (i % 4) * 128:(i % 4 + 1) * 128]
            self.MM(kw + self.xk(range(16), [i // 4]), self.pk([b0 + i // 4]),
                    [(o, self.xhi(kc, i * 128, i * 128 + 128), wt3[:, kc, :], kc == 0, kc == 15) for kc in range(16)])

    def mixer_setup(self):
        P = self.P
        self.E("pool", "memset", [], ["zeros_bf"], ap=self.zeros_bf[:], constant=0.0)
        self.E("pool", "memset", [], ["one_f"], ap=self.one_f[:], constant=1.0)
        self.DMA("sp", "cst", self.ident[:], self.ident_d[:, :], [], ["ident"])
        self.DMA("sp", "cst", self.mask_f[:], self.mask_d[0], [], ["mask_f"])
        self.DMA("sp", "cst", self.mask_b[:], self.mask_d[1], [], ["mask_b"])
        self.DMA("sp", "cst", self.invf[:], self.invf_d[:, :], [], ["invf"])
        lbl = self.lbl
        self.DMA("sp", "cst", lbl[:], self.lbl_d[:, :, :], [], ["lbl"])
        self.E("act", "activation", ["lbl"], ["lbl"], out=lbl[:], in_=lbl[:], func=AF.Exp)
        self.E("dve", "tensor_reduce", ["lbl"], ["lbs"], out=self.lbs[:], in_=lbl[:], axis=AX.X, op=ALU.add)
        self.E("dve", "reciprocal", ["lbs"], ["lbs"], out=self.lbs[:], in_=self.lbs[:])
        self.E("dve", "tensor_tensor", ["lbl", "lbs"], ["lbl"], out=lbl[:], in0=lbl[:],
               in1=self.lbs[:].unsqueeze(2).to_broadcast([128, 16, 4]), op=ALU.mult)
        self.E("pool", "memset", [], ["lb"], ap=self.lb[:, :, 0:1], constant=0.0)
        for j in range(1, 4):
            self.E("dve", "tensor_tensor", ["lbl", "lb"], ["lb"], out=self.lb[:, :, j:j + 1], in0=self.lb[:, :, j - 1:j],
                   in1=lbl[:, :, j:j + 1], op=ALU.add)
        self.E("dve", "tensor_scalar", ["lb"], ["oml"], out=self.oml[:], in0=self.lb[:], scalar1=-1.0, scalar2=1.0,
               op0=ALU.mult, op1=ALU.add)

    def cossin(self, s):
        PI = math.pi
        ang, ka = self.sv(self.M_T1, 512, F32)
        tmp, kt = self.sv(self.M_T2, 512, F32)
        ki_, kki = self.sv(self.M_T3, 512, F32)
        pi_t, kpi = self.sv(self.M_T3 + 4096, 512, F32)
        kint = pi_t.bitcast(I32)
        posf, kpf = self.sv(self.M_B1, 16, F32)
        posi = posf.bitcast(I32)
        self.DMA("sp", "cst", posi, self.pos_d[s], [], kpf)
        self.E("dve", "tensor_copy", kpf, kpf, out=posf, in_=posi)
        ang3 = ang.rearrange("p (i j) -> p i j", i=16)
        self.E("dve", "tensor_tensor", kpf + ["invf"], ka, out=ang3, in0=posf.unsqueeze(2).to_broadcast([128, 16, 32]),
               in1=self.invf[:].unsqueeze(1).to_broadcast([128, 16, 32]), op=ALU.mult)
        for which, shift in ((0, 0.0), (1, PI / 2)):
            self.E("dve", "tensor_scalar", ka, kt, out=tmp, in0=ang, scalar1=shift, scalar2=1.0 / (2 * PI), op0=ALU.add,
                   op1=ALU.mult)
            self.E("dve", "tensor_copy", kt, kpi, out=kint, in_=tmp)
            self.E("dve", "tensor_copy", kpi, kki, out=ki_, in_=kint)
            self.E("dve", "scalar_tensor_tensor", kki + ka, kt, out=tmp, in0=ki_, scalar=-2 * PI, in1=ang, op0=ALU.mult,
                   op1=ALU.add)
            if shift != 0.0:
                self.E("dve", "tensor_scalar", kt, kt, out=tmp, in0=tmp, scalar1=shift, scalar2=None, op0=ALU.add)
            self.E("dve", "tensor_scalar", kt, kki, out=ki_, in0=tmp, scalar1=PI, scalar2=-2 * PI, op0=ALU.is_gt, op1=ALU.mult)
            self.E("dve", "tensor_tensor", kt + kki, kt, out=tmp, in0=tmp, in1=ki_, op=ALU.add)
            self.E("dve", "tensor_scalar", kt, kki, out=ki_, in0=tmp, scalar1=-PI, scalar2=2 * PI, op0=ALU.is_lt, op1=ALU.mult)
            self.E("dve", "tensor_tensor", kt + kki, kt, out=tmp, in0=tmp, in1=ki_, op=ALU.add)
            self.E("dve", "tensor_scalar", kt, kt, out=tmp, in0=tmp, scalar1=PI, scalar2=-PI, op0=ALU.min, op1=ALU.max)
            self.E("act", "activation", kt, kki, out=ki_, in_=tmp, func=AF.Sin)
            self.DMA("sp", "cst", self.cs_s[1 - which], ki_, kki, [("cs",)])

    def mixer(self, l, s):
        P = self.P
        self.wcnt = 0
        rb, krb = self.sv(self.M_RB, 2048, F32)
        for tg in range(4):
            v, kv = self.sv(self.M_RB + tg * 2048, 512, F32)
            self.rstd(tg * 512, 512, v, kv, self.M_SM)
        for i in range(16):
            self.MM(krb + ["one_f"], self.pk([7]), [(self.ps[7][:, i:i + 1], rb[0:1, i * 128:(i + 1) * 128], self.one_f[0:1, 0:1], True, True)])
        self.E("dve", "tensor_copy", self.pk([7]), ["rcol"], out=self.rcol[:], in_=self.ps[7][:, 0:16])
        self.E("dve", "tensor_tensor", ["rcol"], ["r2col"], out=self.r2col[:], in0=self.rcol[:], in1=self.rcol[:], op=ALU.mult)
        self.DMA("sp", "cst", self.small[:], self.small_d[l], [], ["small"])
        self.DMA("sp", "cst", self.bc[:], self.bc_d[l], [], ["bc"])
        parts = self.cfg.get("mix", ("conv", "attn", "hgrn"))
        if "conv" in parts:
            for gi in range(4):
                self.conv_group(l, gi, rb, krb)
        if "attn" in parts:
            self.attn_setup(l)
            for a in range(4):
                self.attn_head(l, a, rb, krb)
        if "hgrn" in parts:
            self.hgrn_setup()
            for h in range(8):
                self.hgrn_head(l, h, rb, krb)
        ych = []
        if "hgrn" in parts:
            ych += list(range(0, 8))
        if "attn" in parts:
            ych += list(range(8, 12))
        if "conv" in parts:
            ych += list(range(12, 16))
        self.wout_stage(l, ych)

    def conv_group(self, l, gi, rb, krb):
        T1, k1 = self.sv(self.M_T1, 2050, F32)
        T2, k2 = self.sv(self.M_T2, 2048, F32)
        T3, k3 = self.sv(self.M_T3, 2048, F32)
        SQ, ksq = self.sv(self.M_B6, 2048, BF16)
        YB, kyb = self.sv(self.M_B5, 2048, BF16)
        sm = self.small
        c0 = gi * 5
        self.E("pool", "memset", [], k1, ap=T1[:, 0:1], constant=0.0)
        self.E("pool", "memset", [], k1, ap=T1[:, 2049:2050], constant=0.0)
        self.proj_fm(l, 56 + gi, 0)
        for tg in range(4):
            self.E("dve", "tensor_tensor", self.pk([tg]) + krb, k1, out=T1[:, 1 + tg * 512:1 + tg * 512 + 512],
                   in0=self.ps[tg][:], in1=rb[:, tg * 512:(tg + 1) * 512], op=ALU.mult)
        self.proj_fm(l, 60 + gi, 4)
        for tg in range(4):
            self.E("dve", "tensor_tensor", self.pk([4 + tg]) + krb, k2, out=T2[:, tg * 512:(tg + 1) * 512],
                   in0=self.ps[4 + tg][:], in1=rb[:, tg * 512:(tg + 1) * 512], op=ALU.mult)
        self.E("pool", "tensor_tensor", k1 + k2, k1, out=T1[:, 1:2049], in0=T1[:, 1:2049], in1=T2, op=ALU.mult)
        self.E("dve", "tensor_scalar", k1 + ["small"], k2, out=T2, in0=T1[:, 1:2049], scalar1=sm[:, c0 + 1:c0 + 2],
               scalar2=sm[:, c0 + 3:c0 + 4], op0=ALU.mult, op1=ALU.add)
        self.E("dve", "scalar_tensor_tensor", k1 + k2 + ["small"], k2, out=T2, in0=T1[:, 0:2048], scalar=sm[:, c0:c0 + 1],
               in1=T2, op0=ALU.mult, op1=ALU.add)
        self.E("dve", "scalar_tensor_tensor", k1 + k2 + ["small"], k2, out=T2, in0=T1[:, 2:2050], scalar=sm[:, c0 + 2:c0 + 3],
               in1=T2, op0=ALU.mult, op1=ALU.add)
        self.proj_fm(l, 52 + gi, 0)
        for tg in range(4):
            self.E("dve", "tensor_tensor", self.pk([tg]) + krb, k3, out=T3[:, tg * 512:(tg + 1) * 512],
                   in0=self.ps[tg][:], in1=rb[:, tg * 512:(tg + 1) * 512], op=ALU.mult)
        self.E("pool", "tensor_tensor", k2 + k3, k2, out=T2, in0=T2, in1=T3, op=ALU.mult)
        self.pnorm_store(T2, k2, T3, k3, SQ, ksq, YB, kyb, 4, sm[:, c0 + 4:c0 + 5], None, None, 12 + gi, 1.0)

    def pnorm_store(self, Tin, kin, Ttmp, ktmp, SQ, ksq, YB, kyb, b0, gain_col, mul_t, kmul, ychunk, in_is_psum_banks=None):
        self.E("act", "activation", kin, ksq, out=SQ, in_=Tin, func=AF.Square)
        for tg in range(4):
            self.MM(ksq + ["ones_bf"], self.pk([b0 + tg]),
                    [(self.ps[b0 + tg][:], self.ones_bf[:], SQ[:, tg * 512:(tg + 1) * 512], True, True)])
            self.E("act", "activation", self.pk([b0 + tg]) + ["eps_t"], ktmp, out=Ttmp[:, tg * 512:(tg + 1) * 512],
                   in_=self.ps[b0 + tg][:], func=AF.Sqrt, bias=self.eps_t[:], scale=1.0 / 128)
        self.E("dve", "reciprocal", ktmp, ktmp, out=Ttmp, in_=Ttmp)
        if mul_t is None:
            self.E("dve", "scalar_tensor_tensor", kin + ktmp + ["small"], kyb, out=YB, in0=Tin, scalar=gain_col, in1=Ttmp,
                   op0=ALU.mult, op1=ALU.mult)
        else:
            self.E("dve", "scalar_tensor_tensor", kin + ktmp + ["small"], ktmp, out=Ttmp, in0=Tin, scalar=gain_col, in1=Ttmp,
                   op0=ALU.mult, op1=ALU.mult)
            self.E("dve", "tensor_tensor", ktmp + kmul, kyb, out=YB, in0=Ttmp, in1=mul_t, op=ALU.mult)
        self.DMA("sp", "ysto", self.y_s[ychunk], YB, kyb, [("y", ychunk)])

    def wout_stage(self, l, ych):
        for tg in range(4):
            yt, ky = self.sv(self.M_T2, 16 * 512, BF16)
            yt3 = yt.rearrange("p (c t) -> p c t", c=16)
            for c in ych:
                self.DMA("sp", "yld", yt3[:, c, :], self.y_s[c][:, tg * 512:(tg + 1) * 512], [("y", c)], ky)
            for n in range(16):
                slot = self.wcnt % 2
                self.wcnt += 1
                wt, kw = self.sv(self.M_WS + slot * 4096, 2048, BF16)
                self.DMA("sp", f"mw{slot}", wt, self.wout_s[l][n * 128:(n + 1) * 128, :], [("W", "d", l)], kw)
                wt3 = wt.rearrange("p (k c) -> p k c", k=16)
                b = n % 2
                self.MM(kw + ky, self.pk([b]), [(self.ps[b][:], wt3[:, c, :], yt3[:, c, :], j == 0, j == len(ych) - 1)
                                                for j, c in enumerate(ych)])
                xv = self.X[:, n, tg * 512:(tg + 1) * 512]
                self.E("dve", "tensor_tensor", self.pk([b]) + self.xk([n], [tg]), self.xk([n], [tg]), out=xv, in0=self.ps[b][:],
                       in1=xv, op=ALU.add)

    def attn_setup(self, l):
        bc3 = self.bc[:].rearrange("p (g d) -> p g d", g=6)
        lt, kl = self.sv(self.M_SM, 128, F32)
        lt3 = lt.rearrange("p (g d) -> p g d", g=2)
        ls, kls = self.sv(self.M_SM + 512, 2, F32)
        self.E("dve", "tensor_tensor", ["bc"], kl, out=lt3, in0=bc3[:, 2:6:2, :], in1=bc3[:, 3:6:2, :], op=ALU.mult)
        self.E("dve", "tensor_reduce", kl, kls, out=ls, in_=lt3, axis=AX.X, op=ALU.add)
        self.E("act", "activation", kls, kls, out=ls, in_=ls, func=AF.Exp)
        lam_init = 0.8 - 0.6 * math.exp(-0.3 * l)
        self.E("dve", "tensor_tensor", kls, ["nlam"], out=self.nlam[:], in0=ls[:, 1:2], in1=ls[:, 0:1], op=ALU.subtract)
        self.E("dve", "tensor_scalar", ["nlam"], ["nlam"], out=self.nlam[:], in0=self.nlam[:], scalar1=-lam_init, scalar2=None,
               op0=ALU.add)
        self.lam_init = lam_init

    def attn_head(self, l, a, rb, krb):
        sv = self.sv
        COS, kcos = sv(self.M_T2, 512, F32)
        SIN, ksin = sv(self.M_T2 + 2048, 512, F32)
        self.DMA("sp", "cst", COS, self.cs_s[0], [("cs",)], kcos)
        self.DMA("sp", "cst", SIN, self.cs_s[1], [("cs",)], ksin)
        cos4 = COS.rearrange("p (i d) -> p i d", i=16).unsqueeze(2).to_broadcast([128, 16, 2, 32])
        sin4 = SIN.rearrange("p (i d) -> p i d", i=16).unsqueeze(2).to_broadcast([128, 16, 2, 32])
        QT, kqt = sv(self.M_B1, 2048, BF16)
        KT, kkt = sv(self.M_B2, 2048, BF16)
        VA, kva = sv(self.M_B3, 16 * 130, BF16)
        VA3 = VA.rearrange("p (i c) -> p i c", i=16)
        QN, kqn = sv(self.M_B4, 2048, BF16)
        QN4 = QN.rearrange("p (i m d) -> p i m d", i=16, m=2)
        QN3 = QN.rearrange("p (i c) -> p i c", i=16)
        T1, k1 = sv(self.M_T1, 2048, F32)
        T1v = T1.rearrange("p (i m d) -> p i m d", i=16, m=2)
        RT, krt = sv(self.M_B5, 2048, F32)
        RT5 = RT.rearrange("p (h i m d) -> p h i m d", h=2, i=16, m=2)
        SS, kss = sv(self.M_SM + 1024, 32, F32)
        SS3 = SS.rearrange("p (i m) -> p i m", i=16)
        bc3 = self.bc[:].rearrange("p (g d) -> p g d", g=6)
        for which, (blk, dst, kdst) in enumerate(((40 + a, QT, kqt), (44 + a, KT, kkt))):
            self.proj_tm(l, blk, 0)
            for b in range(4):
                self.E("act", "activation", self.pk([b]), k1, out=T1[:, b * 512:(b + 1) * 512], in_=self.ps[b][:], func=AF.Square)
            self.E("dve", "tensor_reduce", k1, kss, out=SS, in_=T1.rearrange("p (g d) -> p g d", d=64), axis=AX.X, op=ALU.add)
            self.E("dve", "tensor_tensor", kss + ["r2col"], kss, out=SS3, in0=SS3,
                   in1=self.r2col[:].unsqueeze(2).to_broadcast([128, 16, 2]), op=ALU.mult)
            self.E("act", "activation", kss + ["eps_t"], kss, out=SS, in_=SS, func=AF.Sqrt, bias=self.eps_t[:], scale=1.0 / 64)
            self.E("dve", "reciprocal", kss, kss, out=SS, in_=SS)
            self.E("dve", "tensor_tensor", kss + ["rcol"], kss, out=SS3, in0=SS3,
                   in1=self.rcol[:].unsqueeze(2).to_broadcast([128, 16, 2]), op=ALU.mult)
            if which == 0:
                self.E("dve", "tensor_scalar", kss, kss, out=SS, in0=SS, scalar1=0.125, scalar2=None, op0=ALU.mult)
            for b in range(4):
                self.E("dve", "tensor_tensor", self.pk([b]) + kss, k1, out=T1v[:, 4 * b:4 * b + 4, :, :],
                       in0=self.ps[b][:].rearrange("p (i m d) -> p i m d", i=4, m=2),
                       in1=SS3[:, 4 * b:4 * b + 4, :].unsqueeze(3).to_broadcast([128, 4, 2, 64]), op=ALU.mult)
            self.E("dve", "tensor_tensor", k1 + ["bc"], k1, out=T1.rearrange("p (g d) -> p g d", d=64),
                   in0=T1.rearrange("p (g d) -> p g d", d=64),
                   in1=bc3[:, which, :].unsqueeze(1).to_broadcast([128, 32, 64]), op=ALU.mult)
            t1 = T1v[:, :, :, 0:32]
            t2 = T1v[:, :, :, 32:64]
            self.E("dve", "tensor_tensor", k1 + kcos, krt, out=RT5[:, 0], in0=t1, in1=cos4, op=ALU.mult)
            self.E("pool", "tensor_tensor", k1 + ksin, krt, out=RT5[:, 1], in0=t2, in1=sin4, op=ALU.mult)
            self.E("dve", "tensor_tensor", krt, kqn, out=QN4[:, :, :, 0:32], in0=RT5[:, 0], in1=RT5[:, 1], op=ALU.subtract)
            self.E("dve", "tensor_tensor", k1 + kcos, krt, out=RT5[:, 0], in0=t2, in1=cos4, op=ALU.mult)
            self.E("pool", "tensor_tensor", k1 + ksin, krt, out=RT5[:, 1], in0=t1, in1=sin4, op=ALU.mult)
            self.E("dve", "tensor_tensor", krt, kqn, out=QN4[:, :, :, 32:64], in0=RT5[:, 0], in1=RT5[:, 1], op=ALU.add)
            for hb in range(2):
                pT = self.ps[4 + hb][:].bitcast(BF16)
                self.TR(kqn, self.pk([4 + hb]), [(pT[:, j * 128:(j + 1) * 128], QN3[:, hb * 8 + j, :]) for j in range(8)])
                self.E("act", "activation", self.pk([4 + hb]), kdst, out=dst[:, hb * 1024:(hb + 1) * 1024], in_=pT, func=AF.Copy)
        self.proj_tm(l, 48 + a, 0)
        for b in range(4):
            self.E("dve", "tensor_tensor", self.pk([b]) + ["rcol"], kva, out=VA3[:, 4 * b:4 * b + 4, 0:128],
                   in0=self.ps[b][:].rearrange("p (i c) -> p i c", i=4),
                   in1=self.rcol[:, 4 * b:4 * b + 4].unsqueeze(2).to_broadcast([128, 4, 128]), op=ALU.mult)
        self.E("pool", "memset", [], kva, ap=VA3[:, :, 128:129], constant=1.0)
        PT, kpt = sv(self.M_T2, 16 * 512, BF16)
        PT3 = PT.rearrange("p (k q) -> p k q", k=16)
        ON, kon = sv(self.M_MSK, 1024, F32)
        ON4 = ON.rearrange("p (m q d) -> p m q d", m=2, q=4)
        YB, kyb = sv(self.M_B5, 2048, BF16)
        OB, kob = sv(self.M_B4, 512, BF16)
        OB3 = OB.rearrange("p (q d) -> p q d", q=4)
        RS, krs = sv(self.M_SM + 1280, 1, F32)
        S4, ks4 = sv(self.M_SM + 1536, 4, F32)
        AVB = [0, 1, 6, 7]
        kptk = [sv(self.M_T2 + kt * 1024, 512, BF16)[1] for kt in range(16)]
        groups = [(qg, m) for qg in range(4) for m in range(2)]

        def combine(qg):
            self.E("dve", "scalar_tensor_tensor", kon + ["nlam"], kon, out=ON4[:, 0], in0=ON4[:, 1], scalar=self.nlam[:],
                   in1=ON4[:, 0], op0=ALU.mult, op1=ALU.add)
            self.E("dve", "tensor_tensor", kon, kon, out=ON4[:, 1], in0=ON4[:, 0], in1=ON4[:, 0], op=ALU.mult)
            self.E("dve", "tensor_reduce", kon, ks4, out=S4, in_=ON4[:, 1], axis=AX.X, op=ALU.add)
            self.E("act", "activation", ks4 + ["eps_t"], ks4, out=S4, in_=S4, func=AF.Sqrt, bias=self.eps_t[:], scale=1.0 / 128)
            self.E("dve", "reciprocal", ks4, ks4, out=S4, in_=S4)
            self.E("dve", "tensor_scalar", ks4, ks4, out=S4, in0=S4, scalar1=1.0 - self.lam_init, scalar2=None, op0=ALU.mult)
            self.E("dve", "tensor_tensor", kon + ks4, kob, out=OB3, in0=ON4[:, 0], in1=S4.unsqueeze(2).to_broadcast([128, 4, 128]),
                   op=ALU.mult)
            pT = self.ps[4][:].bitcast(BF16)
            self.TR(kob, self.pk([4]), [(pT[:, j * 128:(j + 1) * 128], OB3[:, j, :]) for j in range(4)])
            self.E("act", "activation", self.pk([4]) + ["small"], kyb, out=YB[:, qg * 512:(qg + 1) * 512], in_=pT[:, 0:512],
                   func=AF.Copy, scale=self.small[:, 28 + a:29 + a])

        for gi in range(len(groups) + 1):
            for kt in range(16):
                if gi < len(groups):
                    qg, m = groups[gi]
                    b = 2 + (kt % 2)
                    self.MM(kkt + kqt, self.pk([b]), [(self.ps[b][:], KT[m * 64:(m + 1) * 64, kt * 128:(kt + 1) * 128],
                                                      QT[m * 64:(m + 1) * 64, qg * 512:(qg + 1) * 512], True, True)])
                if gi >= 1:
                    for qt in range(4):
                        bo = AVB[qt]
                        self.MM(kptk[kt] + kva, self.pk([bo]), [(self.ps[bo][:, 0:129], PT3[:, kt, qt * 128:(qt + 1) * 128],
                                                                VA3[:, kt, 0:129], kt == 0, kt == 15)])
                if gi < len(groups):
                    self.E("act", "activation", self.pk([b]), kptk[kt], out=PT3[:, kt, :], in_=self.ps[b][:], func=AF.Exp)
            if gi >= 1:
                qg_p, m_p = groups[gi - 1]
                for qt in range(4):
                    bo = AVB[qt]
                    self.E("dve", "reciprocal", self.pk([bo]), krs, out=RS, in_=self.ps[bo][:, 128:129])
                    self.E("dve", "tensor_scalar", self.pk([bo]) + krs, kon, out=ON4[:, m_p, qt, :], in0=self.ps[bo][:, 0:128],
                           scalar1=RS, scalar2=None, op0=ALU.mult)
                if m_p == 1:
                    combine(qg_p)
        self.DMA("sp", "ysto", self.y_s[8 + a], YB, kyb, [("y", 8 + a)])

    def hgrn_setup(self):
        MSK, kmsk = self.sv(self.M_MSK, 2048, BF16)
        self.E("pool", "memset", [], kmsk, ap=MSK, constant=1.0)
        self.E("pool", "memset", [], kmsk, ap=MSK.rearrange("p (c k) -> p c k", k=64)[:, :, 0:1], constant=0.0)

    def _hgrn_pre(self, l, h, rb, krb, stage, only=None):
        QS, kqs = self.sv(self.M_B1, 2048, BF16)
        VT, kvt = self.sv(self.M_B3, 2048, BF16)
        VT3 = VT.rearrange("p (i c) -> p i c", i=16)
        if stage == 0:
            self.proj_fm(l, h, 0)
        elif stage == 1:
            for tg in (range(4) if only is None else [only]):
                self.E("dve", "tensor_tensor", self.pk([tg]) + krb, kqs, out=QS[:, tg * 512:(tg + 1) * 512], in0=self.ps[tg][:],
                       in1=rb[:, tg * 512:(tg + 1) * 512], op=ALU.mult)
        elif stage == 2:
            self.proj_tm(l, 8 + h, 0)
        else:
            for b in range(4):
                self.E("dve", "tensor_tensor", self.pk([b]) + ["rcol"], kvt, out=VT3[:, 4 * b:4 * b + 4, :],
                       in0=self.ps[b][:].rearrange("p (i c) -> p i c", i=4),
                       in1=self.rcol[:, 4 * b:4 * b + 4].unsqueeze(2).to_broadcast([128, 4, 128]), op=ALU.mult)

    def hgrn_head(self, l, h, rb, krb):
        sv = self.sv
        MSK, kmsk = sv(self.M_MSK, 2048, BF16)
        T1, k1 = sv(self.M_T1, 2048, F32)
        T2, k2 = sv(self.M_T2, 2048, F32)
        T3, k3 = sv(self.M_T3, 2048, F32)
        QS, kqs = sv(self.M_B1, 2048, BF16)
        GS, kgs = sv(self.M_B2, 2048, BF16)
        VT, kvt = sv(self.M_B3, 2048, BF16)
        VT3 = VT.rearrange("p (i c) -> p i c", i=16)
        QP, kqp = sv(self.M_B4, 2048, BF16)
        KP, kkp = sv(self.M_B5, 2048, BF16)
        KPT, kkpt = sv(self.M_B6, 2048, BF16)
        KPT3 = KPT.rearrange("p (i c) -> p i c", i=16)
        sm0 = self.M_SM
        Sst = [sv(sm0 + j * 512, 128, F32) for j in range(2)]
        Sp = [sv(sm0 + 1024 + j * 256, 128, BF16) for j in range(4)]
        MS = [sv(sm0 + 2048 + j * 256, 128, BF16) for j in range(4)]
        KVb = [sv(sm0 + 3072 + j * 512, 128, F32) for j in range(4)]
        MC, kmc = sv(sm0 + 5120, 32, F32)
        BC, kbc = sv(sm0 + 5248, 32, F32)
        EM, kem = sv(sm0 + 5376, 32, F32)
        EB, keb = sv(sm0 + 5504, 32, F32)
        EBM, kebm = sv(sm0 + 5632, 32, F32)
        T3v = T3.rearrange("p (c k) -> p c k", k=64)
        if h == 0:
            for stage in range(4):
                self._hgrn_pre(l, 0, rb, krb, stage)
        self.proj_fm(l, 32 + h, 0)
        for tg in range(4):
            self.E("dve", "tensor_tensor", self.pk([tg]) + krb, k1, out=T1[:, tg * 512:(tg + 1) * 512], in0=self.ps[tg][:],
                   in1=rb[:, tg * 512:(tg + 1) * 512], op=ALU.mult)
        self.E("act", "activation", k1, kgs, out=GS, in_=T1, func=AF.Silu)
        for b in range(4, 8):
            self.MM(kqs + ["zeros_bf"], self.pk([b]), [(self.ps[b][:], self.zeros_bf[:], QS[:, 0:512], True, True)])
        lbi = lambda d: self.lb[:, d * 8 + h, l:l + 1]
        omli = lambda d: self.oml[:, d * 8 + h, l:l + 1]
        for d in range(2):
            self.proj_fm(l, 16 + 8 * d + h, 0)
            for tg in range(4):
                self.E("dve", "tensor_tensor", self.pk([tg]) + krb, k1, out=T1[:, tg * 512:(tg + 1) * 512], in0=self.ps[tg][:],
                       in1=rb[:, tg * 512:(tg + 1) * 512], op=ALU.mult)
            self.E("act", "activation", k1, k1, out=T1, in_=T1, func=AF.Sigmoid)
            self.E("dve", "tensor_scalar", k1 + ["lb", "oml"], k1, out=T1, in0=T1, scalar1=omli(d), scalar2=lbi(d), op0=ALU.mult,
                   op1=ALU.add)
            self.E("act", "activation", k1, k2, out=T2, in_=T1, func=AF.Ln)
            self.E("dve", "tensor_scalar", k1, k1, out=T1, in0=T1, scalar1=-1.0, scalar2=1.0, op0=ALU.mult, op1=ALU.add)
            self.E("dve", "tensor_tensor_scan", k2 + kmsk, k3, out=T3, data0=MSK, data1=T2, initial=0.0, op0=ALU.mult,
                   op1=ALU.add)
            if d == 0:
                self.E("dve", "tensor_copy", k3, kmc, out=MC, in_=T3v[:, :, 31])
                self.E("dve", "tensor_copy", k3, kbc, out=BC, in_=T3v[:, :, 63])
            else:
                self.E("dve", "tensor_copy", k3, kbc, out=BC, in_=T3v[:, :, 63])
                self.E("dve", "tensor_tensor", k2 + k3, k3, out=T3, in0=T2, in1=T3, op=ALU.subtract)
                self.E("dve", "tensor_tensor", k3 + kbc, k3, out=T3v, in0=T3v, in1=BC.unsqueeze(2).to_broadcast([128, 32, 64]),
                       op=ALU.add)
                self.E("dve", "tensor_copy", k3, kmc, out=MC, in_=T3v[:, :, 32])
            self.E("dve", "tensor_tensor", k3 + kmc, k3, out=T3v, in0=T3v, in1=MC.unsqueeze(2).to_broadcast([128, 32, 64]),
                   op=ALU.subtract)
            self.E("act", "activation", k3, k2, out=T2, in_=T3, func=AF.Exp)
            self.E("dve", "tensor_tensor", k2 + kqs, kqp, out=QP, in0=T2, in1=QS, op=ALU.mult)
            self.E("act", "activation", k3, k2, out=T2, in_=T3, func=AF.Exp, scale=-1.0)
            self.E("dve", "tensor_tensor", k2 + k1, kkp, out=KP, in0=T2, in1=T1, op=ALU.mult)
            self.E("act", "activation", kmc, kem, out=EM, in_=MC, func=AF.Exp)
            self.E("act", "activation", kbc, keb, out=EB, in_=BC, func=AF.Exp)
            self.E("dve", "tensor_tensor", kbc + kmc, kebm, out=EBM, in0=BC, in1=MC, op=ALU.subtract)
            self.E("act", "activation", kebm, kebm, out=EBM, in_=EBM, func=AF.Exp)
            for hb in range(2):
                pT = self.ps[hb][:].bitcast(BF16)
                self.TR(kkp, self.pk([hb]), [(pT[:, j * 128:(j + 1) * 128], KP[:, (hb * 8 + j) * 128:(hb * 8 + j + 1) * 128])
                                             for j in range(8)])
                self.E("act", "activation", self.pk([hb]), kkpt, out=KPT[:, hb * 1024:(hb + 1) * 1024], in_=pT, func=AF.Copy)
            mask = self.mask_f if d == 0 else self.mask_b
            mkey = "mask_f" if d == 0 else "mask_b"
            def c1(i):
                b, q = i % 4, 0
                self.MM(kkp + kqp, self.pq(b, q), [(self.ps[b][:, q * 128:(q + 1) * 128], KP[:, i * 128:(i + 1) * 128],
                                                   QP[:, i * 128:(i + 1) * 128], True, True)])
                ms, kms = MS[i % 4]
                self.E("dve", "scalar_tensor_tensor", self.pq(b, q) + [mkey], kms, out=ms, in0=self.ps[b][:, q * 128:(q + 1) * 128],
                       scalar=3.0e38, in1=mask[:], op0=ALU.min, op1=ALU.mult)

            def c2(i):
                ms, kms = MS[i % 4]
                bo = 4 + i // 4
                self.MM(kms + kvt, self.pq(bo, i % 4), [(self.ps[bo][:, (i % 4) * 128:(i % 4 + 1) * 128], VT3[:, i, :], ms, False, False)])
            for i in range(16 + 2):
                if i < 16:
                    c1(i)
                if i >= 2:
                    c2(i - 2)
            order = list(range(32)) if d == 0 else list(range(31, -1, -1))
            KVA, kkva = sv(self.M_T2, 32 * 128, F32)
            KVA3 = KVA.rearrange("p (c v) -> p c v", c=32)
            for sa in range(31):
                c = order[sa]
                i, hh = c // 2, c % 2
                bnk = sa % 4
                self.MM(kkpt + kvt, self.pk([bnk]), [(self.ps[bnk][:, 0:128], KPT3[hh * 64:(hh + 1) * 64, i, :],
                                                     VT3[hh * 64:(hh + 1) * 64, i, :], True, True)])
                self.E("dve", "tensor_scalar", self.pk([bnk]) + kebm, kkva, out=KVA3[:, sa, :], in0=self.ps[bnk][:, 0:128],
                       scalar1=EBM[:, c:c + 1], scalar2=None, op0=ALU.mult)
            prev = KVA3[:, 0, :]
            kprev = kkva
            nxt = (d == 1 and h < 7)
            if nxt:
                self._hgrn_pre(l, h + 1, rb, krb, 0)
            for sc in range(31):
                c = order[sc]
                if nxt and sc in (3, 6, 9, 12):
                    self._hgrn_pre(l, h + 1, rb, krb, 1, only=(sc // 3 - 1))
                    if sc == 12:
                        self._hgrn_pre(l, h + 1, rb, krb, 2)
                if sc > 0:
                    stn, kstn = Sst[sc % 2]
                    self.E("dve", "scalar_tensor_tensor", kprev + kkva + keb, kstn, out=stn, in0=prev, scalar=EB[:, c:c + 1],
                           in1=KVA3[:, sc, :], op0=ALU.mult, op1=ALU.add)
                    prev, kprev = stn, kstn
                cn = order[sc + 1]
                spn, kspn = Sp[(sc + 1) % 4]
                self.E("act", "activation", kprev + kem, kspn, out=spn, in_=prev, func=AF.Copy, scale=EM[:, cn:cn + 1])
                bo = 4 + cn // 8
                self.MM(kspn + kqp, self.pk([bo]), [(self.ps[bo][:, (cn % 8) * 64:(cn % 8 + 1) * 64], spn,
                                                    QP[:, cn * 64:(cn + 1) * 64], False, False)])
        if h < 7:
            self._hgrn_pre(l, h + 1, rb, krb, 3)
        for tg in range(4):
            self.E("act", "activation", self.pk([4 + tg]), k1, out=T1[:, tg * 512:(tg + 1) * 512], in_=self.ps[4 + tg][:], func=AF.Copy)
        SQ, ksq = sv(self.M_B6, 2048, BF16)
        YB, kyb = sv(self.M_B5, 2048, BF16)
        self.pnorm_store(T1, k1, T2, k2, SQ, ksq, YB, kyb, 0, self.small[:, 20 + h:21 + h], GS, kgs, h)


def build(cfg):
    nc = bass.Bass("TRN2", target_bir_lowering=False)
    with ExitStack() as es:
        k = K(nc, es, cfg)
        k.setup()
        layers = cfg.get("layers", list(range(DEPTH)))
        k.mixer_setup()
        k.prep(layers)
        for s in range(cfg.get("nseq", 2)):
            k.load_x(s)
            if "mix" in cfg["stages"]:
                k.cossin(s)
            for l in layers:
                if "ffn1" in cfg["stages"]:
                    k.ffn(l, 0)
                if "mix" in cfg["stages"]:
                    k.mixer(l, s)
                if "ffn2" in cfg["stages"]:
                    k.ffn(l, 1)
            k.store_x(s)
        k.P.wait_all("sp")
        k.P.emit()
    return nc


def host_layouts(inp):
    L = DEPTH
    f = lambda a: np.ascontiguousarray(np.asarray(a, dtype=np.float32))
    out = {}

    def kxn(w, ncol_blocks):
        w = np.asarray(w, dtype=np.float32).reshape(L, 16, 128, ncol_blocks, 128)
        return w.transpose(0, 3, 2, 1, 4)

    wgu = np.empty((L, 2, NFC, 128, 2, 16, 128), np.float32)
    for fi, (g, u) in enumerate((("ffn1_w_gate", "ffn1_w_up"), ("ffn2_w_gate", "ffn2_w_up"))):
        wgu[:, fi, :, :, 0] = kxn(inp[g], NFC)
        wgu[:, fi, :, :, 1] = kxn(inp[u], NFC)
    out["wgu"] = wgu.reshape(L, 2, NFC, 128, 4096)
    wd = np.empty((L, 2, 16, 128, NFC, 128), np.float32)
    for fi, dname in enumerate(("ffn1_w_down", "ffn2_w_down")):
        w = np.asarray(inp[dname], dtype=np.float32).reshape(L, NFC, 128, 16, 128)
        wd[:, fi] = w.transpose(0, 3, 2, 1, 4)
    out["wd"] = wd.reshape(L, 2, -1, 1376)
    win = kxn(inp["w_in"], NBLK)
    out["win"] = np.ascontiguousarray(win).reshape(L, NBLK // 2, 2, 128, 16, 128).transpose(0, 1, 3, 2, 4, 5).reshape(
        L, NBLK // 2, 128, 4096)
    out["wout"] = np.ascontiguousarray(kxn(inp["w_out"], 16)).reshape(L, -1, 2048)
    gains = np.stack([np.asarray(inp[n], dtype=np.float32).reshape(L, 16, 128).transpose(0, 2, 1)
                      for n in ("ffn1_norm", "mix_norm", "ffn2_norm")], axis=1)
    out["gains"] = f(gains)
    import ml_dtypes
    bf = ml_dtypes.bfloat16
    out["ident"] = np.eye(128, dtype=np.float32).astype(bf)
    ii = np.arange(128)
    same = (ii[:, None] // 64) == (ii[None, :] // 64)
    mf = (same & (ii[:, None] <= ii[None, :])).astype(np.float32)
    mb = (same & (ii[:, None] >= ii[None, :])).astype(np.float32)
    out["masks"] = np.stack([mf, mb]).astype(bf)
    inv_freq = (10000.0 ** (-np.arange(0, 64, 2, dtype=np.float32) / 64)).astype(np.float32)
    out["invf"] = np.broadcast_to(inv_freq[None, :], (128, 32)).copy()
    lg = np.asarray(inp["hgrn_lb_logits"], dtype=np.float32).reshape(2, L, 8, 128)
    out["lbl"] = lg.transpose(3, 0, 2, 1).reshape(128, 16, L)
    small = np.zeros((L, 128, 32), np.float32)
    cw = np.asarray(inp["conv_w"], dtype=np.float32).reshape(L, 3, 4, 128)
    cb = np.asarray(inp["conv_b"], dtype=np.float32).reshape(L, 4, 128)
    cn = np.asarray(inp["conv_norm"], dtype=np.float32).reshape(L, 4, 128)
    for gi in range(4):
        for j in range(3):
            small[:, :, gi * 5 + j] = cw[:, j, gi, :]
        small[:, :, gi * 5 + 3] = cb[:, gi, :]
        small[:, :, gi * 5 + 4] = cn[:, gi, :]
    small[:, :, 20:28] = np.asarray(inp["hgrn_norm"], dtype=np.float32).reshape(L, 8, 128).transpose(0, 2, 1)
    small[:, :, 28:32] = np.asarray(inp["da_out_norm"], dtype=np.float32).reshape(L, 4, 128).transpose(0, 2, 1)
    out["small"] = small
    bcv = np.concatenate([np.asarray(inp[n], dtype=np.float32) for n in
                          ("da_q_norm", "da_k_norm", "da_lambda_q1", "da_lambda_k1", "da_lambda_q2", "da_lambda_k2")], axis=1)
    out["bc"] = np.broadcast_to(bcv[:, None, :], (L, 128, 384)).copy()
    for k_ in out:
        if out[k_].dtype == np.float32:
            out[k_] = f(out[k_])
    return out


CFG_FULL = {"stages": ("ffn1", "mix", "ffn2"), "nseq": 2}


def kernel(**inputs):
    x = np.asarray(inputs["x"], dtype=np.float32)
    shared = host_layouts(inputs)
    nc = build(CFG_FULL)
    in_maps = []
    for core in range(NCORES):
        xs = x[2 * core:2 * core + 2]
        xT = np.ascontiguousarray(xs.transpose(0, 2, 1)).reshape(2, 16, 128, S)
        pos = np.asarray(inputs["positions"])[2 * core:2 * core + 2].astype(np.int32)
        m = {"xT": xT, "pos": np.ascontiguousarray(pos.reshape(2, 16, 128).transpose(0, 2, 1))}
        m.update(shared)
        in_maps.append(m)
    res = run_bass_kernel_spmd(nc, in_maps, core_ids=list(range(NCORES)))
    out = np.empty((16, S, D), np.float32)
    for core in range(NCORES):
        yT = res.results[core]["yT"].reshape(2, D, S)
        out[2 * core:2 * core + 2] = yT.transpose(0, 2, 1)
    return out
```

```python
import math
from contextlib import ExitStack

import numpy as np
import concourse.bass as bass
import concourse.mybir as mybir
from concourse.bass_utils import run_bass_kernel_spmd

F32 = mybir.dt.float32
BF16 = mybir.dt.bfloat16
I32 = mybir.dt.int32
ALU = mybir.AluOpType
AF = mybir.ActivationFunctionType
AX = mybir.AxisListType

D = 2048
S = 2048
DFF = 5504
NFC = 43
DEPTH = 4
NCORES = 8
EPS = 1e-6
NBLK = 64

COMPUTE = ("pe", "act", "dve", "pool")
ALLENG = ("pe", "act", "dve", "pool", "sp")


class Prog:
    def __init__(self, nc, es):
        self.nc = nc
        self.es = es
        self.streams = {e: [] for e in ALLENG}
        self.sem = {e: es.enter_context(nc.semaphore("s_" + e)) for e in COMPUTE}
        self.cnt = {e: 0 for e in COMPUTE}
        self.dsem = {}
        self.dcnt = {}
        self.waited = {e: {} for e in ALLENG}
        self.lastw = {}
        self.readers = {}

    def _deps(self, eng, reads, writes, is_dma=False):
        toks = []
        for r in reads:
            for t in self.lastw.get(r, {}).values():
                toks.append((t, True))
        for w in writes:
            for t in self.lastw.get(w, {}).values():
                toks.append((t, False))
            rd = self.readers.get(w)
            if rd:
                for t in rd.values():
                    toks.append((t, False))
        need = {}
        for (kind, key, val), raw in toks:
            if not is_dma and kind == "e" and key == eng:
                if not raw or eng == "pe":
                    continue
            k = (kind, key)
            if self.waited[eng].get(k, 0) >= val:
                continue
            if need.get(k, 0) < val:
                need[k] = val
        for k, v in need.items():
            self.waited[eng][k] = v
        return [(k[0], k[1], v) for k, v in need.items()]

    def _record(self, tok, reads, writes):
        for w in writes:
            self.lastw.setdefault(w, {})[(tok[0], tok[1])] = tok
            self.readers[w] = {}
        for r in reads:
            self.readers.setdefault(r, {})[(tok[0], tok[1])] = tok

    def op(self, eng, fn, reads=(), writes=()):
        waits = self._deps(eng, reads, writes)
        self.cnt[eng] += 1
        tok = ("e", eng, self.cnt[eng])
        self._record(tok, reads, writes)
        self.streams[eng].append(("op", waits, fn, None))
        return tok

    def dma(self, eng, semname, fn, reads=(), writes=(), n=1):
        if semname not in self.dsem:
            self.dsem[semname] = self.es.enter_context(self.nc.semaphore("d_" + semname))
            self.dcnt[semname] = 0
        waits = self._deps(eng, reads, writes, is_dma=True)
        self.dcnt[semname] += 16 * n
        tok = ("d", semname, self.dcnt[semname])
        self._record(tok, reads, writes)
        self.streams[eng].append(("dma", waits, fn, semname))
        return tok

    def wait_all(self, eng):
        waits = []
        for e in COMPUTE:
            if self.cnt[e] > 0 and e != eng:
                waits.append(("e", e, self.cnt[e]))
        for s, v in self.dcnt.items():
            if v > 0:
                waits.append(("d", s, v))
        self.streams[eng].append(("wait", waits, None, None))

    def emit(self):
        prog = self

        def semh(kind, key):
            return prog.sem[key] if kind == "e" else prog.dsem[key]

        def run(engname, eobj):
            for kind, waits, fn, semname in prog.streams[engname]:
                for (k, key, v) in waits:
                    eobj.wait_ge(semh(k, key), v)
                if kind == "op":
                    fn(eobj).then_inc(prog.sem[engname], 1)
                elif kind == "dma":
                    inss = fn(eobj)
                    if not isinstance(inss, (list, tuple)):
                        inss = [inss]
                    for i in inss:
                        i.then_inc(prog.dsem[semname], 16)

        with self.nc.Block() as block:
            @block.tensor
            def _(e):
                run("pe", e)

            @block.scalar
            def _(e):
                run("act", e)

            @block.vector
            def _(e):
                run("dve", e)

            @block.gpsimd
            def _(e):
                run("pool", e)

            @block.sync
            def _(e):
                run("sp", e)


SCR_BYTES = 75 * 1024
PAGE = 1024
SMALL_START = 70 * 1024


class K:
    def __init__(self, nc, es, cfg):
        self.nc, self.es, self.cfg = nc, es, cfg
        L = DEPTH
        dt = nc.dram_tensor
        ein = dict(kind="ExternalInput")
        self.xT = dt("xT", [2, 16, 128, S], F32, **ein).ap()
        self.wgu = dt("wgu", [L, 2, NFC, 128, 4096], F32, **ein).ap()
        self.wd = dt("wd", [L, 2, 16 * 128 * 4, 1376], F32, **ein).ap()
        self.win = dt("win", [L, NBLK // 2, 128, 4096], F32, **ein).ap()
        self.wout = dt("wout", [L, 16 * 128 * 16 * 128 // 2048, 2048], F32, **ein).ap()
        self.gains = dt("gains", [L, 3, 128, 16], F32, **ein).ap()
        self.yT = dt("yT", [2, 16, 128, S], F32, kind="ExternalOutput").ap()
        self.ident_d = dt("ident", [128, 128], BF16, **ein).ap()
        self.mask_d = dt("masks", [2, 128, 128], BF16, **ein).ap()
        self.invf_d = dt("invf", [128, 32], F32, **ein).ap()
        self.lbl_d = dt("lbl", [128, 16, 4], F32, **ein).ap()
        self.pos_d = dt("pos", [2, 128, 16], I32, **ein).ap()
        self.small_d = dt("small", [L, 128, 32], F32, **ein).ap()
        self.bc_d = dt("bc", [L, 128, 384], F32, **ein).ap()
        self.cs_s = dt("cs_s", [2, 128, 512], F32).ap()
        self.y_s = [dt(f"y_s{c}", [128, S], BF16).ap() for c in range(16)]
        self.wgu_s = [[dt(f"wgu_s{l}_{f}", [NFC, 128, 4096], BF16).ap() for f in range(2)] for l in range(L)]
        self.wd_s = [[dt(f"wd_s{l}_{f}", [16 * 128 * 4, 1376], BF16).ap() for f in range(2)] for l in range(L)]
        self.win_s = [dt(f"win_s{l}", [NBLK // 2, 128, 4096], BF16).ap() for l in range(L)]
        self.wout_s = [dt(f"wout_s{l}", [16 * 128 * 16 * 128 // 2048, 2048], BF16).ap() for l in range(L)]

        sb = lambda name, shape, d: es.enter_context(nc.sbuf_tensor(name, shape, d))
        self.X = sb("X", [128, 16, S], F32)
        self.SCR = sb("SCR", [128, SCR_BYTES // 2], BF16)
        self.ones_bf = sb("ones_bf", [128, 128], BF16)
        self.gtile = sb("gtile", [128, 3, 16], F32)
        self.eps_t = sb("eps_t", [128, 1], F32)
        self.ident = sb("ident_sb", [128, 128], BF16)
        self.zeros_bf = sb("zeros_bf", [128, 128], BF16)
        self.one_f = sb("one_f", [128, 1], F32)
        self.mask_f = sb("mask_f", [128, 128], BF16)
        self.mask_b = sb("mask_b", [128, 128], BF16)
        self.invf = sb("invf_sb", [128, 32], F32)
        self.lbl = sb("lbl_sb", [128, 16, 4], F32)
        self.lbs = sb("lbs", [128, 16], F32)
        self.lb = sb("lb", [128, 16, 4], F32)
        self.oml = sb("oml", [128, 16, 4], F32)
        self.rcol = sb("rcol", [128, 16], F32)
        self.r2col = sb("r2col", [128, 16], F32)
        self.small = sb("small_sb", [128, 32], F32)
        self.bc = sb("bc_sb", [128, 384], F32)
        self.nlam = sb("nlam", [128, 1], F32)
        self.wcnt = 0
        self.ps = [es.enter_context(nc.psum_tensor(f"ps{i}", [128, 512], F32)) for i in range(8)]
        self.P = Prog(nc, es)
        self.Xhi = self.X[:].bitcast(BF16)

    def sv(self, off, nelem, dtype):
        nbytes = nelem * (4 if dtype == F32 else 2)
        assert off % 4 == 0 and off + nbytes <= SCR_BYTES, (off, nbytes)
        v = self.SCR[:, off // 2:(off + nbytes) // 2]
        if dtype == F32:
            v = v.bitcast(F32)
        keys = []
        b, end = off, off + nbytes
        while b < end:
            if b < SMALL_START:
                keys.append(("S", b // PAGE))
                b = (b // PAGE + 1) * PAGE
            else:
                keys.append(("s", b // 256))
                b = (b // 256 + 1) * 256
        return v, keys

    def xhi(self, c, t0, t1):
        return self.Xhi[:, c, 2 * t0 + 1:2 * t1:2]

    @staticmethod
    def xk(cs, tgs):
        return [("X", c, tg) for c in cs for tg in tgs]

    def setup(self):
        P = self.P
        P.op("pool", lambda e: e.memset(self.ones_bf[:], 1.0), writes=["ones_bf"])
        P.op("pool", lambda e: e.memset(self.eps_t[:], EPS), writes=["eps_t"])

    def prep(self, layers):
        P = self.P
        IN_SLOTS, OUT_SLOTS = 3, 3
        in_off = [i * 16384 for i in range(IN_SLOTS)]
        out_off = [IN_SLOTS * 16384 + i * 8192 for i in range(OUT_SLOTS)]
        cnt = 0
        for l in layers:
            P.dma("sp", "gl", lambda e, l=l: e.dma_start(out=self.gtile[:], in_=self.gains[l].rearrange("g p k -> p g k")),
                  writes=["gtile"])
            jobs = []
            for f in range(2):
                for c in range(NFC):
                    jobs.append((self.wgu[l, f, c], self.wgu_s[l][f][c], 0 if f == 0 else 2))
            for b in range(NBLK // 2):
                jobs.append((self.win[l, b], self.win_s[l][b], 1))
            for src, dst, gi in jobs:
                si, so = cnt % IN_SLOTS, cnt % OUT_SLOTS
                tin, kin = self.sv(in_off[si], 4096, F32)
                tout, kout = self.sv(out_off[so], 4096, BF16)
                P.dma("sp", f"pi{si}", lambda e, tin=tin, src=src: e.dma_start(out=tin, in_=src), writes=kin)
                eng = "dve" if cnt % 3 != 2 else "pool"
                g_b = self.gtile[:, gi, :].unsqueeze(1).unsqueeze(3).to_broadcast([128, 2, 16, 128])
                tin4 = tin.rearrange("p (j k c) -> p j k c", j=2, k=16)
                tout4 = tout.rearrange("p (j k c) -> p j k c", j=2, k=16)
                P.op(eng, lambda e, a=tout4, b=tin4, g=g_b: e.tensor_tensor(out=a, in0=b, in1=g, op=ALU.mult),
                     reads=kin + ["gtile"], writes=kout)
                P.dma("act", f"po{so}", lambda e, tout=tout, dst=dst: e.dma_start(out=dst, in_=tout), reads=kout,
                      writes=[("W", "gu_in", l)])
                cnt += 1
            for f in range(2):
                n_rows = self.wd.shape[2]
                step = n_rows // 8
                for i in range(8):
                    P.dma("pool", "cast", lambda e, l=l, f=f, i=i, step=step: e.dma_start(
                        out=self.wd_s[l][f][i * step:(i + 1) * step, :], in_=self.wd[l, f, i * step:(i + 1) * step, :]),
                        writes=[("W", "d", l)])
            n_rows = self.wout.shape[1]
            step = n_rows // 4
            for i in range(4):
                P.dma("pool", "cast", lambda e, l=l, i=i, step=step: e.dma_start(
                    out=self.wout_s[l][i * step:(i + 1) * step, :], in_=self.wout[l, i * step:(i + 1) * step, :]),
                    writes=[("W", "d", l)])

    def load_x(self, s):
        for c in range(16):
            self.P.dma("sp", "ldx", lambda e, c=c: e.dma_start(out=self.X[:, c, :], in_=self.xT[s, c]),
                       writes=self.xk([c], range(4)))

    def store_x(self, s):
        for c in range(16):
            self.P.dma("sp", "stx", lambda e, c=c: e.dma_start(out=self.yT[s, c], in_=self.X[:, c, :]),
                       reads=self.xk([c], range(4)))

    def rstd(self, t0, nt, r_view, r_keys, sq_off, scale_extra=None, rh_view=None, rh_keys=None, bank=6):
        P = self.P
        tgs = sorted({t // 512 for t in (t0, t0 + nt - 1)})
        for c in range(16):
            sq, ksq = self.sv(sq_off + (c % 2) * 1024, 512, BF16)
            P.op("act", lambda e, c=c, sq=sq: e.activation(out=sq[:, 0:nt], in_=self.X[:, c, t0:t0 + nt], func=AF.Square),
                 reads=self.xk([c], tgs), writes=ksq)
            P.op("pe", lambda e, c=c, sq=sq: e.matmul(self.ps[bank][:, 0:nt], lhsT=self.ones_bf[:], rhs=sq[:, 0:nt],
                                                     start=(c == 0), stop=(c == 15)),
                 reads=ksq + ["ones_bf"], writes=self.pk([bank]))
        P.op("act", lambda e: e.activation(out=r_view[:, 0:nt], in_=self.ps[bank][:, 0:nt], func=AF.Sqrt,
                                           bias=self.eps_t[:], scale=1.0 / D),
             reads=self.pk([bank]) + ["eps_t"], writes=r_keys)
        P.op("dve", lambda e: e.reciprocal(out=r_view[:, 0:nt], in_=r_view[:, 0:nt]), reads=r_keys, writes=r_keys)
        if rh_view is not None:
            P.op("pool", lambda e: e.tensor_scalar(out=rh_view[:, 0:nt], in0=r_view[:, 0:nt], scalar1=0.5, scalar2=None,
                                                   op0=ALU.mult), reads=r_keys, writes=rh_keys)

    def ffn(self, l, f):
        P = self.P
        ACT0 = 0
        WS = 44032
        WSZ = 22528
        RB = WS + WSZ
        RH = RB + 2048
        TA = RH + 2048
        TB = TA
        SQ = TA + 4096
        assert SQ + 2048 <= SCR_BYTES
        for tg in range(4):
            self._ffn_tg(l, f, tg, ACT0, WS, RB, RH, TA, TB, SQ)

    def _ffn_tg(self, l, f, tg, ACT0, WS, RB, RH, TA, TB, SQ):
        P = self.P
        if True:
            t0 = tg * 512
            rb, krb = self.sv(RB, 512, F32)
            rh, krh = self.sv(RH, 512, F32)
            self.rstd(t0, 512, rb, krb, SQ, rh_view=rh, rh_keys=krh)
            for c in range(NFC):
                slot = c % 2
                wt, kw = self.sv(WS + slot * 8192, 4096, BF16)
                wt4 = wt.rearrange("p (j k c) -> p j k c", j=2, k=16)
                P.dma("sp", f"wa{slot}", lambda e, wt=wt, c=c: e.dma_start(out=wt, in_=self.wgu_s[l][f][c]),
                      reads=[("W", "gu_in", l)], writes=kw)
                bg, bu = c % 2, 2 + c % 2

                def mm(e, j, bank, wt4=wt4):
                    ins = None
                    for kc in range(16):
                        ins = e.matmul(self.ps[bank][:], lhsT=wt4[:, j, kc, :], rhs=self.xhi(kc, t0, t0 + 512),
                                       start=(kc == 0), stop=(kc == 15))
                    return ins
                P.op("pe", lambda e, mm=mm, bg=bg: mm(e, 0, bg), reads=kw + self.xk(range(16), [tg]), writes=self.pk([bg]))
                P.op("pe", lambda e, mm=mm, bu=bu: mm(e, 1, bu), reads=kw + self.xk(range(16), [tg]), writes=self.pk([bu]))
                ta, kta = self.sv(TA + (c % 2) * 2048, 512, F32)
                P.op("dve", lambda e, ta=ta, bg=bg: e.tensor_tensor(out=ta, in0=self.ps[bg][:], in1=rb, op=ALU.mult),
                     reads=self.pk([bg]) + krb, writes=kta)
                P.op("act", lambda e, ta=ta: e.activation(out=ta, in_=ta, func=AF.Silu), reads=kta, writes=kta)
                av, kav = self.sv(ACT0 + c * 1024, 512, BF16)
                P.op("dve", lambda e, ta=ta, av=av, bu=bu: e.tensor_tensor(out=av, in0=self.ps[bu][:], in1=ta, op=ALU.mult),
                     reads=self.pk([bu]) + kta, writes=kav)
            actv, kact = self.sv(ACT0, NFC * 512, BF16)
            act3 = actv.rearrange("p (c t) -> p c t", c=NFC)
            rows_per_n = 512
            for n in range(16):
                slot = n % 2
                wt, kw = self.sv(WS + slot * 11264, NFC * 128, BF16)
                wt3 = wt.rearrange("p (c n) -> p c n", c=NFC)
                src = self.wd_s[l][f][n * rows_per_n:(n + 1) * rows_per_n, :].rearrange("(p a) b -> p (a b)", p=128)
                P.dma("sp", f"wb{slot}", lambda e, wt=wt, src=src: e.dma_start(out=wt, in_=src),
                      reads=[("W", "d", l)], writes=kw)
                bd = 4 + n % 2

                def mmd(e, bd=bd, wt3=wt3):
                    ins = None
                    for fc in range(NFC):
                        ins = e.matmul(self.ps[bd][:], lhsT=wt3[:, fc, :], rhs=act3[:, fc, :],
                                       start=(fc == 0), stop=(fc == NFC - 1))
                    return ins
                P.op("pe", mmd, reads=kw + kact, writes=self.pk([bd]))
                tb, ktb = self.sv(TB + (n % 2) * 2048, 512, F32)
                P.op("dve", lambda e, tb=tb, bd=bd: e.tensor_tensor(out=tb, in0=self.ps[bd][:], in1=rh, op=ALU.mult),
                     reads=self.pk([bd]) + krh, writes=ktb)
                xv = self.X[:, n, t0:t0 + 512]
                P.op("pool", lambda e, tb=tb, xv=xv: e.tensor_tensor(out=xv, in0=xv, in1=tb, op=ALU.add),
                     reads=ktb + self.xk([n], [tg]), writes=self.xk([n], [tg]))

    def E(self, eng, meth, reads, writes, **kw):
        return self.P.op(eng, lambda e: getattr(e, meth)(**kw), reads=list(reads), writes=list(writes))

    def MM(self, reads, writes, mms):
        def fn(e):
            ins = None
            for (o, l_, r_, st, sp) in mms:
                ins = e.matmul(o, lhsT=l_, rhs=r_, start=st, stop=sp)
            return ins
        return self.P.op("pe", fn, reads=list(reads), writes=list(writes))

    def TR(self, reads, writes, trs):
        def fn(e):
            ins = None
            for (o, i_) in trs:
                ins = e.transpose(o, i_, self.ident[:])
            return ins
        return self.P.op("pe", fn, reads=list(reads) + ["ident"], writes=list(writes))

    def DMA(self, eng, sem, out, in_, reads, writes):
        return self.P.dma(eng, sem, lambda e: e.dma_start(out=out, in_=in_), reads=list(reads), writes=list(writes))

    def pk(self, banks):
        return [("ps", b) for b in banks]

    def pq(self, b, q):
        return [("ps", b)]

    M_RB, M_WS, M_T2, M_T3, M_T1 = 0, 8192, 16384, 24576, 32768
    M_B1, M_B2, M_B3, M_B4, M_B5, M_B6, M_MSK, M_SM = 41984, 46080, 50176, 55296, 59392, 63488, 67584, 71680

    def wblk(self, l, blk):
        return self.win_s[l][blk // 2][:, (blk % 2) * 2048:(blk % 2 + 1) * 2048]

    def load_w(self, l, blk):
        slot = self.wcnt % 2
        self.wcnt += 1
        wt, kw = self.sv(self.M_WS + slot * 4096, 2048, BF16)
        self.DMA("sp", f"mw{slot}", wt, self.wblk(l, blk), [("W", "gu_in", l)], kw)
        return wt.rearrange("p (k c) -> p k c", k=16), kw

    def proj_fm(self, l, blk, b0):
        wt3, kw = self.load_w(l, blk)
        for tg in range(4):
            self.MM(kw + self.xk(range(16), [tg]), self.pk([b0 + tg]),
                    [(self.ps[b0 + tg][:], wt3[:, kc, :], self.xhi(kc, tg * 512, tg * 512 + 512), kc == 0, kc == 15)
                     for kc in range(16)])

    def proj_tm(self, l, blk, b0):
        wt3, kw = self.load_w(l, blk)
        for i in range(16):
            o = self.ps[b0 + i // 4][:, (i % 4) * 128:(i % 4 + 1) * 128]
            self.MM(kw + self.xk(range(16), [i // 4]), self.pk([b0 + i // 4]),
                    [(o, self.xhi(kc, i * 128, i * 128 + 128), wt3[:, kc, :], kc == 0, kc == 15) for kc in range(16)])

    def mixer_setup(self):
        P = self.P
        self.E("pool", "memset", [], ["zeros_bf"], ap=self.zeros_bf[:], constant=0.0)
        self.E("pool", "memset", [], ["one_f"], ap=self.one_f[:], constant=1.0)
        self.DMA("sp", "cst", self.ident[:], self.ident_d[:, :], [], ["ident"])
        self.DMA("sp", "cst", self.mask_f[:], self.mask_d[0], [], ["mask_f"])
        self.DMA("sp", "cst", self.mask_b[:], self.mask_d[1], [], ["mask_b"])
        self.DMA("sp", "cst", self.invf[:], self.invf_d[:, :], [], ["invf"])
        lbl = self.lbl
        self.DMA("sp", "cst", lbl[:], self.lbl_d[:, :, :], [], ["lbl"])
        self.E("act", "activation", ["lbl"], ["lbl"], out=lbl[:], in_=lbl[:], func=AF.Exp)
        self.E("dve", "tensor_reduce", ["lbl"], ["lbs"], out=self.lbs[:], in_=lbl[:], axis=AX.X, op=ALU.add)
        self.E("dve", "reciprocal", ["lbs"], ["lbs"], out=self.lbs[:], in_=self.lbs[:])
        self.E("dve", "tensor_tensor", ["lbl", "lbs"], ["lbl"], out=lbl[:], in0=lbl[:],
               in1=self.lbs[:].unsqueeze(2).to_broadcast([128, 16, 4]), op=ALU.mult)
        self.E("pool", "memset", [], ["lb"], ap=self.lb[:, :, 0:1], constant=0.0)
        for j in range(1, 4):
            self.E("dve", "tensor_tensor", ["lbl", "lb"], ["lb"], out=self.lb[:, :, j:j + 1], in0=self.lb[:, :, j - 1:j],
                   in1=lbl[:, :, j:j + 1], op=ALU.add)
        self.E("dve", "tensor_scalar", ["lb"], ["oml"], out=self.oml[:], in0=self.lb[:], scalar1=-1.0, scalar2=1.0,
               op0=ALU.mult, op1=ALU.add)

    def cossin(self, s):
        PI = math.pi
        ang, ka = self.sv(self.M_T1, 512, F32)
        tmp, kt = self.sv(self.M_T2, 512, F32)
        ki_, kki = self.sv(self.M_T3, 512, F32)
        pi_t, kpi = self.sv(self.M_T3 + 4096, 512, F32)
        kint = pi_t.bitcast(I32)
        posf, kpf = self.sv(self.M_B1, 16, F32)
        posi = posf.bitcast(I32)
        self.DMA("sp", "cst", posi, self.pos_d[s], [], kpf)
        self.E("dve", "tensor_copy", kpf, kpf, out=posf, in_=posi)
        ang3 = ang.rearrange("p (i j) -> p i j", i=16)
        self.E("dve", "tensor_tensor", kpf + ["invf"], ka, out=ang3, in0=posf.unsqueeze(2).to_broadcast([128, 16, 32]),
               in1=self.invf[:].unsqueeze(1).to_broadcast([128, 16, 32]), op=ALU.mult)
        for which, shift in ((0, 0.0), (1, PI / 2)):
            self.E("dve", "tensor_scalar", ka, kt, out=tmp, in0=ang, scalar1=shift, scalar2=1.0 / (2 * PI), op0=ALU.add,
                   op1=ALU.mult)
            self.E("dve", "tensor_copy", kt, kpi, out=kint, in_=tmp)
            self.E("dve", "tensor_copy", kpi, kki, out=ki_, in_=kint)
            self.E("dve", "scalar_tensor_tensor", kki + ka, kt, out=tmp, in0=ki_, scalar=-2 * PI, in1=ang, op0=ALU.mult,
                   op1=ALU.add)
            if shift != 0.0:
                self.E("dve", "tensor_scalar", kt, kt, out=tmp, in0=tmp, scalar1=shift, scalar2=None, op0=ALU.add)
            self.E("dve", "tensor_scalar", kt, kki, out=ki_, in0=tmp, scalar1=PI, scalar2=-2 * PI, op0=ALU.is_gt, op1=ALU.mult)
            self.E("dve", "tensor_tensor", kt + kki, kt, out=tmp, in0=tmp, in1=ki_, op=ALU.add)
            self.E("dve", "tensor_scalar", kt, kki, out=ki_, in0=tmp, scalar1=-PI, scalar2=2 * PI, op0=ALU.is_lt, op1=ALU.mult)
            self.E("dve", "tensor_tensor", kt + kki, kt, out=tmp, in0=tmp, in1=ki_, op=ALU.add)
            self.E("dve", "tensor_scalar", kt, kt, out=tmp, in0=tmp, scalar1=PI, scalar2=-PI, op0=ALU.min, op1=ALU.max)
            self.E("act", "activation", kt, kki, out=ki_, in_=tmp, func=AF.Sin)
            self.DMA("sp", "cst", self.cs_s[1 - which], ki_, kki, [("cs",)])

    def mixer(self, l, s):
        P = self.P
        self.wcnt = 0
        rb, krb = self.sv(self.M_RB, 2048, F32)
        for tg in range(4):
            v, kv = self.sv(self.M_RB + tg * 2048, 512, F32)
            self.rstd(tg * 512, 512, v, kv, self.M_SM)
        for i in range(16):
            self.MM(krb + ["one_f"], self.pk([7]), [(self.ps[7][:, i:i + 1], rb[0:1, i * 128:(i + 1) * 128], self.one_f[0:1, 0:1], True, True)])
        self.E("dve", "tensor_copy", self.pk([7]), ["rcol"], out=self.rcol[:], in_=self.ps[7][:, 0:16])
        self.E("dve", "tensor_tensor", ["rcol"], ["r2col"], out=self.r2col[:], in0=self.rcol[:], in1=self.rcol[:], op=ALU.mult)
        self.DMA("sp", "cst", self.small[:], self.small_d[l], [], ["small"])
        self.DMA("sp", "cst", self.bc[:], self.bc_d[l], [], ["bc"])
        parts = self.cfg.get("mix", ("conv", "attn", "hgrn"))
        if "conv" in parts:
            for gi in range(4):
                self.conv_group(l, gi, rb, krb)
        if "attn" in parts:
            self.attn_setup(l)
            for a in range(4):
                self.attn_head(l, a, rb, krb)
        if "hgrn" in parts:
            self.hgrn_setup()
            for h in range(8):
                self.hgrn_head(l, h, rb, krb)
        ych = []
        if "hgrn" in parts:
            ych += list(range(0, 8))
        if "attn" in parts:
            ych += list(range(8, 12))
        if "conv" in parts:
            ych += list(range(12, 16))
        self.wout_stage(l, ych)

    def conv_group(self, l, gi, rb, krb):
        T1, k1 = self.sv(self.M_T1, 2050, F32)
        T2, k2 = self.sv(self.M_T2, 2048, F32)
        T3, k3 = self.sv(self.M_T3, 2048, F32)
        SQ, ksq = self.sv(self.M_B6, 2048, BF16)
        YB, kyb = self.sv(self.M_B5, 2048, BF16)
        sm = self.small
        c0 = gi * 5
        self.E("pool", "memset", [], k1, ap=T1[:, 0:1], constant=0.0)
        self.E("pool", "memset", [], k1, ap=T1[:, 2049:2050], constant=0.0)
        self.proj_fm(l, 56 + gi, 0)
        for tg in range(4):
            self.E("dve", "tensor_tensor", self.pk([tg]) + krb, k1, out=T1[:, 1 + tg * 512:1 + tg * 512 + 512],
                   in0=self.ps[tg][:], in1=rb[:, tg * 512:(tg + 1) * 512], op=ALU.mult)
        self.proj_fm(l, 60 + gi, 4)
        for tg in range(4):
            self.E("dve", "tensor_tensor", self.pk([4 + tg]) + krb, k2, out=T2[:, tg * 512:(tg + 1) * 512],
                   in0=self.ps[4 + tg][:], in1=rb[:, tg * 512:(tg + 1) * 512], op=ALU.mult)
        self.E("pool", "tensor_tensor", k1 + k2, k1, out=T1[:, 1:2049], in0=T1[:, 1:2049], in1=T2, op=ALU.mult)
        self.E("dve", "tensor_scalar", k1 + ["small"], k2, out=T2, in0=T1[:, 1:2049], scalar1=sm[:, c0 + 1:c0 + 2],
               scalar2=sm[:, c0 + 3:c0 + 4], op0=ALU.mult, op1=ALU.add)
        self.E("dve", "scalar_tensor_tensor", k1 + k2 + ["small"], k2, out=T2, in0=T1[:, 0:2048], scalar=sm[:, c0:c0 + 1],
               in1=T2, op0=ALU.mult, op1=ALU.add)
        self.E("dve", "scalar_tensor_tensor", k1 + k2 + ["small"], k2, out=T2, in0=T1[:, 2:2050], scalar=sm[:, c0 + 2:c0 + 3],
               in1=T2, op0=ALU.mult, op1=ALU.add)
        self.proj_fm(l, 52 + gi, 0)
        for tg in range(4):
            self.E("dve", "tensor_tensor", self.pk([tg]) + krb, k3, out=T3[:, tg * 512:(tg + 1) * 512],
                   in0=self.ps[tg][:], in1=rb[:, tg * 512:(tg + 1) * 512], op=ALU.mult)
        self.E("pool", "tensor_tensor", k2 + k3, k2, out=T2, in0=T2, in1=T3, op=ALU.mult)
        self.pnorm_store(T2, k2, T3, k3, SQ, ksq, YB, kyb, 4, sm[:, c0 + 4:c0 + 5], None, None, 12 + gi, 1.0)

    def pnorm_store(self, Tin, kin, Ttmp, ktmp, SQ, ksq, YB, kyb, b0, gain_col, mul_t, kmul, ychunk, in_is_psum_banks=None):
        self.E("act", "activation", kin, ksq, out=SQ, in_=Tin, func=AF.Square)
        for tg in range(4):
            self.MM(ksq + ["ones_bf"], self.pk([b0 + tg]),
                    [(self.ps[b0 + tg][:], self.ones_bf[:], SQ[:, tg * 512:(tg + 1) * 512], True, True)])
            self.E("act", "activation", self.pk([b0 + tg]) + ["eps_t"], ktmp, out=Ttmp[:, tg * 512:(tg + 1) * 512],
                   in_=self.ps[b0 + tg][:], func=AF.Sqrt, bias=self.eps_t[:], scale=1.0 / 128)
        self.E("dve", "reciprocal", ktmp, ktmp, out=Ttmp, in_=Ttmp)
        if mul_t is None:
            self.E("dve", "scalar_tensor_tensor", kin + ktmp + ["small"], kyb, out=YB, in0=Tin, scalar=gain_col, in1=Ttmp,
                   op0=ALU.mult, op1=ALU.mult)
        else:
            self.E("dve", "scalar_tensor_tensor", kin + ktmp + ["small"], ktmp, out=Ttmp, in0=Tin, scalar=gain_col, in1=Ttmp,
                   op0=ALU.mult, op1=ALU.mult)
            self.E("dve", "tensor_tensor", ktmp + kmul, kyb, out=YB, in0=Ttmp, in1=mul_t, op=ALU.mult)
        self.DMA("sp", "ysto", self.y_s[ychunk], YB, kyb, [("y", ychunk)])

    def wout_stage(self, l, ych):
        for tg in range(4):
            yt, ky = self.sv(self.M_T2, 16 * 512, BF16)
            yt3 = yt.rearrange("p (c t) -> p c t", c=16)
            for c in ych:
                self.DMA("sp", "yld", yt3[:, c, :], self.y_s[c][:, tg * 512:(tg + 1) * 512], [("y", c)], ky)
            for n in range(16):
                slot = self.wcnt % 2
                self.wcnt += 1
                wt, kw = self.sv(self.M_WS + slot * 4096, 2048, BF16)
                self.DMA("sp", f"mw{slot}", wt, self.wout_s[l][n * 128:(n + 1) * 128, :], [("W", "d", l)], kw)
                wt3 = wt.rearrange("p (k c) -> p k c", k=16)
                b = n % 2
                self.MM(kw + ky, self.pk([b]), [(self.ps[b][:], wt3[:, c, :], yt3[:, c, :], j == 0, j == len(ych) - 1)
                                                for j, c in enumerate(ych)])
                xv = self.X[:, n, tg * 512:(tg + 1) * 512]
                self.E("dve", "tensor_tensor", self.pk([b]) + self.xk([n], [tg]), self.xk([n], [tg]), out=xv, in0=self.ps[b][:],
                       in1=xv, op=ALU.add)

    def attn_setup(self, l):
        bc3 = self.bc[:].rearrange("p (g d) -> p g d", g=6)
        lt, kl = self.sv(self.M_SM, 128, F32)
        lt3 = lt.rearrange("p (g d) -> p g d", g=2)
        ls, kls = self.sv(self.M_SM + 512, 2, F32)
        self.E("dve", "tensor_tensor", ["bc"], kl, out=lt3, in0=bc3[:, 2:6:2, :], in1=bc3[:, 3:6:2, :], op=ALU.mult)
        self.E("dve", "tensor_reduce", kl, kls, out=ls, in_=lt3, axis=AX.X, op=ALU.add)
        self.E("act", "activation", kls, kls, out=ls, in_=ls, func=AF.Exp)
        lam_init = 0.8 - 0.6 * math.exp(-0.3 * l)
        self.E("dve", "tensor_tensor", kls, ["nlam"], out=self.nlam[:], in0=ls[:, 1:2], in1=ls[:, 0:1], op=ALU.subtract)
        self.E("dve", "tensor_scalar", ["nlam"], ["nlam"], out=self.nlam[:], in0=self.nlam[:], scalar1=-lam_init, scalar2=None,
               op0=ALU.add)
        self.lam_init = lam_init

    def attn_head(self, l, a, rb, krb):
        sv = self.sv
        COS, kcos = sv(self.M_T2, 512, F32)
        SIN, ksin = sv(self.M_T2 + 2048, 512, F32)
        self.DMA("sp", "cst", COS, self.cs_s[0], [("cs",)], kcos)
        self.DMA("sp", "cst", SIN, self.cs_s[1], [("cs",)], ksin)
        cos4 = COS.rearrange("p (i d) -> p i d", i=16).unsqueeze(2).to_broadcast([128, 16, 2, 32])
        sin4 = SIN.rearrange("p (i d) -> p i d", i=16).unsqueeze(2).to_broadcast([128, 16, 2, 32])
        QT, kqt = sv(self.M_B1, 2048, BF16)
        KT, kkt = sv(self.M_B2, 2048, BF16)
        VA, kva = sv(self.M_B3, 16 * 130, BF16)
        VA3 = VA.rearrange("p (i c) -> p i c", i=16)
        QN, kqn = sv(self.M_B4, 2048, BF16)
        QN4 = QN.rearrange("p (i m d) -> p i m d", i=16, m=2)
        QN3 = QN.rearrange("p (i c) -> p i c", i=16)
        T1, k1 = sv(self.M_T1, 2048, F32)
        T1v = T1.rearrange("p (i m d) -> p i m d", i=16, m=2)
        RT, krt = sv(self.M_B5, 2048, F32)
        RT5 = RT.rearrange("p (h i m d) -> p h i m d", h=2, i=16, m=2)
        SS, kss = sv(self.M_SM + 1024, 32, F32)
        SS3 = SS.rearrange("p (i m) -> p i m", i=16)
        bc3 = self.bc[:].rearrange("p (g d) -> p g d", g=6)
        for which, (blk, dst, kdst) in enumerate(((40 + a, QT, kqt), (44 + a, KT, kkt))):
            self.proj_tm(l, blk, 0)
            for b in range(4):
                self.E("act", "activation", self.pk([b]), k1, out=T1[:, b * 512:(b + 1) * 512], in_=self.ps[b][:], func=AF.Square)
            self.E("dve", "tensor_reduce", k1, kss, out=SS, in_=T1.rearrange("p (g d) -> p g d", d=64), axis=AX.X, op=ALU.add)
            self.E("dve", "tensor_tensor", kss + ["r2col"], kss, out=SS3, in0=SS3,
                   in1=self.r2col[:].unsqueeze(2).to_broadcast([128, 16, 2]), op=ALU.mult)
            self.E("act", "activation", kss + ["eps_t"], kss, out=SS, in_=SS, func=AF.Sqrt, bias=self.eps_t[:], scale=1.0 / 64)
            self.E("dve", "reciprocal", kss, kss, out=SS, in_=SS)
            self.E("dve", "tensor_tensor", kss + ["rcol"], kss, out=SS3, in0=SS3,
                   in1=self.rcol[:].unsqueeze(2).to_broadcast([128, 16, 2]), op=ALU.mult)
            if which == 0:
                self.E("dve", "tensor_scalar", kss, kss, out=SS, in0=SS, scalar1=0.125, scalar2=None, op0=ALU.mult)
            for b in range(4):
                self.E("dve", "tensor_tensor", self.pk([b]) + kss, k1, out=T1v[:, 4 * b:4 * b + 4, :, :],
                       in0=self.ps[b][:].rearrange("p (i m d) -> p i m d", i=4, m=2),
                       in1=SS3[:, 4 * b:4 * b + 4, :].unsqueeze(3).to_broadcast([128, 4, 2, 64]), op=ALU.mult)
            self.E("dve", "tensor_tensor", k1 + ["bc"], k1, out=T1.rearrange("p (g d) -> p g d", d=64),
                   in0=T1.rearrange("p (g d) -> p g d", d=64),
                   in1=bc3[:, which, :].unsqueeze(1).to_broadcast([128, 32, 64]), op=ALU.mult)
            t1 = T1v[:, :, :, 0:32]
            t2 = T1v[:, :, :, 32:64]
            self.E("dve", "tensor_tensor", k1 + kcos, krt, out=RT5[:, 0], in0=t1, in1=cos4, op=ALU.mult)
            self.E("pool", "tensor_tensor", k1 + ksin, krt, out=RT5[:, 1], in0=t2, in1=sin4, op=ALU.mult)
            self.E("dve", "tensor_tensor", krt, kqn, out=QN4[:, :, :, 0:32], in0=RT5[:, 0], in1=RT5[:, 1], op=ALU.subtract)
            self.E("dve", "tensor_tensor", k1 + kcos, krt, out=RT5[:, 0], in0=t2, in1=cos4, op=ALU.mult)
            self.E("pool", "tensor_tensor", k1 + ksin, krt, out=RT5[:, 1], in0=t1, in1=sin4, op=ALU.mult)
            self.E("dve", "tensor_tensor", krt, kqn, out=QN4[:, :, :, 32:64], in0=RT5[:, 0], in1=RT5[:, 1], op=ALU.add)
            for hb in range(2):
                pT = self.ps[4 + hb][:].bitcast(BF16)
                self.TR(kqn, self.pk([4 + hb]), [(pT[:, j * 128:(j + 1) * 128], QN3[:, hb * 8 + j, :]) for j in range(8)])
                self.E("act", "activation", self.pk([4 + hb]), kdst, out=dst[:, hb * 1024:(hb + 1) * 1024], in_=pT, func=AF.Copy)
        self.proj_tm(l, 48 + a, 0)
        for b in range(4):
            self.E("dve", "tensor_tensor", self.pk([b]) + ["rcol"], kva, out=VA3[:, 4 * b:4 * b + 4, 0:128],
                   in0=self.ps[b][:].rearrange("p (i c) -> p i c", i=4),
                   in1=self.rcol[:, 4 * b:4 * b + 4].unsqueeze(2).to_broadcast([128, 4, 128]), op=ALU.mult)
        self.E("pool", "memset", [], kva, ap=VA3[:, :, 128:129], constant=1.0)
        PT, kpt = sv(self.M_T2, 16 * 512, BF16)
        PT3 = PT.rearrange("p (k q) -> p k q", k=16)
        ON, kon = sv(self.M_MSK, 1024, F32)
        ON4 = ON.rearrange("p (m q d) -> p m q d", m=2, q=4)
        YB, kyb = sv(self.M_B5, 2048, BF16)
        OB, kob = sv(self.M_B4, 512, BF16)
        OB3 = OB.rearrange("p (q d) -> p q d", q=4)
        RS, krs = sv(self.M_SM + 1280, 1, F32)
        S4, ks4 = sv(self.M_SM + 1536, 4, F32)
        AVB = [0, 1, 6, 7]
        kptk = [sv(self.M_T2 + kt * 1024, 512, BF16)[1] for kt in range(16)]
        groups = [(qg, m) for qg in range(4) for m in range(2)]

        def combine(qg):
            self.E("dve", "scalar_tensor_tensor", kon + ["nlam"], kon, out=ON4[:, 0], in0=ON4[:, 1], scalar=self.nlam[:],
                   in1=ON4[:, 0], op0=ALU.mult, op1=ALU.add)
            self.E("dve", "tensor_tensor", kon, kon, out=ON4[:, 1], in0=ON4[:, 0], in1=ON4[:, 0], op=ALU.mult)
            self.E("dve", "tensor_reduce", kon, ks4, out=S4, in_=ON4[:, 1], axis=AX.X, op=ALU.add)
            self.E("act", "activation", ks4 + ["eps_t"], ks4, out=S4, in_=S4, func=AF.Sqrt, bias=self.eps_t[:], scale=1.0 / 128)
            self.E("dve", "reciprocal", ks4, ks4, out=S4, in_=S4)
            self.E("dve", "tensor_scalar", ks4, ks4, out=S4, in0=S4, scalar1=1.0 - self.lam_init, scalar2=None, op0=ALU.mult)
            self.E("dve", "tensor_tensor", kon + ks4, kob, out=OB3, in0=ON4[:, 0], in1=S4.unsqueeze(2).to_broadcast([128, 4, 128]),
                   op=ALU.mult)
            pT = self.ps[4][:].bitcast(BF16)
            self.TR(kob, self.pk([4]), [(pT[:, j * 128:(j + 1) * 128], OB3[:, j, :]) for j in range(4)])
            self.E("act", "activation", self.pk([4]) + ["small"], kyb, out=YB[:, qg * 512:(qg + 1) * 512], in_=pT[:, 0:512],
                   func=AF.Copy, scale=self.small[:, 28 + a:29 + a])

        for gi in range(len(groups) + 1):
            for kt in range(16):
                if gi < len(groups):
                    qg, m = groups[gi]
                    b = 2 + (kt % 2)
                    self.MM(kkt + kqt, self.pk([b]), [(self.ps[b][:], KT[m * 64:(m + 1) * 64, kt * 128:(kt + 1) * 128],
                                                      QT[m * 64:(m + 1) * 64, qg * 512:(qg + 1) * 512], True, True)])
                if gi >= 1:
                    for qt in range(4):
                        bo = AVB[qt]
                        self.MM(kptk[kt] + kva, self.pk([bo]), [(self.ps[bo][:, 0:129], PT3[:, kt, qt * 128:(qt + 1) * 128],
                                                                VA3[:, kt, 0:129], kt == 0, kt == 15)])
                if gi < len(groups):
                    self.E("act", "activation", self.pk([b]), kptk[kt], out=PT3[:, kt, :], in_=self.ps[b][:], func=AF.Exp)
            if gi >= 1:
                qg_p, m_p = groups[gi - 1]
                for qt in range(4):
                    bo = AVB[qt]
                    self.E("dve", "reciprocal", self.pk([bo]), krs, out=RS, in_=self.ps[bo][:, 128:129])
                    self.E("dve", "tensor_scalar", self.pk([bo]) + krs, kon, out=ON4[:, m_p, qt, :], in0=self.ps[bo][:, 0:128],
                           scalar1=RS, scalar2=None, op0=ALU.mult)
                if m_p == 1:
                    combine(qg_p)
        self.DMA("sp", "ysto", self.y_s[8 + a], YB, kyb, [("y", 8 + a)])

    def hgrn_setup(self):
        MSK, kmsk = self.sv(self.M_MSK, 2048, BF16)
        self.E("pool", "memset", [], kmsk, ap=MSK, constant=1.0)
        self.E("pool", "memset", [], kmsk, ap=MSK.rearrange("p (c k) -> p c k", k=64)[:, :, 0:1], constant=0.0)

    def _hgrn_pre(self, l, h, rb, krb, stage, only=None):
        QS, kqs = self.sv(self.M_B1, 2048, BF16)
        VT, kvt = self.sv(self.M_B3, 2048, BF16)
        VT3 = VT.rearrange("p (i c) -> p i c", i=16)
        if stage == 0:
            self.proj_fm(l, h, 0)
        elif stage == 1:
            for tg in (range(4) if only is None else [only]):
                self.E("dve", "tensor_tensor", self.pk([tg]) + krb, kqs, out=QS[:, tg * 512:(tg + 1) * 512], in0=self.ps[tg][:],
                       in1=rb[:, tg * 512:(tg + 1) * 512], op=ALU.mult)
        elif stage == 2:
            self.proj_tm(l, 8 + h, 0)
        else:
            for b in range(4):
                self.E("dve", "tensor_tensor", self.pk([b]) + ["rcol"], kvt, out=VT3[:, 4 * b:4 * b + 4, :],
                       in0=self.ps[b][:].rearrange("p (i c) -> p i c", i=4),
                       in1=self.rcol[:, 4 * b:4 * b + 4].unsqueeze(2).to_broadcast([128, 4, 128]), op=ALU.mult)

    def hgrn_head(self, l, h, rb, krb):
        sv = self.sv
        MSK, kmsk = sv(self.M_MSK, 2048, BF16)
        T1, k1 = sv(self.M_T1, 2048, F32)
        T2, k2 = sv(self.M_T2, 2048, F32)
        T3, k3 = sv(self.M_T3, 2048, F32)
        QS, kqs = sv(self.M_B1, 2048, BF16)
        GS, kgs = sv(self.M_B2, 2048, BF16)
        VT, kvt = sv(self.M_B3, 2048, BF16)
        VT3 = VT.rearrange("p (i c) -> p i c", i=16)
        QP, kqp = sv(self.M_B4, 2048, BF16)
        KP, kkp = sv(self.M_B5, 2048, BF16)
        KPT, kkpt = sv(self.M_B6, 2048, BF16)
        KPT3 = KPT.rearrange("p (i c) -> p i c", i=16)
        sm0 = self.M_SM
        Sst = [sv(sm0 + j * 512, 128, F32) for j in range(2)]
        Sp = [sv(sm0 + 1024 + j * 256, 128, BF16) for j in range(4)]
        MS = [sv(sm0 + 2048 + j * 256, 128, BF16) for j in range(4)]
        MC, kmc = sv(sm0 + 3072, 32, F32)
        BC, kbc = sv(sm0 + 3328, 32, F32)
        EM, kem = sv(sm0 + 3584, 32, F32)
        EB, keb = sv(sm0 + 3840, 32, F32)
        EBM, kebm = sv(sm0 + 4096, 32, F32)
        T3v = T3.rearrange("p (c k) -> p c k", k=64)
        if h == 0:
            for stage in range(4):
                self._hgrn_pre(l, 0, rb, krb, stage)
        self.proj_fm(l, 32 + h, 0)
        for tg in range(4):
            self.E("dve", "tensor_tensor", self.pk([tg]) + krb, k1, out=T1[:, tg * 512:(tg + 1) * 512], in0=self.ps[tg][:],
                   in1=rb[:, tg * 512:(tg + 1) * 512], op=ALU.mult)
        self.E("act", "activation", k1, kgs, out=GS, in_=T1, func=AF.Silu)
        for b in range(4, 8):
            self.MM(kqs + ["zeros_bf"], self.pk([b]), [(self.ps[b][:], self.zeros_bf[:], QS[:, 0:512], True, True)])
        lbi = lambda d: self.lb[:, d * 8 + h, l:l + 1]
        omli = lambda d: self.oml[:, d * 8 + h, l:l + 1]
        def front(dd, stage, only=None):
            if stage == 0:
                self.proj_fm(l, 16 + 8 * dd + h, 0)
            elif stage == 1:
                for tg in (range(4) if only is None else [only]):
                    self.E("dve", "tensor_tensor", self.pk([tg]) + krb, k1, out=T1[:, tg * 512:(tg + 1) * 512], in0=self.ps[tg][:],
                           in1=rb[:, tg * 512:(tg + 1) * 512], op=ALU.mult)
            else:
                self.E("act", "activation", k1, k1, out=T1, in_=T1, func=AF.Sigmoid)
                self.E("dve", "tensor_scalar", k1 + ["lb", "oml"], k1, out=T1, in0=T1, scalar1=omli(dd), scalar2=lbi(dd), op0=ALU.mult,
                       op1=ALU.add)

        for d in range(2):
            if d == 0:
                front(0, 0)
                front(0, 1)
                front(0, 2)
            self.E("act", "activation", k1, k2, out=T2, in_=T1, func=AF.Ln)
            self.E("dve", "tensor_scalar", k1, k1, out=T1, in0=T1, scalar1=-1.0, scalar2=1.0, op0=ALU.mult, op1=ALU.add)
            self.E("dve", "tensor_tensor_scan", k2 + kmsk, k3, out=T3, data0=MSK, data1=T2, initial=0.0, op0=ALU.mult,
                   op1=ALU.add)
            if d == 0:
                self.E("dve", "tensor_copy", k3, kmc, out=MC, in_=T3v[:, :, 31])
                self.E("dve", "tensor_copy", k3, kbc, out=BC, in_=T3v[:, :, 63])
            else:
                self.E("dve", "tensor_copy", k3, kbc, out=BC, in_=T3v[:, :, 63])
                self.E("dve", "tensor_tensor", k2 + k3, k3, out=T3, in0=T2, in1=T3, op=ALU.subtract)
                self.E("dve", "tensor_tensor", k3 + kbc, k3, out=T3v, in0=T3v, in1=BC.unsqueeze(2).to_broadcast([128, 32, 64]),
                       op=ALU.add)
                self.E("dve", "tensor_copy", k3, kmc, out=MC, in_=T3v[:, :, 32])
            self.E("dve", "tensor_tensor", k3 + kmc, k3, out=T3v, in0=T3v, in1=MC.unsqueeze(2).to_broadcast([128, 32, 64]),
                   op=ALU.subtract)
            self.E("act", "activation", k3, k2, out=T2, in_=T3, func=AF.Exp)
            self.E("dve", "tensor_tensor", k2 + kqs, kqp, out=QP, in0=T2, in1=QS, op=ALU.mult)
            self.E("act", "activation", k3, k2, out=T2, in_=T3, func=AF.Exp, scale=-1.0)
            self.E("dve", "tensor_tensor", k2 + k1, kkp, out=KP, in0=T2, in1=T1, op=ALU.mult)
            self.E("act", "activation", kmc, kem, out=EM, in_=MC, func=AF.Exp)
            self.E("act", "activation", kbc, keb, out=EB, in_=BC, func=AF.Exp)
            self.E("dve", "tensor_tensor", kbc + kmc, kebm, out=EBM, in0=BC, in1=MC, op=ALU.subtract)
            self.E("act", "activation", kebm, kebm, out=EBM, in_=EBM, func=AF.Exp)
            for hb in range(2):
                pT = self.ps[hb][:].bitcast(BF16)
                self.TR(kkp, self.pk([hb]), [(pT[:, j * 128:(j + 1) * 128], KP[:, (hb * 8 + j) * 128:(hb * 8 + j + 1) * 128])
                                             for j in range(8)])
                self.E("act", "activation", self.pk([hb]), kkpt, out=KPT[:, hb * 1024:(hb + 1) * 1024], in_=pT, func=AF.Copy)
            mask = self.mask_f if d == 0 else self.mask_b
            mkey = "mask_f" if d == 0 else "mask_b"
            def c1(i):
                b, q = i % 4, 0
                self.MM(kkp + kqp, self.pq(b, q), [(self.ps[b][:, q * 128:(q + 1) * 128], KP[:, i * 128:(i + 1) * 128],
                                                   QP[:, i * 128:(i + 1) * 128], True, True)])
                ms, kms = MS[i % 4]
                self.E("dve", "scalar_tensor_tensor", self.pq(b, q) + [mkey], kms, out=ms, in0=self.ps[b][:, q * 128:(q + 1) * 128],
                       scalar=3.0e38, in1=mask[:], op0=ALU.min, op1=ALU.mult)

            def c2(i):
                ms, kms = MS[i % 4]
                bo = 4 + i // 4
                self.MM(kms + kvt, self.pq(bo, i % 4), [(self.ps[bo][:, (i % 4) * 128:(i % 4 + 1) * 128], VT3[:, i, :], ms, False, False)])
            for i in range(16 + 2):
                if i < 16:
                    c1(i)
                if i >= 2:
                    c2(i - 2)
            order = list(range(32)) if d == 0 else list(range(31, -1, -1))
            KVA, kkva = sv(self.M_T2, 32 * 128, F32)
            KVA3 = KVA.rearrange("p (c v) -> p c v", c=32)
            for sa in range(31):
                c = order[sa]
                i, hh = c // 2, c % 2
                bnk = sa % 4
                self.MM(kkpt + kvt, self.pk([bnk]), [(self.ps[bnk][:, 0:128], KPT3[hh * 64:(hh + 1) * 64, i, :],
                                                     VT3[hh * 64:(hh + 1) * 64, i, :], True, True)])
                self.E("dve", "tensor_scalar", self.pk([bnk]) + kebm, kkva, out=KVA3[:, sa, :], in0=self.ps[bnk][:, 0:128],
                       scalar1=EBM[:, c:c + 1], scalar2=None, op0=ALU.mult)
            prev = KVA3[:, 0, :]
            kprev = kkva
            nxt = (d == 1 and h < 7)
            if nxt:
                self._hgrn_pre(l, h + 1, rb, krb, 0)
            if d == 0:
                front(1, 0)
            for sc in range(31):
                c = order[sc]
                if d == 0 and sc in (3, 6, 9, 12):
                    front(1, 1, only=(sc // 3 - 1))
                if d == 0 and sc == 16:
                    front(1, 2)
                if nxt and sc in (3, 6, 9, 12):
                    self._hgrn_pre(l, h + 1, rb, krb, 1, only=(sc // 3 - 1))
                    if sc == 12:
                        self._hgrn_pre(l, h + 1, rb, krb, 2)
                if sc > 0:
                    stn, kstn = Sst[sc % 2]
                    self.E("dve", "scalar_tensor_tensor", kprev + kkva + keb, kstn, out=stn, in0=prev, scalar=EB[:, c:c + 1],
                           in1=KVA3[:, sc, :], op0=ALU.mult, op1=ALU.add)
                    prev, kprev = stn, kstn
                cn = order[sc + 1]
                spn, kspn = Sp[(sc + 1) % 4]
                self.E("act", "activation", kprev + kem, kspn, out=spn, in_=prev, func=AF.Copy, scale=EM[:, cn:cn + 1])
                bo = 4 + cn // 8
                self.MM(kspn + kqp, self.pk([bo]), [(self.ps[bo][:, (cn % 8) * 64:(cn % 8 + 1) * 64], spn,
                                                    QP[:, cn * 64:(cn + 1) * 64], False, False)])
        if h < 7:
            self._hgrn_pre(l, h + 1, rb, krb, 3)
        for tg in range(4):
            self.E("act", "activation", self.pk([4 + tg]), k1, out=T1[:, tg * 512:(tg + 1) * 512], in_=self.ps[4 + tg][:], func=AF.Copy)
        SQ, ksq = sv(self.M_B6, 2048, BF16)
        YB, kyb = sv(self.M_B5, 2048, BF16)
        self.pnorm_store(T1, k1, T2, k2, SQ, ksq, YB, kyb, 0, self.small[:, 20 + h:21 + h], GS, kgs, h)


def build(cfg):
    nc = bass.Bass("TRN2", target_bir_lowering=False)
    with ExitStack() as es:
        k = K(nc, es, cfg)
        k.setup()
        layers = cfg.get("layers", list(range(DEPTH)))
        k.mixer_setup()
        k.prep(layers)
        for s in range(cfg.get("nseq", 2)):
            k.load_x(s)
            if "mix" in cfg["stages"]:
                k.cossin(s)
            for l in layers:
                if "ffn1" in cfg["stages"]:
                    k.ffn(l, 0)
                if "mix" in cfg["stages"]:
                    k.mixer(l, s)
                if "ffn2" in cfg["stages"]:
                    k.ffn(l, 1)
            k.store_x(s)
        k.P.wait_all("sp")
        k.P.emit()
    return nc


def host_layouts(inp):
    L = DEPTH
    f = lambda a: np.ascontiguousarray(np.asarray(a, dtype=np.float32))
    out = {}

    def kxn(w, ncol_blocks):
        w = np.asarray(w, dtype=np.float32).reshape(L, 16, 128, ncol_blocks, 128)
        return w.transpose(0, 3, 2, 1, 4)

    wgu = np.empty((L, 2, NFC, 128, 2, 16, 128), np.float32)
    for fi, (g, u) in enumerate((("ffn1_w_gate", "ffn1_w_up"), ("ffn2_w_gate", "ffn2_w_up"))):
        wgu[:, fi, :, :, 0] = kxn(inp[g], NFC)
        wgu[:, fi, :, :, 1] = kxn(inp[u], NFC)
    out["wgu"] = wgu.reshape(L, 2, NFC, 128, 4096)
    wd = np.empty((L, 2, 16, 128, NFC, 128), np.float32)
    for fi, dname in enumerate(("ffn1_w_down", "ffn2_w_down")):
        w = np.asarray(inp[dname], dtype=np.float32).reshape(L, NFC, 128, 16, 128)
        wd[:, fi] = w.transpose(0, 3, 2, 1, 4)
    out["wd"] = wd.reshape(L, 2, -1, 1376)
    win = kxn(inp["w_in"], NBLK)
    out["win"] = np.ascontiguousarray(win).reshape(L, NBLK // 2, 2, 128, 16, 128).transpose(0, 1, 3, 2, 4, 5).reshape(
        L, NBLK // 2, 128, 4096)
    out["wout"] = np.ascontiguousarray(kxn(inp["w_out"], 16)).reshape(L, -1, 2048)
    gains = np.stack([np.asarray(inp[n], dtype=np.float32).reshape(L, 16, 128).transpose(0, 2, 1)
                      for n in ("ffn1_norm", "mix_norm", "ffn2_norm")], axis=1)
    out["gains"] = f(gains)
    import ml_dtypes
    bf = ml_dtypes.bfloat16
    out["ident"] = np.eye(128, dtype=np.float32).astype(bf)
    ii = np.arange(128)
    same = (ii[:, None] // 64) == (ii[None, :] // 64)
    mf = (same & (ii[:, None] <= ii[None, :])).astype(np.float32)
    mb = (same & (ii[:, None] >= ii[None, :])).astype(np.float32)
    out["masks"] = np.stack([mf, mb]).astype(bf)
    inv_freq = (10000.0 ** (-np.arange(0, 64, 2, dtype=np.float32) / 64)).astype(np.float32)
    out["invf"] = np.broadcast_to(inv_freq[None, :], (128, 32)).copy()
    lg = np.asarray(inp["hgrn_lb_logits"], dtype=np.float32).reshape(2, L, 8, 128)
    out["lbl"] = lg.transpose(3, 0, 2, 1).reshape(128, 16, L)
    small = np.zeros((L, 128, 32), np.float32)
    cw = np.asarray(inp["conv_w"], dtype=np.float32).reshape(L, 3, 4, 128)
    cb = np.asarray(inp["conv_b"], dtype=np.float32).reshape(L, 4, 128)
    cn = np.asarray(inp["conv_norm"], dtype=np.float32).reshape(L, 4, 128)
    for gi in range(4):
        for j in range(3):
            small[:, :, gi * 5 + j] = cw[:, j, gi, :]
        small[:, :, gi * 5 + 3] = cb[:, gi, :]
        small[:, :, gi * 5 + 4] = cn[:, gi, :]
    small[:, :, 20:28] = np.asarray(inp["hgrn_norm"], dtype=np.float32).reshape(L, 8, 128).transpose(0, 2, 1)
    small[:, :, 28:32] = np.asarray(inp["da_out_norm"], dtype=np.float32).reshape(L, 4, 128).transpose(0, 2, 1)
    out["small"] = small
    bcv = np.concatenate([np.asarray(inp[n], dtype=np.float32) for n in
                          ("da_q_norm", "da_k_norm", "da_lambda_q1", "da_lambda_k1", "da_lambda_q2", "da_lambda_k2")], axis=1)
    out["bc"] = np.broadcast_to(bcv[:, None, :], (L, 128, 384)).copy()
    for k_ in out:
        if out[k_].dtype == np.float32:
            out[k_] = f(out[k_])
    return out


CFG_FULL = {"stages": ("ffn1", "mix", "ffn2"), "nseq": 2}


def kernel(**inputs):
    x = np.asarray(inputs["x"], dtype=np.float32)
    shared = host_layouts(inputs)
    nc = build(CFG_FULL)
    in_maps = []
    for core in range(NCORES):
        xs = x[2 * core:2 * core + 2]
        xT = np.ascontiguousarray(xs.transpose(0, 2, 1)).reshape(2, 16, 128, S)
        pos = np.asarray(inputs["positions"])[2 * core:2 * core + 2].astype(np.int32)
        m = {"xT": xT, "pos": np.ascontiguousarray(pos.reshape(2, 16, 128).transpose(0, 2, 1))}
        m.update(shared)
        in_maps.append(m)
    res = run_bass_kernel_spmd(nc, in_maps, core_ids=list(range(NCORES)))
    out = np.empty((16, S, D), np.float32)
    for core in range(NCORES):
        yT = res.results[core]["yT"].reshape(2, D, S)
        out[2 * core:2 * core + 2] = yT.transpose(0, 2, 1)
    return out
```

```python
import math
from contextlib import ExitStack

import numpy as np
import concourse.bass as bass
import concourse.mybir as mybir
from concourse.bass_utils import run_bass_kernel_spmd

F32 = mybir.dt.float32
BF16 = mybir.dt.bfloat16
I32 = mybir.dt.int32
ALU = mybir.AluOpType
AF = mybir.ActivationFunctionType
AX = mybir.AxisListType

D = 2048
S = 2048
DFF = 5504
NFC = 43
DEPTH = 4
NCORES = 8
EPS = 1e-6
NBLK = 64

COMPUTE = ("pe", "act", "dve", "pool")
ALLENG = ("pe", "act", "dve", "pool", "sp")


class Prog:
    def __init__(self, nc, es):
        self.nc = nc
        self.es = es
        self.streams = {e: [] for e in ALLENG}
        self.sem = {e: es.enter_context(nc.semaphore("s_" + e)) for e in COMPUTE}
        self.cnt = {e: 0 for e in COMPUTE}
        self.dsem = {}
        self.dcnt = {}
        self.waited = {e: {} for e in ALLENG}
        self.lastw = {}
        self.readers = {}

    def _deps(self, eng, reads, writes, is_dma=False):
        toks = []
        for r in reads:
            for t in self.lastw.get(r, {}).values():
                toks.append((t, True))
        for w in writes:
            for t in self.lastw.get(w, {}).values():
                toks.append((t, False))
            rd = self.readers.get(w)
            if rd:
                for t in rd.values():
                    toks.append((t, False))
        need = {}
        for (kind, key, val), raw in toks:
            if not is_dma and kind == "e" and key == eng:
                if not raw or eng == "pe":
                    continue
            k = (kind, key)
            if self.waited[eng].get(k, 0) >= val:
                continue
            if need.get(k, 0) < val:
                need[k] = val
        for k, v in need.items():
            self.waited[eng][k] = v
        return [(k[0], k[1], v) for k, v in need.items()]

    def _record(self, tok, reads, writes):
        for w in writes:
            self.lastw.setdefault(w, {})[(tok[0], tok[1])] = tok
            self.readers[w] = {}
        for r in reads:
            self.readers.setdefault(r, {})[(tok[0], tok[1])] = tok

    def op(self, eng, fn, reads=(), writes=()):
        waits = self._deps(eng, reads, writes)
        self.cnt[eng] += 1
        tok = ("e", eng, self.cnt[eng])
        self._record(tok, reads, writes)
        self.streams[eng].append(("op", waits, fn, None))
        return tok

    def dma(self, eng, semname, fn, reads=(), writes=(), n=1):
        if semname not in self.dsem:
            self.dsem[semname] = self.es.enter_context(self.nc.semaphore("d_" + semname))
            self.dcnt[semname] = 0
        waits = self._deps(eng, reads, writes, is_dma=True)
        self.dcnt[semname] += 16 * n
        tok = ("d", semname, self.dcnt[semname])
        self._record(tok, reads, writes)
        self.streams[eng].append(("dma", waits, fn, semname))
        return tok

    def wait_all(self, eng):
        waits = []
        for e in COMPUTE:
            if self.cnt[e] > 0 and e != eng:
                waits.append(("e", e, self.cnt[e]))
        for s, v in self.dcnt.items():
            if v > 0:
                waits.append(("d", s, v))
        self.streams[eng].append(("wait", waits, None, None))

    def emit(self):
        prog = self

        def semh(kind, key):
            return prog.sem[key] if kind == "e" else prog.dsem[key]

        def run(engname, eobj):
            for kind, waits, fn, semname in prog.streams[engname]:
                for (k, key, v) in waits:
                    eobj.wait_ge(semh(k, key), v)
                if kind == "op":
                    fn(eobj).then_inc(prog.sem[engname], 1)
                elif kind == "dma":
                    inss = fn(eobj)
                    if not isinstance(inss, (list, tuple)):
                        inss = [inss]
                    for i in inss:
                        i.then_inc(prog.dsem[semname], 16)

        with self.nc.Block() as block:
            @block.tensor
            def _(e):
                run("pe", e)

            @block.scalar
            def _(e):
                run("act", e)

            @block.vector
            def _(e):
                run("dve", e)

            @block.gpsimd
            def _(e):
                run("pool", e)

            @block.sync
            def _(e):
                run("sp", e)


SCR_BYTES = 75 * 1024
PAGE = 1024
SMALL_START = 70 * 1024


class K:
    def __init__(self, nc, es, cfg):
        self.nc, self.es, self.cfg = nc, es, cfg
        L = DEPTH
        dt = nc.dram_tensor
        ein = dict(kind="ExternalInput")
        self.xT = dt("xT", [2, 16, 128, S], F32, **ein).ap()
        self.wgu = dt("wgu", [L, 2, NFC, 128, 4096], F32, **ein).ap()
        self.wd = dt("wd", [L, 2, 16 * 128 * 4, 1376], F32, **ein).ap()
        self.win = dt("win", [L, NBLK // 2, 128, 4096], F32, **ein).ap()
        self.wout = dt("wout", [L, 16 * 128 * 16 * 128 // 2048, 2048], F32, **ein).ap()
        self.gains = dt("gains", [L, 3, 128, 16], F32, **ein).ap()
        self.yT = dt("yT", [2, 16, 128, S], F32, kind="ExternalOutput").ap()
        self.ident_d = dt("ident", [128, 128], BF16, **ein).ap()
        self.mask_d = dt("masks", [2, 128, 128], BF16, **ein).ap()
        self.invf_d = dt("invf", [128, 32], F32, **ein).ap()
        self.lbl_d = dt("lbl", [128, 16, 4], F32, **ein).ap()
        self.pos_d = dt("pos", [2, 128, 16], I32, **ein).ap()
        self.small_d = dt("small", [L, 128, 32], F32, **ein).ap()
        self.bc_d = dt("bc", [L, 128, 384], F32, **ein).ap()
        self.cs_s = dt("cs_s", [2, 128, 512], F32).ap()
        self.y_s = [dt(f"y_s{c}", [128, S], BF16).ap() for c in range(16)]
        self.wgu_s = [[dt(f"wgu_s{l}_{f}", [NFC, 128, 4096], BF16).ap() for f in range(2)] for l in range(L)]
        self.wd_s = [[dt(f"wd_s{l}_{f}", [16 * 128 * 4, 1376], BF16).ap() for f in range(2)] for l in range(L)]
        self.win_s = [dt(f"win_s{l}", [NBLK // 2, 128, 4096], BF16).ap() for l in range(L)]
        self.wout_s = [dt(f"wout_s{l}", [16 * 128 * 16 * 128 // 2048, 2048], BF16).ap() for l in range(L)]

        sb = lambda name, shape, d: es.enter_context(nc.sbuf_tensor(name, shape, d))
        self.X = sb("X", [128, 16, S], F32)
        self.SCR = sb("SCR", [128, SCR_BYTES // 2], BF16)
        self.ones_bf = sb("ones_bf", [128, 128], BF16)
        self.gtile = sb("gtile", [128, 3, 16], F32)
        self.eps_t = sb("eps_t", [128, 1], F32)
        self.ident = sb("ident_sb", [128, 128], BF16)
        self.zeros_bf = sb("zeros_bf", [128, 128], BF16)
        self.one_f = sb("one_f", [128, 1], F32)
        self.mask_f = sb("mask_f", [128, 128], BF16)
        self.mask_b = sb("mask_b", [128, 128], BF16)
        self.invf = sb("invf_sb", [128, 32], F32)
        self.lbl = sb("lbl_sb", [128, 16, 4], F32)
        self.lbs = sb("lbs", [128, 16], F32)
        self.lb = sb("lb", [128, 16, 4], F32)
        self.oml = sb("oml", [128, 16, 4], F32)
        self.rcol = sb("rcol", [128, 16], F32)
        self.r2col = sb("r2col", [128, 16], F32)
        self.small = sb("small_sb", [128, 32], F32)
        self.bc = sb("bc_sb", [128, 384], F32)
        self.nlam = sb("nlam", [128, 1], F32)
        self.wcnt = 0
        self.ps = [es.enter_context(nc.psum_tensor(f"ps{i}", [128, 512], F32)) for i in range(8)]
        self.P = Prog(nc, es)
        self.Xhi = self.X[:].bitcast(BF16)

    def sv(self, off, nelem, dtype):
        nbytes = nelem * (4 if dtype == F32 else 2)
        assert off % 4 == 0 and off + nbytes <= SCR_BYTES, (off, nbytes)
        v = self.SCR[:, off // 2:(off + nbytes) // 2]
        if dtype == F32:
            v = v.bitcast(F32)
        keys = []
        b, end = off, off + nbytes
        while b < end:
            if b < SMALL_START:
                keys.append(("S", b // PAGE))
                b = (b // PAGE + 1) * PAGE
            else:
                keys.append(("s", b // 256))
                b = (b // 256 + 1) * 256
        return v, keys

    def xhi(self, c, t0, t1):
        return self.Xhi[:, c, 2 * t0 + 1:2 * t1:2]

    @staticmethod
    def xk(cs, tgs):
        return [("X", c, tg) for c in cs for tg in tgs]

    def setup(self):
        P = self.P
        P.op("pool", lambda e: e.memset(self.ones_bf[:], 1.0), writes=["ones_bf"])
        P.op("pool", lambda e: e.memset(self.eps_t[:], EPS), writes=["eps_t"])

    def prep(self, layers):
        P = self.P
        IN_SLOTS, OUT_SLOTS = 3, 3
        in_off = [i * 16384 for i in range(IN_SLOTS)]
        out_off = [IN_SLOTS * 16384 + i * 8192 for i in range(OUT_SLOTS)]
        cnt = 0
        for l in layers:
            P.dma("sp", "gl", lambda e, l=l: e.dma_start(out=self.gtile[:], in_=self.gains[l].rearrange("g p k -> p g k")),
                  writes=["gtile"])
            jobs = []
            for f in range(2):
                for c in range(NFC):
                    jobs.append((self.wgu[l, f, c], self.wgu_s[l][f][c], 0 if f == 0 else 2))
            for b in range(NBLK // 2):
                jobs.append((self.win[l, b], self.win_s[l][b], 1))
            for src, dst, gi in jobs:
                si, so = cnt % IN_SLOTS, cnt % OUT_SLOTS
                tin, kin = self.sv(in_off[si], 4096, F32)
                tout, kout = self.sv(out_off[so], 4096, BF16)
                P.dma("sp", f"pi{si}", lambda e, tin=tin, src=src: e.dma_start(out=tin, in_=src), writes=kin)
                eng = "dve" if cnt % 3 != 2 else "pool"
                g_b = self.gtile[:, gi, :].unsqueeze(1).unsqueeze(3).to_broadcast([128, 2, 16, 128])
                tin4 = tin.rearrange("p (j k c) -> p j k c", j=2, k=16)
                tout4 = tout.rearrange("p (j k c) -> p j k c", j=2, k=16)
                P.op(eng, lambda e, a=tout4, b=tin4, g=g_b: e.tensor_tensor(out=a, in0=b, in1=g, op=ALU.mult),
                     reads=kin + ["gtile"], writes=kout)
                P.dma("act", f"po{so}", lambda e, tout=tout, dst=dst: e.dma_start(out=dst, in_=tout), reads=kout,
                      writes=[("W", "gu_in", l)])
                cnt += 1
            for f in range(2):
                n_rows = self.wd.shape[2]
                step = n_rows // 8
                for i in range(8):
                    P.dma("pool", "cast", lambda e, l=l, f=f, i=i, step=step: e.dma_start(
                        out=self.wd_s[l][f][i * step:(i + 1) * step, :], in_=self.wd[l, f, i * step:(i + 1) * step, :]),
                        writes=[("W", "d", l)])
            n_rows = self.wout.shape[1]
            step = n_rows // 4
            for i in range(4):
                P.dma("pool", "cast", lambda e, l=l, i=i, step=step: e.dma_start(
                    out=self.wout_s[l][i * step:(i + 1) * step, :], in_=self.wout[l, i * step:(i + 1) * step, :]),
                    writes=[("W", "d", l)])

    def load_x(self, s):
        for c in range(16):
            self.P.dma("sp", "ldx", lambda e, c=c: e.dma_start(out=self.X[:, c, :], in_=self.xT[s, c]),
                       writes=self.xk([c], range(4)))

    def store_x(self, s):
        for c in range(16):
            self.P.dma("sp", "stx", lambda e, c=c: e.dma_start(out=self.yT[s, c], in_=self.X[:, c, :]),
                       reads=self.xk([c], range(4)))

    def rstd(self, t0, nt, r_view, r_keys, sq_off, scale_extra=None, rh_view=None, rh_keys=None, bank=6):
        P = self.P
        tgs = sorted({t // 512 for t in (t0, t0 + nt - 1)})
        for c in range(16):
            sq, ksq = self.sv(sq_off + (c % 2) * 1024, 512, BF16)
            P.op("act", lambda e, c=c, sq=sq: e.activation(out=sq[:, 0:nt], in_=self.X[:, c, t0:t0 + nt], func=AF.Square),
                 reads=self.xk([c], tgs), writes=ksq)
            P.op("pe", lambda e, c=c, sq=sq: e.matmul(self.ps[bank][:, 0:nt], lhsT=self.ones_bf[:], rhs=sq[:, 0:nt],
                                                     start=(c == 0), stop=(c == 15)),
                 reads=ksq + ["ones_bf"], writes=self.pk([bank]))
        P.op("act", lambda e: e.activation(out=r_view[:, 0:nt], in_=self.ps[bank][:, 0:nt], func=AF.Sqrt,
                                           bias=self.eps_t[:], scale=1.0 / D),
             reads=self.pk([bank]) + ["eps_t"], writes=r_keys)
        P.op("dve", lambda e: e.reciprocal(out=r_view[:, 0:nt], in_=r_view[:, 0:nt]), reads=r_keys, writes=r_keys)
        if rh_view is not None:
            P.op("pool", lambda e: e.tensor_scalar(out=rh_view[:, 0:nt], in0=r_view[:, 0:nt], scalar1=0.5, scalar2=None,
                                                   op0=ALU.mult), reads=r_keys, writes=rh_keys)

    def ffn(self, l, f):
        P = self.P
        ACT0 = 0
        WS = 44032
        WSZ = 22528
        RB = WS + WSZ
        RH = RB + 2048
        TA = RH + 2048
        TB = TA
        SQ = TA + 4096
        assert SQ + 2048 <= SCR_BYTES
        for tg in range(4):
            self._ffn_tg(l, f, tg, ACT0, WS, RB, RH, TA, TB, SQ)

    def _ffn_tg(self, l, f, tg, ACT0, WS, RB, RH, TA, TB, SQ):
        P = self.P
        if True:
            t0 = tg * 512
            rb, krb = self.sv(RB, 512, F32)
            rh, krh = self.sv(RH, 512, F32)
            self.rstd(t0, 512, rb, krb, SQ, rh_view=rh, rh_keys=krh)
            for c in range(NFC):
                slot = c % 2
                wt, kw = self.sv(WS + slot * 8192, 4096, BF16)
                wt4 = wt.rearrange("p (j k c) -> p j k c", j=2, k=16)
                P.dma("sp", f"wa{slot}", lambda e, wt=wt, c=c: e.dma_start(out=wt, in_=self.wgu_s[l][f][c]),
                      reads=[("W", "gu_in", l)], writes=kw)
                bg, bu = c % 2, 2 + c % 2

                def mm(e, j, bank, wt4=wt4):
                    ins = None
                    for kc in range(16):
                        ins = e.matmul(self.ps[bank][:], lhsT=wt4[:, j, kc, :], rhs=self.xhi(kc, t0, t0 + 512),
                                       start=(kc == 0), stop=(kc == 15))
                    return ins
                P.op("pe", lambda e, mm=mm, bg=bg: mm(e, 0, bg), reads=kw + self.xk(range(16), [tg]), writes=self.pk([bg]))
                P.op("pe", lambda e, mm=mm, bu=bu: mm(e, 1, bu), reads=kw + self.xk(range(16), [tg]), writes=self.pk([bu]))
                ta, kta = self.sv(TA + (c % 2) * 2048, 512, F32)
                P.op("dve", lambda e, ta=ta, bg=bg: e.tensor_tensor(out=ta, in0=self.ps[bg][:], in1=rb, op=ALU.mult),
                     reads=self.pk([bg]) + krb, writes=kta)
                P.op("act", lambda e, ta=ta: e.activation(out=ta, in_=ta, func=AF.Silu), reads=kta, writes=kta)
                av, kav = self.sv(ACT0 + c * 1024, 512, BF16)
                P.op("dve", lambda e, ta=ta, av=av, bu=bu: e.tensor_tensor(out=av, in0=self.ps[bu][:], in1=ta, op=ALU.mult),
                     reads=self.pk([bu]) + kta, writes=kav)
            actv, kact = self.sv(ACT0, NFC * 512, BF16)
            act3 = actv.rearrange("p (c t) -> p c t", c=NFC)
            rows_per_n = 512
            for n in range(16):
                slot = n % 2
                wt, kw = self.sv(WS + slot * 11264, NFC * 128, BF16)
                wt3 = wt.rearrange("p (c n) -> p c n", c=NFC)
                src = self.wd_s[l][f][n * rows_per_n:(n + 1) * rows_per_n, :].rearrange("(p a) b -> p (a b)", p=128)
                P.dma("sp", f"wb{slot}", lambda e, wt=wt, src=src: e.dma_start(out=wt, in_=src),
                      reads=[("W", "d", l)], writes=kw)
                bd = 4 + n % 2

                def mmd(e, bd=bd, wt3=wt3):
                    ins = None
                    for fc in range(NFC):
                        ins = e.matmul(self.ps[bd][:], lhsT=wt3[:, fc, :], rhs=act3[:, fc, :],
                                       start=(fc == 0), stop=(fc == NFC - 1))
                    return ins
                P.op("pe", mmd, reads=kw + kact, writes=self.pk([bd]))
                tb, ktb = self.sv(TB + (n % 2) * 2048, 512, F32)
                P.op("dve", lambda e, tb=tb, bd=bd: e.tensor_tensor(out=tb, in0=self.ps[bd][:], in1=rh, op=ALU.mult),
                     reads=self.pk([bd]) + krh, writes=ktb)
                xv = self.X[:, n, t0:t0 + 512]
                P.op("pool", lambda e, tb=tb, xv=xv: e.tensor_tensor(out=xv, in0=xv, in1=tb, op=ALU.add),
                     reads=ktb + self.xk([n], [tg]), writes=self.xk([n], [tg]))

    def E(self, eng, meth, reads, writes, **kw):
        return self.P.op(eng, lambda e: getattr(e, meth)(**kw), reads=list(reads), writes=list(writes))

    def MM(self, reads, writes, mms):
        def fn(e):
            ins = None
            for (o, l_, r_, st, sp) in mms:
                ins = e.matmul(o, lhsT=l_, rhs=r_, start=st, stop=sp)
            return ins
        return self.P.op("pe", fn, reads=list(reads), writes=list(writes))

    def TR(self, reads, writes, trs):
        def fn(e):
            ins = None
            for (o, i_) in trs:
                ins = e.transpose(o, i_, self.ident[:])
            return ins
        return self.P.op("pe", fn, reads=list(reads) + ["ident"], writes=list(writes))

    def DMA(self, eng, sem, out, in_, reads, writes):
        return self.P.dma(eng, sem, lambda e: e.dma_start(out=out, in_=in_), reads=list(reads), writes=list(writes))

    def pk(self, banks):
        return [("ps", b) for b in banks]

    def pq(self, b, q):
        return [("ps", b)]

    M_RB, M_WS, M_T2, M_T3, M_T1 = 0, 8192, 16384, 24576, 32768
    M_B1, M_B2, M_B3, M_B4, M_B5, M_B6, M_MSK, M_SM = 41984, 46080, 50176, 55296, 59392, 63488, 67584, 71680

    def wblk(self, l, blk):
        return self.win_s[l][blk // 2][:, (blk % 2) * 2048:(blk % 2 + 1) * 2048]

    def load_w(self, l, blk):
        slot = self.wcnt % 2
        self.wcnt += 1
        wt, kw = self.sv(self.M_WS + slot * 4096, 2048, BF16)
        self.DMA("sp", f"mw{slot}", wt, self.wblk(l, blk), [("W", "gu_in", l)], kw)
        return wt.rearrange("p (k c) -> p k c", k=16), kw

    def proj_fm(self, l, blk, b0):
        wt3, kw = self.load_w(l, blk)
        for tg in range(4):
            self.MM(kw + self.xk(range(16), [tg]), self.pk([b0 + tg]),
                    [(self.ps[b0 + tg][:], wt3[:, kc, :], self.xhi(kc, tg * 512, tg * 512 + 512), kc == 0, kc == 15)
                     for kc in range(16)])

    def proj_tm(self, l, blk, b0):
        wt3, kw = self.load_w(l, blk)
        for i in range(16):
            o = self.ps[b0 + i // 4][:, (i % 4) * 128:(i % 4 + 1) * 128]
            self.MM(kw + self.xk(range(16), [i // 4]), self.pk([b0 + i // 4]),
                    [(o, self.xhi(kc, i * 128, i * 128 + 128), wt3[:, kc, :], kc == 0, kc == 15) for kc in range(16)])

    def mixer_setup(self):
        P = self.P
        self.E("pool", "memset", [], ["zeros_bf"], ap=self.zeros_bf[:], constant=0.0)
        self.E("pool", "memset", [], ["one_f"], ap=self.one_f[:], constant=1.0)
        self.DMA("sp", "cst", self.ident[:], self.ident_d[:, :], [], ["ident"])
        self.DMA("sp", "cst", self.mask_f[:], self.mask_d[0], [], ["mask_f"])
        self.DMA("sp", "cst", self.mask_b[:], self.mask_d[1], [], ["mask_b"])
        self.DMA("sp", "cst", self.invf[:], self.invf_d[:, :], [], ["invf"])
        lbl = self.lbl
        self.DMA("sp", "cst", lbl[:], self.lbl_d[:, :, :], [], ["lbl"])
        self.E("act", "activation", ["lbl"], ["lbl"], out=lbl[:], in_=lbl[:], func=AF.Exp)
        self.E("dve", "tensor_reduce", ["lbl"], ["lbs"], out=self.lbs[:], in_=lbl[:], axis=AX.X, op=ALU.add)
        self.E("dve", "reciprocal", ["lbs"], ["lbs"], out=self.lbs[:], in_=self.lbs[:])
        self.E("dve", "tensor_tensor", ["lbl", "lbs"], ["lbl"], out=lbl[:], in0=lbl[:],
               in1=self.lbs[:].unsqueeze(2).to_broadcast([128, 16, 4]), op=ALU.mult)
        self.E("pool", "memset", [], ["lb"], ap=self.lb[:, :, 0:1], constant=0.0)
        for j in range(1, 4):
            self.E("dve", "tensor_tensor", ["lbl", "lb"], ["lb"], out=self.lb[:, :, j:j + 1], in0=self.lb[:, :, j - 1:j],
                   in1=lbl[:, :, j:j + 1], op=ALU.add)
        self.E("dve", "tensor_scalar", ["lb"], ["oml"], out=self.oml[:], in0=self.lb[:], scalar1=-1.0, scalar2=1.0,
               op0=ALU.mult, op1=ALU.add)

    def cossin(self, s):
        PI = math.pi
        ang, ka = self.sv(self.M_T1, 512, F32)
        tmp, kt = self.sv(self.M_T2, 512, F32)
        ki_, kki = self.sv(self.M_T3, 512, F32)
        pi_t, kpi = self.sv(self.M_T3 + 4096, 512, F32)
        kint = pi_t.bitcast(I32)
        posf, kpf = self.sv(self.M_B1, 16, F32)
        posi = posf.bitcast(I32)
        self.DMA("sp", "cst", posi, self.pos_d[s], [], kpf)
        self.E("dve", "tensor_copy", kpf, kpf, out=posf, in_=posi)
        ang3 = ang.rearrange("p (i j) -> p i j", i=16)
        self.E("dve", "tensor_tensor", kpf + ["invf"], ka, out=ang3, in0=posf.unsqueeze(2).to_broadcast([128, 16, 32]),
               in1=self.invf[:].unsqueeze(1).to_broadcast([128, 16, 32]), op=ALU.mult)
        for which, shift in ((0, 0.0), (1, PI / 2)):
            self.E("dve", "tensor_scalar", ka, kt, out=tmp, in0=ang, scalar1=shift, scalar2=1.0 / (2 * PI), op0=ALU.add,
                   op1=ALU.mult)
            self.E("dve", "tensor_copy", kt, kpi, out=kint, in_=tmp)
            self.E("dve", "tensor_copy", kpi, kki, out=ki_, in_=kint)
            self.E("dve", "scalar_tensor_tensor", kki + ka, kt, out=tmp, in0=ki_, scalar=-2 * PI, in1=ang, op0=ALU.mult,
                   op1=ALU.add)
            if shift != 0.0:
                self.E("dve", "tensor_scalar", kt, kt, out=tmp, in0=tmp, scalar1=shift, scalar2=None, op0=ALU.add)
            self.E("dve", "tensor_scalar", kt, kki, out=ki_, in0=tmp, scalar1=PI, scalar2=-2 * PI, op0=ALU.is_gt, op1=ALU.mult)
            self.E("dve", "tensor_tensor", kt + kki, kt, out=tmp, in0=tmp, in1=ki_, op=ALU.add)
            self.E("dve", "tensor_scalar", kt, kki, out=ki_, in0=tmp, scalar1=-PI, scalar2=2 * PI, op0=ALU.is_lt, op1=ALU.mult)
            self.E("dve", "tensor_tensor", kt + kki, kt, out=tmp, in0=tmp, in1=ki_, op=ALU.add)
            self.E("dve", "tensor_scalar", kt, kt, out=tmp, in0=tmp, scalar1=PI, scalar2=-PI, op0=ALU.min, op1=ALU.max)
            self.E("act", "activation", kt, kki, out=ki_, in_=tmp, func=AF.Sin)
            self.DMA("sp", "cst", self.cs_s[1 - which], ki_, kki, [("cs",)])

    def mixer(self, l, s):
        P = self.P
        self.wcnt = 0
        rb, krb = self.sv(self.M_RB, 2048, F32)
        for tg in range(4):
            v, kv = self.sv(self.M_RB + tg * 2048, 512, F32)
            self.rstd(tg * 512, 512, v, kv, self.M_SM)
        for i in range(16):
            self.MM(krb + ["one_f"], self.pk([7]), [(self.ps[7][:, i:i + 1], rb[0:1, i * 128:(i + 1) * 128], self.one_f[0:1, 0:1], True, True)])
        self.E("dve", "tensor_copy", self.pk([7]), ["rcol"], out=self.rcol[:], in_=self.ps[7][:, 0:16])
        self.E("dve", "tensor_tensor", ["rcol"], ["r2col"], out=self.r2col[:], in0=self.rcol[:], in1=self.rcol[:], op=ALU.mult)
        self.DMA("sp", "cst", self.small[:], self.small_d[l], [], ["small"])
        self.DMA("sp", "cst", self.bc[:], self.bc_d[l], [], ["bc"])
        parts = self.cfg.get("mix", ("conv", "attn", "hgrn"))
        if "conv" in parts:
            for gi in range(4):
                self.conv_group(l, gi, rb, krb)
        if "attn" in parts:
            self.attn_setup(l)
            for a in range(4):
                self.attn_head(l, a, rb, krb)
        if "hgrn" in parts:
            self.hgrn_setup()
            for h in range(8):
                self.hgrn_head(l, h, rb, krb)
        ych = []
        if "hgrn" in parts:
            ych += list(range(0, 8))
        if "attn" in parts:
            ych += list(range(8, 12))
        if "conv" in parts:
            ych += list(range(12, 16))
        self.wout_stage(l, ych)

    def conv_group(self, l, gi, rb, krb):
        T1, k1 = self.sv(self.M_T1, 2050, F32)
        T2, k2 = self.sv(self.M_T2, 2048, F32)
        T3, k3 = self.sv(self.M_T3, 2048, F32)
        SQ, ksq = self.sv(self.M_B6, 2048, BF16)
        YB, kyb = self.sv(self.M_B5, 2048, BF16)
        sm = self.small
        c0 = gi * 5
        self.E("pool", "memset", [], k1, ap=T1[:, 0:1], constant=0.0)
        self.E("pool", "memset", [], k1, ap=T1[:, 2049:2050], constant=0.0)
        self.proj_fm(l, 56 + gi, 0)
        for tg in range(4):
            self.E("dve", "tensor_tensor", self.pk([tg]) + krb, k1, out=T1[:, 1 + tg * 512:1 + tg * 512 + 512],
                   in0=self.ps[tg][:], in1=rb[:, tg * 512:(tg + 1) * 512], op=ALU.mult)
        self.proj_fm(l, 60 + gi, 4)
        for tg in range(4):
            self.E("dve", "tensor_tensor", self.pk([4 + tg]) + krb, k2, out=T2[:, tg * 512:(tg + 1) * 512],
                   in0=self.ps[4 + tg][:], in1=rb[:, tg * 512:(tg + 1) * 512], op=ALU.mult)
        self.E("dve", "tensor_tensor", k1 + k2, k1, out=T1[:, 1:2049], in0=T1[:, 1:2049], in1=T2, op=ALU.mult)
        self.E("dve", "tensor_scalar", k1 + ["small"], k2, out=T2, in0=T1[:, 1:2049], scalar1=sm[:, c0 + 1:c0 + 2],
               scalar2=sm[:, c0 + 3:c0 + 4], op0=ALU.mult, op1=ALU.add)
        self.E("dve", "scalar_tensor_tensor", k1 + k2 + ["small"], k2, out=T2, in0=T1[:, 0:2048], scalar=sm[:, c0:c0 + 1],
               in1=T2, op0=ALU.mult, op1=ALU.add)
        self.E("dve", "scalar_tensor_tensor", k1 + k2 + ["small"], k2, out=T2, in0=T1[:, 2:2050], scalar=sm[:, c0 + 2:c0 + 3],
               in1=T2, op0=ALU.mult, op1=ALU.add)
        self.proj_fm(l, 52 + gi, 0)
        for tg in range(4):
            self.E("dve", "tensor_tensor", self.pk([tg]) + krb, k3, out=T3[:, tg * 512:(tg + 1) * 512],
                   in0=self.ps[tg][:], in1=rb[:, tg * 512:(tg + 1) * 512], op=ALU.mult)
        self.E("dve", "tensor_tensor", k2 + k3, k2, out=T2, in0=T2, in1=T3, op=ALU.mult)
        self.pnorm_store(T2, k2, T3, k3, SQ, ksq, YB, kyb, 4, sm[:, c0 + 4:c0 + 5], None, None, 12 + gi, 1.0)

    def pnorm_store(self, Tin, kin, Ttmp, ktmp, SQ, ksq, YB, kyb, b0, gain_col, mul_t, kmul, ychunk, in_is_psum_banks=None):
        self.E("act", "activation", kin, ksq, out=SQ, in_=Tin, func=AF.Square)
        for tg in range(4):
            self.MM(ksq + ["ones_bf"], self.pk([b0 + tg]),
                    [(self.ps[b0 + tg][:], self.ones_bf[:], SQ[:, tg * 512:(tg + 1) * 512], True, True)])
            self.E("act", "activation", self.pk([b0 + tg]) + ["eps_t"], ktmp, out=Ttmp[:, tg * 512:(tg + 1) * 512],
                   in_=self.ps[b0 + tg][:], func=AF.Sqrt, bias=self.eps_t[:], scale=1.0 / 128)
        self.E("dve", "reciprocal", ktmp, ktmp, out=Ttmp, in_=Ttmp)
        if mul_t is None:
            self.E("dve", "scalar_tensor_tensor", kin + ktmp + ["small"], kyb, out=YB, in0=Tin, scalar=gain_col, in1=Ttmp,
                   op0=ALU.mult, op1=ALU.mult)
        else:
            self.E("dve", "scalar_tensor_tensor", kin + ktmp + ["small"], ktmp, out=Ttmp, in0=Tin, scalar=gain_col, in1=Ttmp,
                   op0=ALU.mult, op1=ALU.mult)
            self.E("dve", "tensor_tensor", ktmp + kmul, kyb, out=YB, in0=Ttmp, in1=mul_t, op=ALU.mult)
        self.DMA("sp", "ysto", self.y_s[ychunk], YB, kyb, [("y", ychunk)])

    def wout_stage(self, l, ych):
        for tg in range(4):
            yt, ky = self.sv(self.M_T2, 16 * 512, BF16)
            yt3 = yt.rearrange("p (c t) -> p c t", c=16)
            for c in ych:
                self.DMA("sp", "yld", yt3[:, c, :], self.y_s[c][:, tg * 512:(tg + 1) * 512], [("y", c)], ky)
            for n in range(16):
                slot = self.wcnt % 2
                self.wcnt += 1
                wt, kw = self.sv(self.M_WS + slot * 4096, 2048, BF16)
                self.DMA("sp", f"mw{slot}", wt, self.wout_s[l][n * 128:(n + 1) * 128, :], [("W", "d", l)], kw)
                wt3 = wt.rearrange("p (k c) -> p k c", k=16)
                b = n % 2
                self.MM(kw + ky, self.pk([b]), [(self.ps[b][:], wt3[:, c, :], yt3[:, c, :], j == 0, j == len(ych) - 1)
                                                for j, c in enumerate(ych)])
                xv = self.X[:, n, tg * 512:(tg + 1) * 512]
                self.E("dve", "tensor_tensor", self.pk([b]) + self.xk([n], [tg]), self.xk([n], [tg]), out=xv, in0=self.ps[b][:],
                       in1=xv, op=ALU.add)

    def attn_setup(self, l):
        bc3 = self.bc[:].rearrange("p (g d) -> p g d", g=6)
        lt, kl = self.sv(self.M_SM, 128, F32)
        lt3 = lt.rearrange("p (g d) -> p g d", g=2)
        ls, kls = self.sv(self.M_SM + 512, 2, F32)
        self.E("dve", "tensor_tensor", ["bc"], kl, out=lt3, in0=bc3[:, 2:6:2, :], in1=bc3[:, 3:6:2, :], op=ALU.mult)
        self.E("dve", "tensor_reduce", kl, kls, out=ls, in_=lt3, axis=AX.X, op=ALU.add)
        self.E("act", "activation", kls, kls, out=ls, in_=ls, func=AF.Exp)
        lam_init = 0.8 - 0.6 * math.exp(-0.3 * l)
        self.E("dve", "tensor_tensor", kls, ["nlam"], out=self.nlam[:], in0=ls[:, 1:2], in1=ls[:, 0:1], op=ALU.subtract)
        self.E("dve", "tensor_scalar", ["nlam"], ["nlam"], out=self.nlam[:], in0=self.nlam[:], scalar1=-lam_init, scalar2=None,
               op0=ALU.add)
        self.lam_init = lam_init

    def attn_head(self, l, a, rb, krb):
        sv = self.sv
        COS, kcos = sv(self.M_T2, 512, F32)
        SIN, ksin = sv(self.M_T2 + 2048, 512, F32)
        self.DMA("sp", "cst", COS, self.cs_s[0], [("cs",)], kcos)
        self.DMA("sp", "cst", SIN, self.cs_s[1], [("cs",)], ksin)
        cos4 = COS.rearrange("p (i d) -> p i d", i=16).unsqueeze(2).to_broadcast([128, 16, 2, 32])
        sin4 = SIN.rearrange("p (i d) -> p i d", i=16).unsqueeze(2).to_broadcast([128, 16, 2, 32])
        QT, kqt = sv(self.M_B1, 2048, BF16)
        KT, kkt = sv(self.M_B2, 2048, BF16)
        VA, kva = sv(self.M_B3, 16 * 130, BF16)
        VA3 = VA.rearrange("p (i c) -> p i c", i=16)
        QN, kqn = sv(self.M_B4, 2048, BF16)
        QN4 = QN.rearrange("p (i m d) -> p i m d", i=16, m=2)
        QN3 = QN.rearrange("p (i c) -> p i c", i=16)
        T1, k1 = sv(self.M_T1, 2048, F32)
        T1v = T1.rearrange("p (i m d) -> p i m d", i=16, m=2)
        RT, krt = sv(self.M_B5, 2048, F32)
        RT5 = RT.rearrange("p (h i m d) -> p h i m d", h=2, i=16, m=2)
        SS, kss = sv(self.M_SM + 1024, 32, F32)
        SS3 = SS.rearrange("p (i m) -> p i m", i=16)
        bc3 = self.bc[:].rearrange("p (g d) -> p g d", g=6)
        for which, (blk, dst, kdst) in enumerate(((40 + a, QT, kqt), (44 + a, KT, kkt))):
            self.proj_tm(l, blk, 0)
            for b in range(4):
                self.E("act", "activation", self.pk([b]), k1, out=T1[:, b * 512:(b + 1) * 512], in_=self.ps[b][:], func=AF.Square)
            self.E("dve", "tensor_reduce", k1, kss, out=SS, in_=T1.rearrange("p (g d) -> p g d", d=64), axis=AX.X, op=ALU.add)
            self.E("dve", "tensor_tensor", kss + ["r2col"], kss, out=SS3, in0=SS3,
                   in1=self.r2col[:].unsqueeze(2).to_broadcast([128, 16, 2]), op=ALU.mult)
            self.E("act", "activation", kss + ["eps_t"], kss, out=SS, in_=SS, func=AF.Sqrt, bias=self.eps_t[:], scale=1.0 / 64)
            self.E("dve", "reciprocal", kss, kss, out=SS, in_=SS)
            self.E("dve", "tensor_tensor", kss + ["rcol"], kss, out=SS3, in0=SS3,
                   in1=self.rcol[:].unsqueeze(2).to_broadcast([128, 16, 2]), op=ALU.mult)
            if which == 0:
                self.E("dve", "tensor_scalar", kss, kss, out=SS, in0=SS, scalar1=0.125, scalar2=None, op0=ALU.mult)
            for b in range(4):
                self.E("dve", "tensor_tensor", self.pk([b]) + kss, k1, out=T1v[:, 4 * b:4 * b + 4, :, :],
                       in0=self.ps[b][:].rearrange("p (i m d) -> p i m d", i=4, m=2),
                       in1=SS3[:, 4 * b:4 * b + 4, :].unsqueeze(3).to_broadcast([128, 4, 2, 64]), op=ALU.mult)
            self.E("dve", "tensor_tensor", k1 + ["bc"], k1, out=T1.rearrange("p (g d) -> p g d", d=64),
                   in0=T1.rearrange("p (g d) -> p g d", d=64),
                   in1=bc3[:, which, :].unsqueeze(1).to_broadcast([128, 32, 64]), op=ALU.mult)
            t1 = T1v[:, :, :, 0:32]
            t2 = T1v[:, :, :, 32:64]
            self.E("dve", "tensor_tensor", k1 + kcos, krt, out=RT5[:, 0], in0=t1, in1=cos4, op=ALU.mult)
            self.E("dve", "tensor_tensor", k1 + ksin, krt, out=RT5[:, 1], in0=t2, in1=sin4, op=ALU.mult)
            self.E("dve", "tensor_tensor", krt, kqn, out=QN4[:, :, :, 0:32], in0=RT5[:, 0], in1=RT5[:, 1], op=ALU.subtract)
            self.E("dve", "tensor_tensor", k1 + kcos, krt, out=RT5[:, 0], in0=t2, in1=cos4, op=ALU.mult)
            self.E("dve", "tensor_tensor", k1 + ksin, krt, out=RT5[:, 1], in0=t1, in1=sin4, op=ALU.mult)
            self.E("dve", "tensor_tensor", krt, kqn, out=QN4[:, :, :, 32:64], in0=RT5[:, 0], in1=RT5[:, 1], op=ALU.add)
            for hb in range(2):
                pT = self.ps[4 + hb][:].bitcast(BF16)
                self.TR(kqn, self.pk([4 + hb]), [(pT[:, j * 128:(j + 1) * 128], QN3[:, hb * 8 + j, :]) for j in range(8)])
                self.E("act", "activation", self.pk([4 + hb]), kdst, out=dst[:, hb * 1024:(hb + 1) * 1024], in_=pT, func=AF.Copy)
        self.proj_tm(l, 48 + a, 0)
        for b in range(4):
            self.E("dve", "tensor_tensor", self.pk([b]) + ["rcol"], kva, out=VA3[:, 4 * b:4 * b + 4, 0:128],
                   in0=self.ps[b][:].rearrange("p (i c) -> p i c", i=4),
                   in1=self.rcol[:, 4 * b:4 * b + 4].unsqueeze(2).to_broadcast([128, 4, 128]), op=ALU.mult)
        self.E("pool", "memset", [], kva, ap=VA3[:, :, 128:129], constant=1.0)
        PT, kpt = sv(self.M_T2, 16 * 512, BF16)
        PT3 = PT.rearrange("p (k q) -> p k q", k=16)
        ON, kon = sv(self.M_MSK, 1024, F32)
        ON4 = ON.rearrange("p (m q d) -> p m q d", m=2, q=4)
        YB, kyb = sv(self.M_B5, 2048, BF16)
        OB, kob = sv(self.M_B4, 512, BF16)
        OB3 = OB.rearrange("p (q d) -> p q d", q=4)
        RS, krs = sv(self.M_SM + 1280, 1, F32)
        S4, ks4 = sv(self.M_SM + 1536, 4, F32)
        AVB = [0, 1, 6, 7]
        kptk = [sv(self.M_T2 + kt * 1024, 512, BF16)[1] for kt in range(16)]
        groups = [(qg, m) for qg in range(4) for m in range(2)]

        def combine(qg):
            self.E("dve", "scalar_tensor_tensor", kon + ["nlam"], kon, out=ON4[:, 0], in0=ON4[:, 1], scalar=self.nlam[:],
                   in1=ON4[:, 0], op0=ALU.mult, op1=ALU.add)
            self.E("dve", "tensor_tensor", kon, kon, out=ON4[:, 1], in0=ON4[:, 0], in1=ON4[:, 0], op=ALU.mult)
            self.E("dve", "tensor_reduce", kon, ks4, out=S4, in_=ON4[:, 1], axis=AX.X, op=ALU.add)
            self.E("act", "activation", ks4 + ["eps_t"], ks4, out=S4, in_=S4, func=AF.Sqrt, bias=self.eps_t[:], scale=1.0 / 128)
            self.E("dve", "reciprocal", ks4, ks4, out=S4, in_=S4)
            self.E("dve", "tensor_scalar", ks4, ks4, out=S4, in0=S4, scalar1=1.0 - self.lam_init, scalar2=None, op0=ALU.mult)
            self.E("dve", "tensor_tensor", kon + ks4, kob, out=OB3, in0=ON4[:, 0], in1=S4.unsqueeze(2).to_broadcast([128, 4, 128]),
                   op=ALU.mult)
            pT = self.ps[4][:].bitcast(BF16)
            self.TR(kob, self.pk([4]), [(pT[:, j * 128:(j + 1) * 128], OB3[:, j, :]) for j in range(4)])
            self.E("act", "activation", self.pk([4]) + ["small"], kyb, out=YB[:, qg * 512:(qg + 1) * 512], in_=pT[:, 0:512],
                   func=AF.Copy, scale=self.small[:, 28 + a:29 + a])

        for gi in range(len(groups) + 1):
            for kt in range(16):
                if gi < len(groups):
                    qg, m = groups[gi]
                    b = 2 + (kt % 2)
                    self.MM(kkt + kqt, self.pk([b]), [(self.ps[b][:], KT[m * 64:(m + 1) * 64, kt * 128:(kt + 1) * 128],
                                                      QT[m * 64:(m + 1) * 64, qg * 512:(qg + 1) * 512], True, True)])
                if gi >= 1:
                    for qt in range(4):
                        bo = AVB[qt]
                        self.MM(kptk[kt] + kva, self.pk([bo]), [(self.ps[bo][:, 0:129], PT3[:, kt, qt * 128:(qt + 1) * 128],
                                                                VA3[:, kt, 0:129], kt == 0, kt == 15)])
                if gi < len(groups):
                    self.E("act", "activation", self.pk([b]), kptk[kt], out=PT3[:, kt, :], in_=self.ps[b][:], func=AF.Exp)
            if gi >= 1:
                qg_p, m_p = groups[gi - 1]
                for qt in range(4):
                    bo = AVB[qt]
                    self.E("dve", "reciprocal", self.pk([bo]), krs, out=RS, in_=self.ps[bo][:, 128:129])
                    self.E("dve", "tensor_scalar", self.pk([bo]) + krs, kon, out=ON4[:, m_p, qt, :], in0=self.ps[bo][:, 0:128],
                           scalar1=RS, scalar2=None, op0=ALU.mult)
                if m_p == 1:
                    combine(qg_p)
        self.DMA("sp", "ysto", self.y_s[8 + a], YB, kyb, [("y", 8 + a)])

    def hgrn_setup(self):
        MSK, kmsk = self.sv(self.M_MSK, 2048, BF16)
        self.E("pool", "memset", [], kmsk, ap=MSK, constant=1.0)
        self.E("pool", "memset", [], kmsk, ap=MSK.rearrange("p (c k) -> p c k", k=64)[:, :, 0:1], constant=0.0)

    def _hgrn_pre(self, l, h, rb, krb, stage, only=None):
        QS, kqs = self.sv(self.M_B1, 2048, BF16)
        VT, kvt = self.sv(self.M_B3, 2048, BF16)
        VT3 = VT.rearrange("p (i c) -> p i c", i=16)
        if stage == 0:
            self.proj_fm(l, h, 0)
        elif stage == 1:
            for tg in (range(4) if only is None else [only]):
                self.E("dve", "tensor_tensor", self.pk([tg]) + krb, kqs, out=QS[:, tg * 512:(tg + 1) * 512], in0=self.ps[tg][:],
                       in1=rb[:, tg * 512:(tg + 1) * 512], op=ALU.mult)
        elif stage == 2:
            self.proj_tm(l, 8 + h, 0)
        else:
            for b in range(4):
                self.E("dve", "tensor_tensor", self.pk([b]) + ["rcol"], kvt, out=VT3[:, 4 * b:4 * b + 4, :],
                       in0=self.ps[b][:].rearrange("p (i c) -> p i c", i=4),
                       in1=self.rcol[:, 4 * b:4 * b + 4].unsqueeze(2).to_broadcast([128, 4, 128]), op=ALU.mult)

    def hgrn_head(self, l, h, rb, krb):
        sv = self.sv
        MSK, kmsk = sv(self.M_MSK, 2048, BF16)
        T1, k1 = sv(self.M_T1, 2048, F32)
        T2, k2 = sv(self.M_T2, 2048, F32)
        T3, k3 = sv(self.M_T3, 2048, F32)
        QS, kqs = sv(self.M_B1, 2048, BF16)
        GS, kgs = sv(self.M_B2, 2048, BF16)
        VT, kvt = sv(self.M_B3, 2048, BF16)
        VT3 = VT.rearrange("p (i c) -> p i c", i=16)
        QP, kqp = sv(self.M_B4, 2048, BF16)
        KP, kkp = sv(self.M_B5, 2048, BF16)
        KPT, kkpt = sv(self.M_B6, 2048, BF16)
        KPT3 = KPT.rearrange("p (i c) -> p i c", i=16)
        sm0 = self.M_SM
        Sst = [sv(sm0 + j * 512, 128, F32) for j in range(2)]
        Sp = [sv(sm0 + 1024 + j * 256, 128, BF16) for j in range(4)]
        MS = [sv(sm0 + 2048 + j * 256, 128, BF16) for j in range(4)]
        MC, kmc = sv(sm0 + 3072, 32, F32)
        BC, kbc = sv(sm0 + 3328, 32, F32)
        EM, kem = sv(sm0 + 3584, 32, F32)
        EB, keb = sv(sm0 + 3840, 32, F32)
        EBM, kebm = sv(sm0 + 4096, 32, F32)
        T3v = T3.rearrange("p (c k) -> p c k", k=64)
        if h == 0:
            for stage in range(4):
                self._hgrn_pre(l, 0, rb, krb, stage)
        self.proj_fm(l, 32 + h, 0)
        for tg in range(4):
            self.E("dve", "tensor_tensor", self.pk([tg]) + krb, k1, out=T1[:, tg * 512:(tg + 1) * 512], in0=self.ps[tg][:],
                   in1=rb[:, tg * 512:(tg + 1) * 512], op=ALU.mult)
        self.E("act", "activation", k1, kgs, out=GS, in_=T1, func=AF.Silu)
        for b in range(4, 8):
            self.MM(kqs + ["zeros_bf"], self.pk([b]), [(self.ps[b][:], self.zeros_bf[:], QS[:, 0:512], True, True)])
        lbi = lambda d: self.lb[:, d * 8 + h, l:l + 1]
        omli = lambda d: self.oml[:, d * 8 + h, l:l + 1]
        def front(dd, stage, only=None):
            if stage == 0:
                self.proj_fm(l, 16 + 8 * dd + h, 0)
            elif stage == 1:
                for tg in (range(4) if only is None else [only]):
                    self.E("dve", "tensor_tensor", self.pk([tg]) + krb, k1, out=T1[:, tg * 512:(tg + 1) * 512], in0=self.ps[tg][:],
                           in1=rb[:, tg * 512:(tg + 1) * 512], op=ALU.mult)
            else:
                self.E("act", "activation", k1, k1, out=T1, in_=T1, func=AF.Sigmoid)
                self.E("dve", "tensor_scalar", k1 + ["lb", "oml"], k1, out=T1, in0=T1, scalar1=omli(dd), scalar2=lbi(dd), op0=ALU.mult,
                       op1=ALU.add)

        for d in range(2):
            if d == 0:
                front(0, 0)
                front(0, 1)
                front(0, 2)
            self.E("act", "activation", k1, k2, out=T2, in_=T1, func=AF.Ln)
            self.E("dve", "tensor_scalar", k1, k1, out=T1, in0=T1, scalar1=-1.0, scalar2=1.0, op0=ALU.mult, op1=ALU.add)
            self.E("dve", "tensor_tensor_scan", k2 + kmsk, k3, out=T3, data0=MSK, data1=T2, initial=0.0, op0=ALU.mult,
                   op1=ALU.add)
            if d == 0:
                self.E("dve", "tensor_copy", k3, kmc, out=MC, in_=T3v[:, :, 31])
                self.E("dve", "tensor_copy", k3, kbc, out=BC, in_=T3v[:, :, 63])
            else:
                self.E("dve", "tensor_copy", k3, kbc, out=BC, in_=T3v[:, :, 63])
                self.E("dve", "tensor_tensor", k2 + k3, k3, out=T3, in0=T2, in1=T3, op=ALU.subtract)
                self.E("dve", "tensor_tensor", k3 + kbc, k3, out=T3v, in0=T3v, in1=BC.unsqueeze(2).to_broadcast([128, 32, 64]),
                       op=ALU.add)
                self.E("dve", "tensor_copy", k3, kmc, out=MC, in_=T3v[:, :, 32])
            self.E("dve", "tensor_tensor", k3 + kmc, k3, out=T3v, in0=T3v, in1=MC.unsqueeze(2).to_broadcast([128, 32, 64]),
                   op=ALU.subtract)
            self.E("act", "activation", k3, k2, out=T2, in_=T3, func=AF.Exp)
            self.E("dve", "tensor_tensor", k2 + kqs, kqp, out=QP, in0=T2, in1=QS, op=ALU.mult)
            self.E("act", "activation", k3, k2, out=T2, in_=T3, func=AF.Exp, scale=-1.0)
            self.E("dve", "tensor_tensor", k2 + k1, kkp, out=KP, in0=T2, in1=T1, op=ALU.mult)
            self.E("act", "activation", kmc, kem, out=EM, in_=MC, func=AF.Exp)
            self.E("act", "activation", kbc, keb, out=EB, in_=BC, func=AF.Exp)
            self.E("dve", "tensor_tensor", kbc + kmc, kebm, out=EBM, in0=BC, in1=MC, op=ALU.subtract)
            self.E("act", "activation", kebm, kebm, out=EBM, in_=EBM, func=AF.Exp)
            for hb in range(2):
                pT = self.ps[hb][:].bitcast(BF16)
                self.TR(kkp, self.pk([hb]), [(pT[:, j * 128:(j + 1) * 128], KP[:, (hb * 8 + j) * 128:(hb * 8 + j + 1) * 128])
                                             for j in range(8)])
                self.E("act", "activation", self.pk([hb]), kkpt, out=KPT[:, hb * 1024:(hb + 1) * 1024], in_=pT, func=AF.Copy)
            mask = self.mask_f if d == 0 else self.mask_b
            mkey = "mask_f" if d == 0 else "mask_b"
            def c1(i):
                b, q = i % 4, 0
                self.MM(kkp + kqp, self.pq(b, q), [(self.ps[b][:, q * 128:(q + 1) * 128], KP[:, i * 128:(i + 1) * 128],
                                                   QP[:, i * 128:(i + 1) * 128], True, True)])
                ms, kms = MS[i % 4]
                self.E("dve", "scalar_tensor_tensor", self.pq(b, q) + [mkey], kms, out=ms, in0=self.ps[b][:, q * 128:(q + 1) * 128],
                       scalar=3.0e38, in1=mask[:], op0=ALU.min, op1=ALU.mult)

            def c2(i):
                ms, kms = MS[i % 4]
                bo = 4 + i // 4
                self.MM(kms + kvt, self.pq(bo, i % 4), [(self.ps[bo][:, (i % 4) * 128:(i % 4 + 1) * 128], VT3[:, i, :], ms, False, False)])
            for i in range(16 + 2):
                if i < 16:
                    c1(i)
                if i >= 2:
                    c2(i - 2)
            order = list(range(32)) if d == 0 else list(range(31, -1, -1))
            KVA, kkva = sv(self.M_T2, 32 * 128, F32)
            KVA3 = KVA.rearrange("p (c v) -> p c v", c=32)
            for sa in range(31):
                c = order[sa]
                i, hh = c // 2, c % 2
                bnk = sa % 4
                self.MM(kkpt + kvt, self.pk([bnk]), [(self.ps[bnk][:, 0:128], KPT3[hh * 64:(hh + 1) * 64, i, :],
                                                     VT3[hh * 64:(hh + 1) * 64, i, :], True, True)])
                self.E("dve", "tensor_scalar", self.pk([bnk]) + kebm, kkva, out=KVA3[:, sa, :], in0=self.ps[bnk][:, 0:128],
                       scalar1=EBM[:, c:c + 1], scalar2=None, op0=ALU.mult)
            prev = KVA3[:, 0, :]
            kprev = kkva
            nxt = (d == 1 and h < 7)
            if nxt:
                self._hgrn_pre(l, h + 1, rb, krb, 0)
            if d == 0:
                front(1, 0)
            for sc in range(31):
                c = order[sc]
                if d == 0 and sc in (3, 6, 9, 12):
                    front(1, 1, only=(sc // 3 - 1))
                if d == 0 and sc == 16:
                    front(1, 2)
                if nxt and sc in (3, 6, 9, 12):
                    self._hgrn_pre(l, h + 1, rb, krb, 1, only=(sc // 3 - 1))
                    if sc == 12:
                        self._hgrn_pre(l, h + 1, rb, krb, 2)
                if sc > 0:
                    stn, kstn = Sst[sc % 2]
                    self.E("dve", "scalar_tensor_tensor", kprev + kkva + keb, kstn, out=stn, in0=prev, scalar=EB[:, c:c + 1],
                           in1=KVA3[:, sc, :], op0=ALU.mult, op1=ALU.add)
                    prev, kprev = stn, kstn
                cn = order[sc + 1]
                spn, kspn = Sp[(sc + 1) % 4]
                self.E("act", "activation", kprev + kem, kspn, out=spn, in_=prev, func=AF.Copy, scale=EM[:, cn:cn + 1])
                bo = 4 + cn // 8
                self.MM(kspn + kqp, self.pk([bo]), [(self.ps[bo][:, (cn % 8) * 64:(cn % 8 + 1) * 64], spn,
                                                    QP[:, cn * 64:(cn + 1) * 64], False, False)])
        if h < 7:
            self._hgrn_pre(l, h + 1, rb, krb, 3)
        for tg in range(4):
            self.E("act", "activation", self.pk([4 + tg]), k1, out=T1[:, tg * 512:(tg + 1) * 512], in_=self.ps[4 + tg][:], func=AF.Copy)
        SQ, ksq = sv(self.M_B6, 2048, BF16)
        YB, kyb = sv(self.M_B5, 2048, BF16)
        self.pnorm_store(T1, k1, T2, k2, SQ, ksq, YB, kyb, 0, self.small[:, 20 + h:21 + h], GS, kgs, h)


def build(cfg):
    nc = bass.Bass("TRN2", target_bir_lowering=False)
    with ExitStack() as es:
        k = K(nc, es, cfg)
        k.setup()
        layers = cfg.get("layers", list(range(DEPTH)))
        k.mixer_setup()
        k.prep(layers)
        for s in range(cfg.get("nseq", 2)):
            k.load_x(s)
            if "mix" in cfg["stages"]:
                k.cossin(s)
            for l in layers:
                if "ffn1" in cfg["stages"]:
                    k.ffn(l, 0)
                if "mix" in cfg["stages"]:
                    k.mixer(l, s)
                if "ffn2" in cfg["stages"]:
                    k.ffn(l, 1)
            k.store_x(s)
        k.P.wait_all("sp")
        k.P.emit()
    return nc


def host_layouts(inp):
    L = DEPTH
    f = lambda a: np.ascontiguousarray(np.asarray(a, dtype=np.float32))
    out = {}

    def kxn(w, ncol_blocks):
        w = np.asarray(w, dtype=np.float32).reshape(L, 16, 128, ncol_blocks, 128)
        return w.transpose(0, 3, 2, 1, 4)

    wgu = np.empty((L, 2, NFC, 128, 2, 16, 128), np.float32)
    for fi, (g, u) in enumerate((("ffn1_w_gate", "ffn1_w_up"), ("ffn2_w_gate", "ffn2_w_up"))):
        wgu[:, fi, :, :, 0] = kxn(inp[g], NFC)
        wgu[:, fi, :, :, 1] = kxn(inp[u], NFC)
    out["wgu"] = wgu.reshape(L, 2, NFC, 128, 4096)
    wd = np.empty((L, 2, 16, 128, NFC, 128), np.float32)
    for fi, dname in enumerate(("ffn1_w_down", "ffn2_w_down")):
        w = np.asarray(inp[dname], dtype=np.float32).reshape(L, NFC, 128, 16, 128)
        wd[:, fi] = w.transpose(0, 3, 2, 1, 4)
    out["wd"] = wd.reshape(L, 2, -1, 1376)
    win = kxn(inp["w_in"], NBLK)
    out["win"] = np.ascontiguousarray(win).reshape(L, NBLK // 2, 2, 128, 16, 128).transpose(0, 1, 3, 2, 4, 5).reshape(
        L, NBLK // 2, 128, 4096)
    out["wout"] = np.ascontiguousarray(kxn(inp["w_out"], 16)).reshape(L, -1, 2048)
    gains = np.stack([np.asarray(inp[n], dtype=np.float32).reshape(L, 16, 128).transpose(0, 2, 1)
                      for n in ("ffn1_norm", "mix_norm", "ffn2_norm")], axis=1)
    out["gains"] = f(gains)
    import ml_dtypes
    bf = ml_dtypes.bfloat16
    out["ident"] = np.eye(128, dtype=np.float32).astype(bf)
    ii = np.arange(128)
    same = (ii[:, None] // 64) == (ii[None, :] // 64)
    mf = (same & (ii[:, None] <= ii[None, :])).astype(np.float32)
    mb = (same & (ii[:, None] >= ii[None, :])).astype(np.float32)
    out["masks"] = np.stack([mf, mb]).astype(bf)
    inv_freq = (10000.0 ** (-np.arange(0, 64, 2, dtype=np.float32) / 64)).astype(np.float32)
    out["invf"] = np.broadcast_to(inv_freq[None, :], (128, 32)).copy()
    lg = np.asarray(inp["hgrn_lb_logits"], dtype=np.float32).reshape(2, L, 8, 128)
    out["lbl"] = lg.transpose(3, 0, 2, 1).reshape(128, 16, L)
    small = np.zeros((L, 128, 32), np.float32)
    cw = np.asarray(inp["conv_w"], dtype=np.float32).reshape(L, 3, 4, 128)
    cb = np.asarray(inp["conv_b"], dtype=np.float32).reshape(L, 4, 128)
    cn = np.asarray(inp["conv_norm"], dtype=np.float32).reshape(L, 4, 128)
    for gi in range(4):
        for j in range(3):
            small[:, :, gi * 5 + j] = cw[:, j, gi, :]
        small[:, :, gi * 5 + 3] = cb[:, gi, :]
        small[:, :, gi * 5 + 4] = cn[:, gi, :]
    small[:, :, 20:28] = np.asarray(inp["hgrn_norm"], dtype=np.float32).reshape(L, 8, 128).transpose(0, 2, 1)
    small[:, :, 28:32] = np.asarray(inp["da_out_norm"], dtype=np.float32).reshape(L, 4, 128).transpose(0, 2, 1)
    out["small"] = small
    bcv = np.concatenate([np.asarray(inp[n], dtype=np.float32) for n in
                          ("da_q_norm", "da_k_norm", "da_lambda_q1", "da_lambda_k1", "da_lambda_q2", "da_lambda_k2")], axis=1)
    out["bc"] = np.broadcast_to(bcv[:, None, :], (L, 128, 384)).copy()
    for k_ in out:
        if out[k_].dtype == np.float32:
            out[k_] = f(out[k_])
    return out


CFG_FULL = {"stages": ("ffn1", "mix", "ffn2"), "nseq": 2}


def kernel(**inputs):
    x = np.asarray(inputs["x"], dtype=np.float32)
    shared = host_layouts(inputs)
    nc = build(CFG_FULL)
    in_maps = []
    for core in range(NCORES):
        xs = x[2 * core:2 * core + 2]
        xT = np.ascontiguousarray(xs.transpose(0, 2, 1)).reshape(2, 16, 128, S)
        pos = np.asarray(inputs["positions"])[2 * core:2 * core + 2].astype(np.int32)
        m = {"xT": xT, "pos": np.ascontiguousarray(pos.reshape(2, 16, 128).transpose(0, 2, 1))}
        m.update(shared)
        in_maps.append(m)
    res = run_bass_kernel_spmd(nc, in_maps, core_ids=list(range(NCORES)))
    out = np.empty((16, S, D), np.float32)
    for core in range(NCORES):
        yT = res.results[core]["yT"].reshape(2, D, S)
        out[2 * core:2 * core + 2] = yT.transpose(0, 2, 1)
    return out
```
